# Optimizing a Trainium2 kernel written in Bass

```python
import math
import jax, jax.numpy as jnp
from jax import lax
import numpy as np


D_MODEL = 4096
BATCH = 2
SEQ = 4096
DEPTH = 1

HEAD_DIM = 128
ATTN_WIDTH = D_MODEL // 2
N_Q_HEADS = ATTN_WIDTH // HEAD_DIM
N_KV_HEADS = N_Q_HEADS // 4
KV_WIDTH = N_KV_HEADS * HEAD_DIM
WINDOW = 128
BLOCK = 128
ROPE_DIM = HEAD_DIM // 4
ROPE_THETA = 500000.0

SSD_WIDTH = D_MODEL // 2
SSD_HEAD_DIM = 64
SSD_HEADS = SSD_WIDTH // SSD_HEAD_DIM
SSD_GROUPS = 4
D_STATE = 128
CONV_WIDTH = 5
CHUNK = 128
XBC_WIDTH = SSD_WIDTH + 2 * SSD_GROUPS * D_STATE

MIX_WIDTH = ATTN_WIDTH + SSD_WIDTH
IN_WIDTH = ATTN_WIDTH + 2 * KV_WIDTH + SSD_WIDTH + XBC_WIDTH + 2 * SSD_HEADS
D_FF = -(-8 * D_MODEL // (3 * 256)) * 256
LN_EPS = 1e-5
RMS_EPS = 1e-6

kernel_name = 'hybrid_swa_ssd_parallel_encoder_block'


def layer_norm(x, g, b):
    xf = x.astype(jnp.float32)
    mu = xf.mean(-1, keepdims=True)
    var = jnp.square(xf - mu).mean(-1, keepdims=True)
    return ((xf - mu) * lax.rsqrt(var + LN_EPS)).astype(x.dtype) * g + b


def rms_norm(x, g):
    xf = x.astype(jnp.float32)
    return (xf * lax.rsqrt(jnp.square(xf).mean(-1, keepdims=True) + RMS_EPS)).astype(x.dtype) * g


def partial_rope(t, positions):
    inv_freq = ROPE_THETA ** (-jnp.arange(0, ROPE_DIM, 2, dtype=jnp.float32) / ROPE_DIM)
    ang = positions.astype(jnp.float32)[..., None] * inv_freq
    cos = jnp.cos(ang)[:, :, None, :]
    sin = jnp.sin(ang)[:, :, None, :]
    rot = t[..., :ROPE_DIM].astype(jnp.float32)
    x1, x2 = rot[..., :ROPE_DIM // 2], rot[..., ROPE_DIM // 2:]
    rot = jnp.concatenate([x1 * cos - x2 * sin, x2 * cos + x1 * sin], axis=-1)
    return jnp.concatenate([rot.astype(t.dtype), t[..., ROPE_DIM:]], axis=-1)


def windowed_attention(q, k, v, sink):
    bsz, seq = q.shape[0], q.shape[1]
    nb = seq // BLOCK
    grp = N_Q_HEADS // N_KV_HEADS
    qb = q.reshape(bsz, nb, BLOCK, N_KV_HEADS, grp, HEAD_DIM)

    def band(t):
        tp = jnp.pad(t, ((0, 0), (BLOCK, BLOCK), (0, 0), (0, 0)))
        tb = tp.reshape(bsz, nb + 2, BLOCK, N_KV_HEADS, HEAD_DIM)
        return jnp.concatenate([tb[:, :-2], tb[:, 1:-1], tb[:, 2:]], axis=2)

    kw, vw = band(k), band(v)
    scores = jnp.einsum('bnqkgd,bnskd->bnkgqs', qb, kw).astype(jnp.float32) * (HEAD_DIM ** -0.5)
    qi = jnp.arange(BLOCK)[:, None]
    sj = jnp.arange(3 * BLOCK)[None, :]
    in_window = jnp.abs(sj - BLOCK - qi) <= WINDOW
    kpos = jnp.arange(nb)[:, None] * BLOCK + jnp.arange(3 * BLOCK)[None, :] - BLOCK
    valid = (kpos >= 0) & (kpos < seq)
    mask = in_window[None] & valid[:, None, :]
    scores = jnp.where(mask[None, :, None, None], scores, -jnp.inf)
    sink_l = sink.astype(jnp.float32).reshape(N_KV_HEADS, grp)[None, None, :, :, None, None]
    m = jnp.maximum(scores.max(-1, keepdims=True), sink_l)
    p = jnp.exp(scores - m)
    denom = p.sum(-1, keepdims=True) + jnp.exp(sink_l - m)
    probs = (p / denom).astype(v.dtype)
    out = jnp.einsum('bnkgqs,bnskd->bnqkgd', probs, vw)
    return out.reshape(bsz, seq, ATTN_WIDTH)


def depthwise_conv(u, w, b):
    pad = CONV_WIDTH // 2
    out = lax.conv_general_dilated(u, w[:, None, :], (1,), [(pad, pad)],
                                   dimension_numbers=('NWC', 'WIO', 'NWC'),
                                   feature_group_count=u.shape[-1])
    return out + b


def ssd_scan(xh, dt, a, bm, cm):
    bsz, seq = xh.shape[0], xh.shape[1]
    nc = seq // CHUNK
    rep = SSD_HEADS // SSD_GROUPS
    x = xh.astype(jnp.float32).reshape(bsz, nc, CHUNK, SSD_GROUPS, rep, SSD_HEAD_DIM)
    dtc = dt.reshape(bsz, nc, CHUNK, SSD_GROUPS, rep)
    bc = bm.astype(jnp.float32).reshape(bsz, nc, CHUNK, SSD_GROUPS, D_STATE)
    cc = cm.astype(jnp.float32).reshape(bsz, nc, CHUNK, SSD_GROUPS, D_STATE)
    xdt = x * dtc[..., None]
    a_dt = jnp.moveaxis(dtc * a.reshape(SSD_GROUPS, rep), 2, -1)
    a_cum = jnp.cumsum(a_dt, axis=-1)
    tri = jnp.tril(jnp.ones((CHUNK, CHUNK), dtype=bool))
    seg = a_cum[..., :, None] - a_cum[..., None, :]
    decay = jnp.exp(jnp.where(tri, seg, -jnp.inf))
    cb = jnp.einsum('bclgn,bcsgn->bcgls', cc, bc)
    y_diag = jnp.einsum('bcgls,bcgrls,bcsgrp->bclgrp', cb, decay, xdt)
    decay_states = jnp.exp(a_cum[..., -1:] - a_cum)
    states = jnp.einsum('bclgn,bcgrl,bclgrp->bcgrpn', bc, decay_states, xdt)
    chunk_decay = jnp.exp(a_cum[..., -1])

    def step(h, inp):
        st, dec = inp
        return h * dec[..., None, None] + st, h

    init = jnp.zeros((bsz, SSD_GROUPS, rep, SSD_HEAD_DIM, D_STATE), jnp.float32)
    _, prev = lax.scan(step, init, (jnp.moveaxis(states, 1, 0), jnp.moveaxis(chunk_decay, 1, 0)))
    prev = jnp.moveaxis(prev, 0, 1)
    y_off = jnp.einsum('bclgn,bcgrpn,bcgrl->bclgrp', cc, prev, jnp.exp(a_cum))
    return (y_diag + y_off).reshape(bsz, seq, SSD_HEADS, SSD_HEAD_DIM)


def hybrid_mixer(h, positions, w_in, conv_w, conv_b, attn_sink, a_log_fwd, a_log_bwd,
                 dt_bias_fwd, dt_bias_bwd, ssd_d, ssd_norm_w, attn_norm_w, w_out):
    bsz, seq = h.shape[0], h.shape[1]
    proj = h @ w_in
    cuts = [ATTN_WIDTH, ATTN_WIDTH + KV_WIDTH, ATTN_WIDTH + 2 * KV_WIDTH,
            ATTN_WIDTH + 2 * KV_WIDTH + SSD_WIDTH,
            ATTN_WIDTH + 2 * KV_WIDTH + SSD_WIDTH + XBC_WIDTH]
    q, k, v, z, xbc, dt_raw = jnp.split(proj, cuts, axis=-1)
    q = partial_rope(q.reshape(bsz, seq, N_Q_HEADS, HEAD_DIM), positions)
    k = partial_rope(k.reshape(bsz, seq, N_KV_HEADS, HEAD_DIM), positions)
    v = v.reshape(bsz, seq, N_KV_HEADS, HEAD_DIM)
    attn = rms_norm(windowed_attention(q, k, v, attn_sink), attn_norm_w)
    xbc = jax.nn.silu(depthwise_conv(xbc, conv_w, conv_b))
    xs, bm, cm = jnp.split(xbc, [SSD_WIDTH, SSD_WIDTH + SSD_GROUPS * D_STATE], axis=-1)
    xh = xs.reshape(bsz, seq, SSD_HEADS, SSD_HEAD_DIM)
    bm = bm.reshape(bsz, seq, SSD_GROUPS, D_STATE)
    cm = cm.reshape(bsz, seq, SSD_GROUPS, D_STATE)
    dt_raw = dt_raw.astype(jnp.float32)
    dt_f = jax.nn.softplus(dt_raw[..., :SSD_HEADS] + dt_bias_fwd.astype(jnp.float32))
    dt_b = jax.nn.softplus(dt_raw[..., SSD_HEADS:] + dt_bias_bwd.astype(jnp.float32))
    a_f = -jnp.exp(a_log_fwd.astype(jnp.float32))
    a_b = -jnp.exp(a_log_bwd.astype(jnp.float32))
    flip = lambda t: jnp.flip(t, axis=1)
    y = ssd_scan(xh, dt_f, a_f, bm, cm) + flip(ssd_scan(flip(xh), flip(dt_b), a_b, flip(bm), flip(cm)))
    y = y + ssd_d.astype(jnp.float32)[:, None] * xh.astype(jnp.float32)
    y = y.reshape(bsz, seq, SSD_WIDTH) * jax.nn.silu(z.astype(jnp.float32))
    yg = y.reshape(bsz, seq, SSD_GROUPS, SSD_WIDTH // SSD_GROUPS)
    yg = yg * lax.rsqrt(jnp.square(yg).mean(-1, keepdims=True) + RMS_EPS)
    ssd = yg.reshape(bsz, seq, SSD_WIDTH).astype(h.dtype) * ssd_norm_w
    return jnp.concatenate([attn, ssd], axis=-1) @ w_out


def swiglu(h, w_gate, w_up, w_down):
    return (jax.nn.silu(h @ w_gate) * (h @ w_up)) @ w_down


def setup_inputs(seed: int = 0) -> dict:
    key = jax.random.key(seed)
    ks = jax.random.split(key, 24)
    f32 = jnp.float32
    beta = (8 * DEPTH) ** -0.25

    def nrm(k, shape, scale):
        return jax.random.normal(k, shape, f32) * scale

    def make_dt_bias(k):
        dt = jnp.exp(jax.random.uniform(k, (DEPTH, SSD_HEADS), f32, math.log(1e-3), math.log(1e-1)))
        return dt + jnp.log(-jnp.expm1(-dt))

    x = nrm(ks[0], (BATCH, SEQ, D_MODEL), 1.0)
    c = nrm(ks[1], (BATCH, D_MODEL), 1.0)
    positions = jnp.broadcast_to(jnp.arange(SEQ, dtype=jnp.int32), (BATCH, SEQ))
    w_ada = nrm(ks[2], (DEPTH, D_MODEL, 6 * D_MODEL), 0.5 * D_MODEL ** -0.5)
    b_ada = nrm(ks[3], (DEPTH, 6 * D_MODEL), 0.02)
    col_scale = jnp.concatenate([jnp.ones((ATTN_WIDTH + KV_WIDTH,), f32),
                                 jnp.full((KV_WIDTH,), beta, f32),
                                 jnp.ones((IN_WIDTH - ATTN_WIDTH - 2 * KV_WIDTH,), f32)])
    w_in = nrm(ks[4], (DEPTH, D_MODEL, IN_WIDTH), D_MODEL ** -0.5) * col_scale
    conv_w = nrm(ks[5], (DEPTH, CONV_WIDTH, XBC_WIDTH), CONV_WIDTH ** -0.5)
    conv_b = nrm(ks[6], (DEPTH, XBC_WIDTH), 0.02)
    attn_sink = nrm(ks[7], (DEPTH, N_Q_HEADS), 0.5)
    a_log_fwd = jnp.log(jax.random.uniform(ks[8], (DEPTH, SSD_HEADS), f32, 1.0, 16.0))
    a_log_bwd = jnp.log(jax.random.uniform(ks[9], (DEPTH, SSD_HEADS), f32, 1.0, 16.0))
    dt_bias_fwd = make_dt_bias(ks[10])
    dt_bias_bwd = make_dt_bias(ks[11])
    ssd_d = 1.0 + nrm(ks[12], (DEPTH, SSD_HEADS), 0.1)
    ssd_norm_w = 1.0 + nrm(ks[13], (DEPTH, SSD_WIDTH), 0.02)
    attn_norm_w = 1.0 + nrm(ks[14], (DEPTH, ATTN_WIDTH), 0.02)
    w_out = nrm(ks[15], (DEPTH, MIX_WIDTH, D_MODEL), beta * MIX_WIDTH ** -0.5)
    ln1_g = 1.0 + nrm(ks[16], (DEPTH, D_MODEL), 0.02)
    ln1_b = nrm(ks[17], (DEPTH, D_MODEL), 0.02)
    w_gate = nrm(ks[18], (DEPTH, D_MODEL, D_FF), beta * D_MODEL ** -0.5)
    w_up = nrm(ks[19], (DEPTH, D_MODEL, D_FF), beta * D_MODEL ** -0.5)
    w_down = nrm(ks[20], (DEPTH, D_FF, D_MODEL), beta * D_FF ** -0.5)
    ln2_g = 1.0 + nrm(ks[21], (DEPTH, D_MODEL), 0.02)
    ln2_b = nrm(ks[22], (DEPTH, D_MODEL), 0.02)
    return {'x': x, 'c': c, 'positions': positions, 'w_ada': w_ada, 'b_ada': b_ada,
            'w_in': w_in, 'conv_w': conv_w, 'conv_b': conv_b, 'attn_sink': attn_sink,
            'a_log_fwd': a_log_fwd, 'a_log_bwd': a_log_bwd,
            'dt_bias_fwd': dt_bias_fwd, 'dt_bias_bwd': dt_bias_bwd,
            'ssd_d': ssd_d, 'ssd_norm_w': ssd_norm_w, 'attn_norm_w': attn_norm_w,
            'w_out': w_out, 'ln1_g': ln1_g, 'ln1_b': ln1_b,
            'w_gate': w_gate, 'w_up': w_up, 'w_down': w_down,
            'ln2_g': ln2_g, 'ln2_b': ln2_b}


def reference(x, c, positions, w_ada, b_ada, w_in, conv_w, conv_b, attn_sink,
              a_log_fwd, a_log_bwd, dt_bias_fwd, dt_bias_bwd, ssd_d, ssd_norm_w,
              attn_norm_w, w_out, ln1_g, ln1_b, w_gate, w_up, w_down, ln2_g, ln2_b):
    alpha = (2 * DEPTH) ** 0.25
    cond = jax.nn.silu(c)
    for l in range(DEPTH):
        mod = cond @ w_ada[l] + b_ada[l]
        sh1, sc1, g1, sh2, sc2, g2 = [m[:, None, :] for m in jnp.split(mod, 6, axis=-1)]
        h = x * (1.0 + sc1) + sh1
        mix = hybrid_mixer(h, positions, w_in[l], conv_w[l], conv_b[l], attn_sink[l],
                           a_log_fwd[l], a_log_bwd[l], dt_bias_fwd[l], dt_bias_bwd[l],
                           ssd_d[l], ssd_norm_w[l], attn_norm_w[l], w_out[l])
        x = layer_norm(alpha * x + g1 * mix, ln1_g[l], ln1_b[l])
        h = x * (1.0 + sc2) + sh2
        x = layer_norm(alpha * x + g2 * swiglu(h, w_gate[l], w_up[l], w_down[l]), ln2_g[l], ln2_b[l])
    return x
```

```python
import numpy as np
from contextlib import ExitStack
import concourse.bass as bass
import concourse.mybir as mybir
from concourse.bass_utils import run_bass_kernel_spmd

F32 = mybir.dt.float32
BF16 = mybir.dt.bfloat16
I32 = mybir.dt.int32
AF = mybir.ActivationFunctionType
ALU = mybir.AluOpType

NCORES = 8
D = 4096
KC = 32
DFF = 11008
KCF = 86
NSLOT = 32
SW = 132
GROUPS = [(0, 1, 2), (3, 4, 5), (6, 7, 8), (9, 10, 11), (12, 13, 14), (15, 16, 17),
          (18, 19, 20), (21, 22, 23), (24, 25, 26), (27, 28, 29), (30, 31)]
OWNG = [(0, 1, 2), (3, 4, 5), (6, 7)]
CQ, CK, CV, CZ, CX, CB, CC, CDT = 0, 2048, 2560, 3072, 5120, 7168, 7680, 8192
SCALE = 128.0 ** -0.5
PI = float(np.pi)
ALPHA = 2.0 ** 0.25
LN_EPS = 1e-5
RMS_EPS = 1e-6
KVSLOT = {0: 0, 1: 1, 2: 2, 3: 3, 4: 4, 5: 5, 6: 6, 7: 7, 8: 8, 31: 9}


class Tok:
    __slots__ = ("key", "val")

    def __init__(self, key, val=None):
        self.key, self.val = key, val


class Buf:
    def __init__(self, name):
        self.name = name
        self.w = None
        self.r = {}


class K:
    CE = ("pe", "act", "dve", "pool", "sp")

    def __init__(self, nc, es):
        self.nc = nc
        self.eng = {"pe": nc.tensor, "act": nc.scalar, "dve": nc.vector, "pool": nc.gpsimd, "sp": nc.sync}
        self.sem = {e: es.enter_context(nc.semaphore("s_" + e)) for e in self.eng}
        self.cnt = {e: 0 for e in self.eng}
        self.pend = {e: [] for e in self.eng}
        self.last = {e: None for e in self.eng}
        self.lasttok = {e: None for e in self.eng}
        self.waited = {e: {} for e in self.eng}
        self.dsem = {}
        self.dpos = {}
        for q in ("sp", "pool"):
            self.dsem[q] = []
            for i in range(12):
                key = "d_%s%d" % (q, i)
                self.sem[key] = es.enter_context(nc.semaphore(key))
                self.dsem[q].append([key, 0, None])
            self.dpos[q] = 0
        self.outtoks = []
        self.ninst = 0

    def _resolve(self, tok):
        if tok.val is None:
            e = tok.key
            self.last[e].then_inc(self.sem[e], 1)
            self.cnt[e] += 1
            for t in self.pend[e]:
                t.val = self.cnt[e]
            self.pend[e] = []
        return tok.val

    def _wait(self, e, tok):
        if tok is None:
            return
        if tok.key == e and e == "pe":
            return
        v = self._resolve(tok)
        if self.waited[e].get(tok.key, 0) >= v:
            return
        self.eng[e].wait_ge(self.sem[tok.key], v)
        self.waited[e][tok.key] = v

    def _deps(self, e, r, w):
        for b in r:
            self._wait(e, b.w)
        for b in w:
            self._wait(e, b.w)
            for t in list(b.r.values()):
                self._wait(e, t)

    def _mark(self, tok, r, w):
        for b in r:
            b.r[tok.key] = tok
        for b in w:
            b.w = tok
            b.r = {}

    def op(self, e, fn, r=(), w=()):
        self._deps(e, r, w)
        inst = fn(self.eng[e])
        self.ninst += 1
        self.last[e] = inst
        tok = Tok(e)
        self.pend[e].append(tok)
        self.lasttok[e] = tok
        self._mark(tok, r, w)
        return tok

    def dma(self, q, out, in_, r=(), w=(), is_out=False):
        self._deps(q, r, w)
        slot = self.dsem[q][self.dpos[q] % len(self.dsem[q])]
        self.dpos[q] += 1
        if slot[2] is not None:
            self._wait(q, slot[2])
        slot[1] += 16
        self.eng[q].dma_start(out=out, in_=in_).then_inc(self.sem[slot[0]], 16)
        self.ninst += 1
        tok = Tok(slot[0], slot[1])
        slot[2] = tok
        self._mark(tok, r, w)
        if is_out:
            self.outtoks.append(tok)
        return tok

    def barrier(self):
        toks = [self.lasttok[e] for e in self.CE if self.lasttok[e] is not None]
        for q in ("sp", "pool"):
            toks += [s[2] for s in self.dsem[q] if s[2] is not None]
        for e in self.CE:
            for t in toks:
                self._wait(e, t)

    def finish(self):
        self.barrier()
        for t in self.outtoks:
            self._wait("sp", t)


def build_nc(dbg=()):
    nc = bass.Bass("TRN2", target_bir_lowering=False)

    in_names = []
    stop_after = [d for d in dbg if d.startswith("stop:")]
    stop_after = stop_after[0][5:] if stop_after else None
    nomod = "nomod" in dbg

    def din(name, shape, dt=F32, need=True):
        if not need:
            return None
        in_names.append(name)
        return nc.dram_tensor(name, list(shape), dt, kind="ExternalInput").ap()

    xm = din("xm", [NSLOT * 128, D])
    xh = din("xh", [NSLOT * 4, D])
    hmask = din("hmask", [128, NSLOT * 4])
    posr = din("posr", [10 * 128], I32)
    amask = din("amask", [3, 128, 384])
    actf = din("actf", [128, NSLOT])
    actb = din("actb", [128, NSLOT])
    cmat_d = din("cmat", [128, 6 * 128])
    sel_d = din("sel", [8, 8 * 128])
    ropec = din("ropec", [32, 2])
    psw_d = din("pswap", [32, 32])
    cvec = din("cvec", [32, 128], need=not nomod)
    w_ada = din("w_ada", [D, 6 * D], need=not nomod)
    b_ada = din("b_ada", [6 * D], need=not nomod)
    mod_in = din("mod_in", [6 * D], need=nomod)
    full = stop_after is None
    w_in = din("w_in", [D, 8256])
    conv_w = din("conv_w", [5, 3072])
    conv_b = din("conv_b", [3072])
    attn_sink = din("attn_sink", [16])
    a_log_f = din("a_log_fwd", [32])
    a_log_b = din("a_log_bwd", [32])
    dtb_f = din("dt_bias_fwd", [32])
    dtb_b = din("dt_bias_bwd", [32])
    ssd_d = din("ssd_d", [32])
    ssd_nw = din("ssd_norm_w", [2048])
    attn_nw = din("attn_norm_w", [2048])
    w_out = din("w_out", [D, D], need=full)
    ln1_g = din("ln1_g", [D], need=full)
    ln1_b = din("ln1_b", [D], need=full)
    w_gate = din("w_gate", [D, DFF], need=full)
    w_up = din("w_up", [D, DFF], need=full)
    w_down = din("w_down", [DFF, D], need=full)
    ln2_g = din("ln2_g", [D], need=full)
    ln2_b = din("ln2_b", [D], need=full)
    y_out = nc.dram_tensor("y_out", [1024, D], F32, kind="ExternalOutput").ap()
    mod_d = nc.dram_tensor("mod_d", [6 * D], F32).ap()
    sb_d = nc.dram_tensor("sb_d", [NSLOT, 128, 2048], F32).ap()
    eb_d = nc.dram_tensor("eb_d", [NSLOT, 128, 32], F32).ap()
    hb_d = nc.dram_tensor("hb_d", [8, 128, 2048], BF16).ap()
    mix_d = nc.dram_tensor("mix_d", [1024, D], BF16).ap()
    mixr_d = nc.dram_tensor("mixr_d", [1024, D], F32).ap()
    x1_d = nc.dram_tensor("x1_d", [1024, D], F32).ap()
    ffn_d = nc.dram_tensor("ffn_d", [1024, D], F32).ap()
    dbg_out = {}
    if "hf" in dbg:
        dbg_out["hf"] = nc.dram_tensor("dbg_hf", [128, 2048], F32, kind="ExternalOutput").ap()
    if "kv" in dbg:
        dbg_out["kT"] = nc.dram_tensor("dbg_kT", [128, 4 * 1280], F32, kind="ExternalOutput").ap()
        dbg_out["v"] = nc.dram_tensor("dbg_v", [128, 10 * 512], F32, kind="ExternalOutput").ap()
    for nm, shp, dt in (("mod", [6 * D], F32), ("hb", [8, 128, 2048], BF16), ("mix", [1024, D], BF16),
                        ("mixr", [1024, D], F32), ("x1", [1024, D], F32), ("ffn", [1024, D], F32)):
        if nm in dbg:
            dbg_out[nm] = nc.dram_tensor("dbg_" + nm, shp, dt, kind="ExternalOutput").ap()
    es = ExitStack()
    with es:
        k = K(nc, es)

        used_names = {}
        want_dump = "p3dump" in dbg

        def dump(name, ap, bufs, dt=F32):
            if not want_dump:
                return
            shp = list(ap.shape)
            o = nc.dram_tensor("dmp_" + name, shp, dt, kind="ExternalOutput").ap()
            k.dma("sp", o, ap, r=bufs, is_out=True)

        def sb(name, shape, dt=F32, stack=None):
            n = used_names.get(name, 0)
            used_names[name] = n + 1
            if n:
                name = "%s_r%d" % (name, n)
            return (stack or es).enter_context(nc.sbuf_tensor(name, list(shape), dt))

        psA = es.enter_context(nc.psum_tensor("psA", [128, 2048], F32))
        psB = es.enter_context(nc.psum_tensor("psB", [128, 2048], F32))
        bank = [psA[:, i * 512:(i + 1) * 512] for i in range(4)] + [psB[:, i * 512:(i + 1) * 512] for i in range(4)]
        bkb = [Buf("bank%d" % i) for i in range(8)]

        def V(ap, c):
            return ap.rearrange("p (s c) -> p s c", c=c)

        cmat = sb("cmat_s", [128, 6, 128]); b_cm = Buf("cmat")
        k.dma("sp", cmat[:].rearrange("p a b -> p (a b)"), cmat_d[:, :], w=[b_cm])
        ident, Uin, Lin, Ust, Lst, ones = [cmat[:, i, :] for i in range(6)]
        cb16 = sb("cb16", [128, 2, 128], BF16); b_c16 = Buf("cb16")
        identb, onesb = cb16[:, 0, :], cb16[:, 1, :]
        k.op("dve", lambda e: e.tensor_copy(out=identb, in_=ident), r=[b_cm], w=[b_c16])
        k.op("dve", lambda e: e.tensor_copy(out=onesb, in_=ones), r=[b_cm], w=[b_c16])
        hm, b_hm = sb("hm", [128, NSLOT, 4]), Buf("hm")
        k.dma("sp", hm[:].rearrange("p a b -> p (a b)"), hmask[:, :], w=[b_hm])
        af_t, b_af = sb("af_t", [128, NSLOT]), Buf("af")
        k.dma("sp", af_t[:], actf[:, :], w=[b_af])
        ab_t, b_ab = sb("ab_t", [128, NSLOT]), Buf("ab")
        k.dma("sp", ab_t[:], actb[:, :], w=[b_ab])
        sink_t, b_sink = sb("sink_t", [128, 16]), Buf("sink")
        k.dma("sp", sink_t[:], attn_sink.partition_broadcast(128), w=[b_sink])
        nsink = sb("nsink", [128, 16]); b_nsink = Buf("nsink")
        k.op("dve", lambda e: e.tensor_scalar(out=nsink[:], in0=sink_t[:], scalar1=-1.0, scalar2=None, op0=ALU.mult), r=[b_sink], w=[b_nsink])
        hp = sb("hp", [128, 160]); b_hp = Buf("hp")
        k.dma("sp", hp[:, 0:32], dtb_f.partition_broadcast(128), w=[b_hp])
        k.dma("sp", hp[:, 32:64], dtb_b.partition_broadcast(128), w=[b_hp])
        k.dma("sp", hp[:, 64:96], a_log_f.partition_broadcast(128), w=[b_hp])
        k.dma("sp", hp[:, 96:128], a_log_b.partition_broadcast(128), w=[b_hp])
        k.dma("sp", hp[:, 128:160], ssd_d.partition_broadcast(128), w=[b_hp])
        k.op("act", lambda e: e.activation(out=hp[:, 64:128], in_=hp[:, 64:128], func=AF.Exp), r=[b_hp], w=[b_hp])
        k.op("dve", lambda e: e.tensor_scalar(out=hp[:, 64:128], in0=hp[:, 64:128], scalar1=-1.0, scalar2=None, op0=ALU.mult), r=[b_hp], w=[b_hp])

        def load_cols(name, src2d, R, nb):
            t = sb(name, [128, nb, R]); bt = Buf(name)
            with nc.sbuf_tensor(name + "_s", [R, nb * 128], F32) as stg:
                bs = Buf(name + "_s")
                k.dma("sp", stg[:], src2d, w=[bs])
                per = max(1, 512 // R)
                for b0 in range(0, nb, per):
                    n = min(per, nb - b0)
                    for i in range(n):
                        k.op("pe", lambda e, i=i: e.transpose(out=bank[2][:, i * R:(i + 1) * R], in_=stg[:, (b0 + i) * 128:(b0 + i + 1) * 128], identity=ident[0:R, 0:R]), r=[bs, b_cm], w=[bkb[2]])
                    k.op("dve", lambda e, n=n: e.tensor_copy(out=t[:, b0:b0 + n, :].rearrange("p a b -> p (a b)"), in_=bank[2][:, 0:n * R]), r=[bkb[2]], w=[bt])
                k.barrier()
            return t, bt

        cw, b_cw = load_cols("cw", conv_w[:, :], 5, 24)
        cbc, b_cbc = load_cols("cbc", conv_b.rearrange("(a b) -> a b", b=128), 24, 1)
        anw, b_anw = load_cols("anw", attn_nw.rearrange("(a b) -> a b", b=128), 16, 1)
        snw, b_snw = load_cols("snw", ssd_nw.rearrange("(a b) -> a b", b=128), 16, 1)

        condT = sb("condT", [128, 32], BF16); b_cond = Buf("condT")
        if nomod:
            k.dma("sp", mod_d, mod_in)
            k.barrier()
        else:
            with ExitStack() as ps_:
                cst = sb("cst", [32, 128], stack=ps_); b_cst = Buf("cst")
                k.dma("sp", cst[:], cvec[:, :], w=[b_cst])
                k.op("pe", lambda e: e.transpose(out=bank[2][:, 0:32], in_=cst[:], identity=ident[0:32, 0:32]), r=[b_cst, b_cm], w=[bkb[2]])
                k.op("act", lambda e: e.activation(out=condT[:], in_=bank[2][:, 0:32], func=AF.Silu), r=[bkb[2]], w=[b_cond])
                k.barrier()
        xt = sb("xt", [128, D]); b_xt = Buf("xt")
        NW = 3
        wsl = [sb("wsl%d" % i, [128, 32, 128], BF16) for i in range(NW)]; b_wsl = [Buf("wsl%d" % i) for i in range(NW)]
        wc = [0]
        gc = [0]

        def transp_mod(src_tile, b_src, nrows, dst_fn, b_dst, mc, b_mc):
            for k0 in range(0, KC, 4):
                pb = 6 + (k0 // 4) % 2
                for j in range(4):
                    kc = k0 + j
                    k.op("pe", lambda e, kc=kc, j=j: e.transpose(out=bank[pb][:, j * 128:j * 128 + nrows], in_=src_tile[0:nrows, kc * 128:(kc + 1) * 128], identity=ident[0:nrows, 0:nrows]), r=[b_src, b_cm], w=[bkb[pb]])
                for j in range(4):
                    kc = k0 + j
                    for (dst, src) in dst_fn(kc, bank[pb][:, j * 128:j * 128 + nrows]):
                        k.op("act", lambda e, kc=kc, dst=dst, src=src: e.activation(out=dst, in_=src, func=AF.Identity, scale=mc[:, 32 + kc:33 + kc], bias=mc[:, kc:kc + 1]), r=[bkb[pb], b_mc], w=[b_dst])

        def make_hT(src_rows, nrows, dst_fn, b_dst):
            k.dma("sp", xt[0:nrows, :], src_rows, w=[b_xt])
            transp_mod(xt, b_xt, nrows, dst_fn, b_dst, mA, b_mcA)

        pend_epi = [None]
        bg_hook = [lambda: None]

        def flush_epi():
            if pend_epi[0] is not None:
                f = pend_epi[0]; pend_epi[0] = None
                f()

        def gemm(act_fn, N, nkc, wsrc, col0, nblk, epi, r_act, units=1, two_phase=False):
            flush_epi()
            usz = [nkc // units + (1 if u < nkc % units else 0) for u in range(units)]
            uoff = [sum(usz[:u]) for u in range(units)]
            for blk in range(nblk):
                c0 = col0 + blk * 128
                pb = gc[0] % 2; gc[0] += 1
                for u in range(units):
                    wi = wc[0] % NW; wc[0] += 1
                    per = usz[u]
                    k.dma("pool", wsl[wi][:, 0:per, :], wsrc[uoff[u] * 128:(uoff[u] + per) * 128, c0:c0 + 128].rearrange("(kc p) c -> p kc c", p=128), w=[b_wsl[wi]])
                    for kk in range(per):
                        kc = uoff[u] + kk
                        k.op("pe", lambda e, kc=kc, kk=kk, wi=wi: e.matmul(bank[pb][:, 0:N], lhsT=wsl[wi][:, kk, :], rhs=act_fn(kc), start=(kc == 0), stop=(kc == nkc - 1)), r=[b_wsl[wi]] + r_act, w=[bkb[pb]])
                flush_epi()
                if two_phase:
                    pend_epi[0] = epi(blk, bank[pb], bkb[pb])
                else:
                    pend_epi[0] = (lambda blk=blk, pb=pb: epi(blk, bank[pb], bkb[pb]))
                bg_hook[0]()

        abar = sb("abar", [1, 4, 128]); b_abar = [Buf("abar%d" % i) for i in range(4)]
        amrow = sb("amrow", [1, 4, 128]); b_amrow = [Buf("amrow%d" % i) for i in range(4)]
        adc = [0]

        def ada_unit(u):
            c0 = u * 128
            i = adc[0] % 4; adc[0] += 1
            pb = gc[0] % 2; gc[0] += 1
            wi = wc[0] % NW; wc[0] += 1
            flush_epi()
            k.dma("pool", wsl[wi][:, 0:KC, :], w_ada[:, c0:c0 + 128].rearrange("(kc p) c -> p kc c", p=128), w=[b_wsl[wi]])
            k.dma("sp", abar[0:1, i, :], b_ada[c0:c0 + 128].rearrange("(a b) -> a b", a=1), w=[b_abar[i]])
            for kc in range(KC):
                k.op("pe", lambda e, kc=kc: e.matmul(bank[pb][0:1, 0:128], lhsT=condT[:, kc:kc + 1], rhs=wsl[wi][:, kc, :], start=(kc == 0), stop=False), r=[b_cond, b_wsl[wi]], w=[bkb[pb]])
            k.op("pe", lambda e: e.matmul(bank[pb][0:1, 0:128], lhsT=ones[0:1, 0:1], rhs=abar[0:1, i, :], start=False, stop=True), r=[b_cm, b_abar[i]], w=[bkb[pb]])

            def epi():
                k.op("act", lambda e: e.activation(out=amrow[0:1, i, :], in_=bank[pb][0:1, 0:128], func=AF.Identity), r=[bkb[pb]], w=[b_amrow[i]])
                k.dma("sp", mod_d[c0:c0 + 128].rearrange("(a b) -> a b", a=1), amrow[0:1, i, :], r=[b_amrow[i]])
            pend_epi[0] = epi
            bg_hook[0]()

        if not nomod:
            for u in range(64):
                ada_unit(u)
            flush_epi()
            k.barrier()
        mcA, b_mcA = load_cols("mcA", mod_d[0:2 * D].rearrange("(a b) -> a b", b=128), 64, 1)
        mA = mcA[:, 0, :]
        k.op("dve", lambda e: e.tensor_scalar(out=mA[:, 32:64], in0=mA[:, 32:64], scalar1=1.0, scalar2=None, op0=ALU.add), r=[b_mcA], w=[b_mcA])
        mcB = sb("mcB", [128, 1, 64]); b_mcB = Buf("mcB")
        mB = mcB[:, 0, :]

        Hf = sb("Hf", [128, 2048]); b_Hf = Buf("Hf")
        k.op("dve", lambda e: e.memset(Hf[:], 0.0), w=[b_Hf])
        wdt = sb("wdt", [128, KC, 64], BF16); b_wdt = Buf("wdt")
        k.dma("pool", wdt[:], w_in[:, CDT:CDT + 64].rearrange("(kc p) c -> p kc c", p=128), w=[b_wdt])
        ast_ = ExitStack()
        cosT = sb("cosT", [32, 1280], stack=ast_); sinT = sb("sinT", [32, 1280], stack=ast_); b_rope = Buf("rope")
        psw = sb("psw", [32, 32], stack=ast_); b_psw = Buf("psw")
        k.dma("sp", psw[:], psw_d[:, :], w=[b_psw])
        with ExitStack() as ps_:
            pi_i = sb("pi_i", [32, 1280], I32, stack=ps_); ang = sb("ang", [32, 1280], stack=ps_)
            rt = sb("rt", [32, 1280], stack=ps_); rc = sb("rc", [32, 2], stack=ps_)
            b_pi, b_ang, b_rt, b_rc = Buf("pi"), Buf("ang"), Buf("rt"), Buf("rc")
            k.dma("sp", pi_i[:], posr.partition_broadcast(32), w=[b_pi])
            k.dma("sp", rc[:], ropec[:, :], w=[b_rc])
            k.op("dve", lambda e: e.tensor_copy(out=ang[:], in_=pi_i[:]), r=[b_pi], w=[b_ang])
            k.op("dve", lambda e: e.tensor_scalar(out=ang[:], in0=ang[:], scalar1=rc[:, 0:1], scalar2=None, op0=ALU.mult), r=[b_ang, b_rc], w=[b_ang])
            for (dstT, off) in ((sinT, 0.0), (cosT, PI / 2)):
                k.op("dve", lambda e: e.tensor_scalar(out=rt[:], in0=ang[:], scalar1=off, scalar2=1.0 / (2 * PI), op0=ALU.add, op1=ALU.mult), r=[b_ang], w=[b_rt])
                k.op("dve", lambda e: e.tensor_copy(out=pi_i[:], in_=rt[:]), r=[b_rt], w=[b_pi])
                k.op("dve", lambda e: e.tensor_copy(out=rt[:], in_=pi_i[:]), r=[b_pi], w=[b_rt])
                k.op("dve", lambda e: e.scalar_tensor_tensor(out=rt[:], in0=rt[:], scalar=-2 * PI, in1=ang[:], op0=ALU.mult, op1=ALU.add), r=[b_rt, b_ang], w=[b_rt])
                k.op("dve", lambda e: e.tensor_scalar(out=rt[:], in0=rt[:], scalar1=off, scalar2=PI, op0=ALU.add, op1=ALU.min), r=[b_rt], w=[b_rt])
                k.op("dve", lambda e: e.tensor_scalar(out=rt[:], in0=rt[:], scalar1=-PI, scalar2=None, op0=ALU.max), r=[b_rt], w=[b_rt])
                k.op("act", lambda e, dstT=dstT: e.activation(out=dstT[:], in_=rt[:], func=AF.Sin), r=[b_rt], w=[b_rope])
            k.op("dve", lambda e: e.tensor_scalar(out=sinT[:], in0=sinT[:], scalar1=rc[:, 1:2], scalar2=None, op0=ALU.mult), r=[b_rope, b_rc], w=[b_rope])
            k.barrier()

        qr32 = sb("qr32", [32, 512], stack=ast_); b_qr = Buf("qr32")
        rtmp = sb("rtmp", [32, 2, 512], stack=ast_); b_rtmp = Buf("rtmp")

        def rope_rows(ps, pbuf, src_cols, tab_col0, n, dst, b_dst):
            k.op("act", lambda e: e.activation(out=qr32[:, 0:n], in_=ps[0:32, src_cols:src_cols + n], func=AF.Identity), r=[pbuf], w=[b_qr])
            k.op("pe", lambda e: e.matmul(bank[2][0:32, 0:n], lhsT=psw[:], rhs=qr32[:, 0:n], start=True, stop=True), r=[b_psw, b_qr], w=[bkb[2]])
            k.op("dve", lambda e: e.tensor_tensor(out=rtmp[:, 0, 0:n], in0=qr32[:, 0:n], in1=cosT[:, tab_col0:tab_col0 + n], op=ALU.mult), r=[b_qr, b_rope], w=[b_rtmp])
            k.op("dve", lambda e: e.tensor_tensor(out=rtmp[:, 1, 0:n], in0=bank[2][0:32, 0:n], in1=sinT[:, tab_col0:tab_col0 + n], op=ALU.mult), r=[bkb[2], b_rope], w=[b_rtmp])
            k.op("dve", lambda e: e.tensor_tensor(out=dst, in0=rtmp[:, 0, 0:n], in1=rtmp[:, 1, 0:n], op=ALU.add), r=[b_rtmp], w=[b_dst])

        kT_all = sb("kT_all", [128, 4, 1280], BF16, stack=ast_); b_kT = Buf("kT")
        v_tok = sb("v_tok", [128, 10, 512], BF16, stack=ast_); b_v = Buf("v")

        hTgL = xsL = BtL = ue = dg = dtraw = dts = adt = ex = wv = etfa = xw = sst = vT = None
        b_hTg = [Buf("hTg0"), Buf("hTg1")]; b_xs = [Buf("xs0"), Buf("xs1")]; b_Bt = [Buf("Bt0"), Buf("Bt1")]
        b_ue = [Buf("ue0"), Buf("ue1")]; b_dg = [Buf("dg0"), Buf("dg1")]
        b_dtraw = [Buf("dtraw0"), Buf("dtraw1")]
        b_dts = [[Buf("dts") for _ in range(3)] for _ in range(2)]; b_adt = [[Buf("adt") for _ in range(3)] for _ in range(2)]
        b_ex = [[Buf("ex") for _ in range(3)] for _ in range(2)]; b_wv = [[Buf("wv") for _ in range(3)] for _ in range(2)]
        b_etfa = Buf("etfa"); b_xw = [Buf("xw0"), Buf("xw1")]; b_sst = Buf("sst"); b_vT = Buf("vT")
        uec = [0]

        def alloc_group(gst, tag, p1):
            nonlocal hTgL, xsL, BtL, ue, dg, dtraw, dts, adt, ex, wv, etfa, xw, sst, vT
            npar = 2 if p1 else 1
            hTgL = [sb("hTg" + tag, [128, KC, 3 * SW], BF16, stack=gst) for _ in range(npar)]
            xsL = [sb("xs_tok" + tag, [128, 3, 2048], BF16, stack=gst) for _ in range(npar)]
            BtL = [sb("B_tok" + tag, [128, 3, 512], BF16, stack=gst) for _ in range(npar)]
            ue = [sb("ue%d" % i + tag, [128, 3 * SW], BF16, stack=gst) for i in range(2)]
            dg = [sb("dg%d" % i + tag, [128, 6, 128], BF16, stack=gst) for i in range(2)]
            dtraw = sb("dtraw" + tag, [128, npar, 3, 64], stack=gst)
            dts = sb("dts" + tag, [128, npar, 3, 64], stack=gst)
            adt = sb("adt" + tag, [128, npar, 3, 64], stack=gst)
            ex = sb("ex" + tag, [128, npar, 3, 192], stack=gst)
            wv = sb("wv" + tag, [128, npar, 3, 64], stack=gst)
            etfa = sb("etfa" + tag, [128, 32], stack=gst)
            xw = [sb("xw%d" % i + tag, [128, 2048], BF16, stack=gst) for i in range(2 if p1 else 1)]
            if p1:
                sst = sb("sst" + tag, [128, 1024], stack=gst)
                vT = sb("vT" + tag, [128, 3 * SW], BF16, stack=gst)

        from collections import deque
        bgq = deque()

        def bg_step(n=3):
            for _ in range(n):
                if bgq:
                    bgq.popleft()()

        def bg_drain():
            while bgq:
                bgq.popleft()()

        def hT_tile_tasks(src_rows, nrows, dst_fn, b_dst):
            tasks = [lambda: k.dma("sp", xt[0:nrows, :], src_rows, w=[b_xt])]
            for k0 in range(0, KC, 4):
                def t(k0=k0):
                    pb = 6 + (k0 // 4) % 2
                    for j in range(4):
                        kc = k0 + j
                        k.op("pe", lambda e, kc=kc, j=j: e.transpose(out=bank[pb][:, j * 128:j * 128 + nrows], in_=xt[0:nrows, kc * 128:(kc + 1) * 128], identity=ident[0:nrows, 0:nrows]), r=[b_xt, b_cm], w=[bkb[pb]])
                    for j in range(4):
                        kc = k0 + j
                        for (dst, src) in dst_fn(kc, bank[pb][:, j * 128:j * 128 + nrows]):
                            k.op("act", lambda e, kc=kc, dst=dst, src=src: e.activation(out=dst, in_=src, func=AF.Identity, scale=mA[:, 32 + kc:33 + kc], bias=mA[:, kc:kc + 1]), r=[bkb[pb], b_mcA], w=[b_dst])
                tasks.append(t)
            return tasks

        def group_hT_tasks(slots, par):
            ns = len(slots); s0 = slots[0]
            h = hTgL[par]
            tasks = []
            for si, s in enumerate(slots):
                tasks += hT_tile_tasks(xm[s * 128:(s + 1) * 128, :], 128, lambda kc, src, si=si: [(h[:, kc, si * SW + 2:si * SW + 130], src)], b_hTg[par])

            def hdst(kc, src):
                hv = V(h[:, kc, 0:ns * SW], SW)
                sv = V(src, 4)
                return [(hv[:, :, 0:2], sv[:, :, 0:2]), (hv[:, :, 130:132], sv[:, :, 2:4])]
            tasks += hT_tile_tasks(xh[4 * s0:4 * s0 + 4 * ns, :], 4 * ns, hdst, b_hTg[par])
            return tasks

        def conv_epi(blk_ch, ps, pbuf, slots, dst_tok, b_dsttok, dst_col0, feat=None):
            ns = len(slots); s0 = slots[0]
            i = uec[0] % 2; uec[0] += 1
            u = ue[i]
            k.op("act", lambda e: e.activation(out=u[:, 0:ns * SW], in_=ps[:, 0:ns * SW], func=AF.Identity), r=[pbuf], w=[b_ue[i]])
            u3 = V(u[:, 0:ns * SW], SW)
            k.op("dve", lambda e: e.tensor_tensor(out=u3[:, :, 0:2], in0=u3[:, :, 0:2], in1=hm[:, s0:s0 + ns, 0:2], op=ALU.mult), r=[b_hm, b_ue[i]], w=[b_ue[i]])
            k.op("dve", lambda e: e.tensor_tensor(out=u3[:, :, 130:132], in0=u3[:, :, 130:132], in1=hm[:, s0:s0 + ns, 2:4], op=ALU.mult), r=[b_hm, b_ue[i]], w=[b_ue[i]])
            if dst_tok is not None:
                for j in range(5):
                    k.op("dve", lambda e, j=j: e.tensor_scalar(out=dg[i][:, j, :], in0=identb, scalar1=cw[:, blk_ch, j:j + 1], scalar2=None, op0=ALU.mult), r=[b_c16, b_cw], w=[b_dg[i]])
                k.op("dve", lambda e: e.tensor_scalar(out=dg[i][:, 5, :], in0=identb, scalar1=cbc[:, 0, blk_ch:blk_ch + 1], scalar2=None, op0=ALU.mult), r=[b_c16, b_cbc], w=[b_dg[i]])
                pass

            def late():
                if dst_tok is not None:
                    pc = 3
                    for si in range(ns):
                        o = bank[pc][:, si * 128:(si + 1) * 128]
                        for j in range(5):
                            k.op("pe", lambda e, j=j, si=si, o=o: e.matmul(o, lhsT=u[:, si * SW + j:si * SW + j + 128], rhs=dg[i][:, j, :], start=(j == 0), stop=False), r=[b_ue[i], b_dg[i]], w=[bkb[pc]])
                        k.op("pe", lambda e, o=o: e.matmul(o, lhsT=onesb, rhs=dg[i][:, 5, :], start=False, stop=True), r=[b_c16, b_dg[i]], w=[bkb[pc]])
                    k.op("act", lambda e: e.activation(out=dst_tok[:, 0:ns, dst_col0:dst_col0 + 128], in_=V(bank[pc][:, 0:ns * 128], 128), func=AF.Silu), r=[bkb[pc]], w=[b_dsttok])
                if feat is not None:
                    dstT, b_dstT, acc, b_acc = feat
                    k.op("dve", lambda e: e.tensor_scalar(out=acc[:, 0:ns, :], in0=u3[:, :, 0:128], scalar1=cw[:, blk_ch, 0:1], scalar2=None, op0=ALU.mult), r=[b_ue[i], b_cw], w=[b_acc])
                    for j in range(1, 5):
                        k.op("dve", lambda e, j=j: e.scalar_tensor_tensor(out=acc[:, 0:ns, :], in0=u3[:, :, j:j + 128], scalar=cw[:, blk_ch, j:j + 1], in1=acc[:, 0:ns, :], op0=ALU.mult, op1=ALU.add), r=[b_ue[i], b_cw, b_acc], w=[b_acc])
                    k.op("act", lambda e: e.activation(out=dstT[:, 0:ns, :], in_=acc[:, 0:ns, :], func=AF.Silu, bias=cbc[:, 0, blk_ch:blk_ch + 1]), r=[b_acc, b_cbc], w=[b_dstT])

            return late

        def hT_fn(ns, par=0):
            return lambda kc: hTgL[par][:, kc, 0:ns * SW]

        def dt_matmuls(slots, par):
            ns = len(slots)
            for si in range(ns):
                for kc in range(KC):
                    k.op("pe", lambda e, kc=kc, si=si: e.matmul(bank[2][:, si * 64:(si + 1) * 64], lhsT=hTgL[par][:, kc, si * SW + 2:si * SW + 130], rhs=wdt[:, kc, :], start=(kc == 0), stop=(kc == KC - 1)), r=[b_hTg[par], b_wdt], w=[bkb[2]])
            k.op("dve", lambda e: e.tensor_copy(out=dtraw[:, par, 0:ns, :], in_=V(bank[2][:, 0:ns * 64], 64)), r=[bkb[2]], w=[b_dtraw[par]])

        def chain_dt_tasks(slots, par):
            ns = len(slots)
            T = []
            rng = range(ns)
            T.append(lambda: [k.op("dve", lambda e, si=si: e.tensor_tensor(out=dts[:, par, si, :], in0=dtraw[:, par, si, :], in1=hp[:, 0:64], op=ALU.add), r=[b_dtraw[par], b_hp], w=[b_dts[par][si]]) for si in rng])
            T.append(lambda: [k.op("act", lambda e, si=si: e.activation(out=dts[:, par, si, :], in_=dts[:, par, si, :], func=AF.Exp), r=[b_dts[par][si]], w=[b_dts[par][si]]) for si in rng])
            T.append(lambda: [k.op("act", lambda e, si=si: e.activation(out=dts[:, par, si, :], in_=dts[:, par, si, :], func=AF.Ln, bias=1.0), r=[b_dts[par][si]], w=[b_dts[par][si]]) for si in rng])
            T.append(lambda: [k.op("dve", lambda e, si=si: e.tensor_tensor(out=adt[:, par, si, :], in0=dts[:, par, si, :], in1=hp[:, 64:128], op=ALU.mult), r=[b_dts[par][si], b_hp], w=[b_adt[par][si]]) for si in rng])

            def emats():
                for si in rng:
                    pb = 2 + si % 2
                    for (c0, n, M, a0) in ((0, 32, Uin, 0), (32, 32, Lin, 32), (64, 32, Lst, 0), (96, 32, Ust, 32), (128, 64, ones, 0)):
                        k.op("pe", lambda e, c0=c0, n=n, M=M, a0=a0, si=si, pb=pb: e.matmul(bank[pb][:, c0:c0 + n], lhsT=M, rhs=adt[:, par, si, a0:a0 + n], start=True, stop=True), r=[b_cm, b_adt[par][si]], w=[bkb[pb]])
                    k.op("act", lambda e, si=si, pb=pb: e.activation(out=ex[:, par, si, :], in_=bank[pb][:, 0:192], func=AF.Exp), r=[bkb[pb]], w=[b_ex[par][si]])
            T.append(emats)
            T.append(lambda: [k.op("dve", lambda e, si=si: e.tensor_tensor(out=wv[:, par, si, :], in0=dts[:, par, si, :], in1=ex[:, par, si, 64:128], op=ALU.mult), r=[b_dts[par][si], b_ex[par][si]], w=[b_wv[par][si]]) for si in rng])
            return T

        def bc_hp(ap32, nh=32):
            return ap32.unsqueeze(2).broadcast_to([128, nh, 64])

        def slot_states(si, s, do_f, do_b, act_f_col=None, par=0):
            xs3 = V(xsL[par][:, si, :], 64)
            wv_ = wv[:, par, si, :]; ex_ = ex[:, par, si, :]
            bwv = b_wv[par][si]; bex = b_ex[par][si]; B_tok = BtL[par]; bBt = b_Bt[par]; bxs = b_xs[par]
            if do_f:
                k.op("dve", lambda e: e.tensor_tensor(out=V(xw[0][:], 64), in0=xs3, in1=bc_hp(wv_[:, 0:32]), op=ALU.mult), r=[bxs, bwv], w=[b_xw[0]])
                if act_f_col is not None:
                    k.op("dve", lambda e: e.tensor_scalar(out=etfa[:, 0:32], in0=ex_[:, 128:160], scalar1=act_f_col, scalar2=None, op0=ALU.mult), r=[bex, b_af], w=[b_etfa])
                    ecol = etfa[:, 0:32]
                else:
                    ecol = ex_[:, 128:160]
                for g in range(4):
                    pb = 4 + g % 2
                    k.op("pe", lambda e, g=g, pb=pb: e.matmul(bank[pb][:, :], lhsT=B_tok[:, si, g * 128:(g + 1) * 128], rhs=xw[0][:, g * 512:(g + 1) * 512], start=True, stop=True), r=[bBt, b_xw[0]], w=[bkb[pb]])
                    hg = V(Hf[:, g * 512:(g + 1) * 512], 64)
                    k.op("dve", lambda e, g=g, hg=hg: e.tensor_tensor(out=hg, in0=hg, in1=bc_hp(ecol[:, g * 8:(g + 1) * 8], 8), op=ALU.mult), r=[bex, b_etfa], w=[b_Hf])
                    if act_f_col is not None:
                        k.op("dve", lambda e, g=g, pb=pb: e.scalar_tensor_tensor(out=Hf[:, g * 512:(g + 1) * 512], in0=bank[pb][:, :], scalar=act_f_col, in1=Hf[:, g * 512:(g + 1) * 512], op0=ALU.mult, op1=ALU.add), r=[bkb[pb], b_af], w=[b_Hf])
                    else:
                        k.op("dve", lambda e, g=g, pb=pb: e.tensor_tensor(out=Hf[:, g * 512:(g + 1) * 512], in0=Hf[:, g * 512:(g + 1) * 512], in1=bank[pb][:, :], op=ALU.add), r=[bkb[pb]], w=[b_Hf])
            if do_b:
                k.op("dve", lambda e: e.tensor_tensor(out=V(xw[1][:], 64), in0=xs3, in1=bc_hp(wv_[:, 32:64]), op=ALU.mult), r=[bxs, bwv], w=[b_xw[1]])
                for g in range(4):
                    pb = 4 + g % 2
                    k.op("pe", lambda e, g=g, pb=pb: e.matmul(bank[pb][:, :], lhsT=B_tok[:, si, g * 128:(g + 1) * 128], rhs=xw[1][:, g * 512:(g + 1) * 512], start=True, stop=True), r=[bBt, b_xw[1]], w=[bkb[pb]])
                    k.op("act", lambda e, g=g, pb=pb: e.activation(out=sst[:, (g % 2) * 512:(g % 2 + 1) * 512], in_=bank[pb][:, :], func=AF.Identity), r=[bkb[pb]], w=[b_sst])
                    if g % 2 == 1:
                        k.dma("sp", sb_d[s][:, (g - 1) * 512:(g + 1) * 512], sst[:], r=[b_sst])
                k.dma("sp", eb_d[s], ex_[:, 160:192], r=[bex])

        gst1 = ExitStack()
        alloc_group(gst1, "a", True)
        bg_hook[0] = lambda: bg_step(3)

        def chain_all_tasks(G, par):
            T = chain_dt_tasks(G, par)
            for si, s in enumerate(G):
                T.append(lambda si=si, s=s: slot_states(si, s, do_f=(s >= 8), do_b=False, act_f_col=af_t[:, s:s + 1], par=par))
                T.append(lambda si=si, s=s: slot_states(si, s, do_f=False, do_b=True, par=par))
            return T

        for t_ in group_hT_tasks(GROUPS[0], 0):
            t_()
        for gi, G in enumerate(GROUPS):
            ns = len(G)
            par = gi % 2
            bg_drain()
            ta = group_hT_tasks(GROUPS[gi + 1], 1 - par) if gi + 1 < len(GROUPS) else []
            tb = chain_all_tasks(GROUPS[gi - 1], 1 - par) if gi >= 1 else []
            while ta or tb:
                if ta:
                    bgq.append(ta.pop(0))
                    if ta:
                        bgq.append(ta.pop(0))
                if tb:
                    bgq.append(tb.pop(0))
            xs_, bxs_, Bt_, bBt_ = xsL[par], b_xs[par], BtL[par], b_Bt[par]
            gemm(hT_fn(ns, par), ns * SW, KC, w_in, CX, 16, lambda blk, ps, pbuf, G=G, xs_=xs_, bxs_=bxs_: conv_epi(blk, ps, pbuf, G, xs_, bxs_, blk * 128), [b_hTg[par]], two_phase=True)
            gemm(hT_fn(ns, par), ns * SW, KC, w_in, CB, 4, lambda blk, ps, pbuf, G=G, Bt_=Bt_, bBt_=bBt_: conv_epi(16 + blk, ps, pbuf, G, Bt_, bBt_, blk * 128), [b_hTg[par]], two_phase=True)
            kvs = [(si, s) for si, s in enumerate(G) if s in KVSLOT]
            if kvs:
                def epi_k(blk, ps, pbuf, kvs=kvs):
                    for si, s in kvs:
                        ti = KVSLOT[s]
                        k.op("act", lambda e, si=si, ti=ti: e.activation(out=kT_all[:, blk, ti * 128:(ti + 1) * 128], in_=ps[:, si * SW + 2:si * SW + 130], func=AF.Identity), r=[pbuf], w=[b_kT])
                        rope_rows(ps, pbuf, si * SW + 2, ti * 128, 128, kT_all[0:32, blk, ti * 128:(ti + 1) * 128], b_kT)
                gemm(hT_fn(ns, par), ns * SW, KC, w_in, CK, 4, epi_k, [b_hTg[par]])

                def epi_v(blk, ps, pbuf, kvs=kvs, ns=ns):
                    k.op("act", lambda e: e.activation(out=vT[:, 0:ns * SW], in_=ps[:, 0:ns * SW], func=AF.Identity), r=[pbuf], w=[b_vT])
                    pvb = bank[3].bitcast(BF16)
                    for si, s in kvs:
                        k.op("pe", lambda e, si=si: e.transpose(out=pvb[:, si * 128:(si + 1) * 128], in_=vT[:, si * SW + 2:si * SW + 130], identity=identb), r=[b_vT, b_c16], w=[bkb[3]])
                    for si, s in kvs:
                        ti = KVSLOT[s]
                        k.op("dve", lambda e, si=si, ti=ti: e.tensor_copy(out=v_tok[:, ti, blk * 128:(blk + 1) * 128], in_=pvb[:, si * 128:(si + 1) * 128]), r=[bkb[3]], w=[b_v])
                gemm(hT_fn(ns, par), ns * SW, KC, w_in, CV, 4, epi_v, [b_hTg[par]])
            flush_epi()
            dt_matmuls(G, par)
        bg_drain()
        for t_ in chain_all_tasks(GROUPS[-1], (len(GROUPS) - 1) % 2):
            t_()
        bg_hook[0] = lambda: None
        if "hf" in dbg:
            k.dma("sp", dbg_out["hf"], Hf[:], r=[b_Hf], is_out=True)
        if "kv" in dbg:
            with ExitStack() as ps_:
                t1 = sb("dbgt1", [128, 4 * 1280], stack=ps_); t2 = sb("dbgt2", [128, 10 * 512], stack=ps_)
                bt1, bt2 = Buf("t1"), Buf("t2")
                k.op("dve", lambda e: e.tensor_copy(out=t1[:], in_=kT_all[:].rearrange("p a b -> p (a b)")), r=[b_kT], w=[bt1])
                k.op("dve", lambda e: e.tensor_copy(out=t2[:], in_=v_tok[:].rearrange("p a b -> p (a b)")), r=[b_v], w=[bt2])
                k.dma("sp", dbg_out["kT"], t1[:], r=[bt1], is_out=True)
                k.dma("sp", dbg_out["v"], t2[:], r=[bt2], is_out=True)
                k.barrier()

        gst1.close()
        k.barrier()
        p2s = ExitStack()
        Hb = sb("Hb", [128, 2048], stack=p2s); b_Hb = Buf("Hb")
        sbt = [sb("sbt%d" % i, [128, 2048], stack=p2s) for i in range(2)]; b_sbt = [Buf("sbt0"), Buf("sbt1")]
        ebt = [sb("ebt%d" % i, [128, 32], stack=p2s) for i in range(2)]; b_ebt = [Buf("ebt0"), Buf("ebt1")]
        hsv = [sb("hsv%d" % i, [128, 2048], BF16, stack=p2s) for i in range(2)]; b_hsv = [Buf("hsv0"), Buf("hsv1")]
        k.op("dve", lambda e: e.memset(Hb[:], 0.0), w=[b_Hb])

        def p2_step(n_):
            s = 31 - n_
            i = n_ % 2
            k.dma("sp", sbt[i][:], sb_d[s], w=[b_sbt[i]])
            k.dma("sp", ebt[i][:], eb_d[s], w=[b_ebt[i]])
            if s <= 7:
                k.op("act", lambda e: e.activation(out=hsv[i][:], in_=Hb[:], func=AF.Identity), r=[b_Hb], w=[b_hsv[i]])
                k.dma("sp", hb_d[s], hsv[i][:], r=[b_hsv[i]])
            k.op("dve", lambda e: e.tensor_scalar(out=ebt[i][:], in0=ebt[i][:], scalar1=ab_t[:, s:s + 1], scalar2=None, op0=ALU.mult), r=[b_ab, b_ebt[i]], w=[b_ebt[i]])
            k.op("pool", lambda e: e.tensor_tensor(out=V(Hb[:], 64), in0=V(Hb[:], 64), in1=bc_hp(ebt[i][:]), op=ALU.mult), r=[b_ebt[i], b_Hb], w=[b_Hb])
            k.op("pool", lambda e: e.tensor_scalar(out=sbt[i][:], in0=sbt[i][:], scalar1=ab_t[:, s:s + 1], scalar2=None, op0=ALU.mult), r=[b_ab, b_sbt[i]], w=[b_sbt[i]])
            k.op("pool", lambda e: e.tensor_tensor(out=Hb[:], in0=Hb[:], in1=sbt[i][:], op=ALU.add), r=[b_sbt[i], b_Hb], w=[b_Hb])
        p2_tasks = [(lambda n_=n_: p2_step(n_)) for n_ in range(32)]

        with ExitStack() as ps_:
            hTa = sb("hTa", [128, KC, 256], BF16, stack=ps_); b_hTa = Buf("hTa")
            qTg = sb("qTg", [128, 16, 256], BF16, stack=ps_); b_qT = Buf("qTg")
            amb = sb("amb", [128, 3, 384], BF16, stack=ps_); b_amb = Buf("amb")
            k.dma("pool", amb[:], amask.rearrange("a p c -> p a c"), w=[b_amb])
            ao = [sb("ao%d" % i, [128, 2048], stack=ps_) for i in range(2)]; b_ao = [Buf("ao0"), Buf("ao1")]
            aon = sb("aon", [128, 2048], BF16, stack=ps_); b_aon = Buf("aon")
            Pm = [sb("Pm%d" % i, [128, 384], BF16, stack=ps_) for i in range(2)]; b_Pm = [Buf("Pm0"), Buf("Pm1")]
            PT = [sb("PT%d" % i, [128, 3, 128], BF16, stack=ps_) for i in range(2)]; b_PT = [Buf("PT0"), Buf("PT1")]
            st = [sb("ast%d" % i, [128, 8], stack=ps_) for i in range(2)]; b_st = [Buf("ast0"), Buf("ast1")]
            sq = sb("asq", [128, 20], stack=ps_); b_sq = Buf("asq")
            aosq = sb("aosq", [128, 2048], stack=ps_); b_aosq = Buf("aosq")
            pc = [0]
            for pr in range(4):
                for si in range(2):
                    s = pr * 2 + si
                    make_hT(xm[s * 128:(s + 1) * 128, :], 128, lambda kc, src, si=si: [(hTa[:, kc, si * 128:(si + 1) * 128], src)], b_hTa)

                def epi_q(blk, ps, pbuf, pr=pr):
                    k.op("act", lambda e: e.activation(out=qTg[:, blk, :], in_=ps[:, 0:256], func=AF.Identity), r=[pbuf], w=[b_qT])
                    rope_rows(ps, pbuf, 0, pr * 256, 256, qTg[0:32, blk, :], b_qT)
                gemm(lambda kc: hTa[:, kc, :], 256, KC, w_in, CQ, 16, epi_q, [b_hTa])
                flush_epi()
                items = [(si, hq) for si in range(2) for hq in range(16)]

                def geo(n):
                    si, hq = items[n]
                    c = pr * 2 + si
                    kvt = [9 if c == 0 else c - 1, c, c + 1]
                    mt = 0 if c == 0 else (2 if c == 7 else 1)
                    return si, hq, c, kvt, mt, hq // 4, n % 2

                def stA(n):
                    si, hq, c, kvt, mt, g, i = geo(n)
                    pS = 2 + i
                    k.op("pe", lambda e: e.matmul(bank[pS][:, 0:384], lhsT=identb, rhs=amb[:, mt, :], start=True, stop=False), r=[b_c16, b_amb], w=[bkb[pS]])
                    for kb in range(3):
                        k.op("pe", lambda e, kb=kb: e.matmul(bank[pS][:, kb * 128:(kb + 1) * 128], lhsT=qTg[:, hq, si * 128:(si + 1) * 128], rhs=kT_all[:, g, kvt[kb] * 128:(kvt[kb] + 1) * 128], start=False, stop=(kb == 2)), r=[b_qT, b_kT], w=[bkb[pS]])

                def stB(n):
                    si, hq, c, kvt, mt, g, i = geo(n)
                    pS = 2 + i
                    s_ = st[i]; bs_ = b_st[i]
                    k.op("dve", lambda e: e.reduce_max(out=s_[:, 0:1], in_=bank[pS][:, 0:384], axis=mybir.AxisListType.X), r=[bkb[pS]], w=[bs_])
                    k.op("dve", lambda e: e.tensor_scalar(out=s_[:, 1:2], in0=s_[:, 0:1], scalar1=-SCALE, scalar2=None, op0=ALU.mult), r=[bs_], w=[bs_])
                    k.op("dve", lambda e: e.tensor_scalar(out=s_[:, 1:2], in0=s_[:, 1:2], scalar1=nsink[:, hq:hq + 1], scalar2=None, op0=ALU.min), r=[bs_, b_nsink], w=[bs_])
                    k.op("act", lambda e: e.activation(out=Pm[i][:], in_=bank[pS][:, 0:384], func=AF.Exp, scale=SCALE, bias=s_[:, 1:2]), r=[bkb[pS], bs_], w=[b_Pm[i]])
                    k.op("act", lambda e: e.activation(out=s_[:, 3:4], in_=s_[:, 1:2], func=AF.Exp, bias=sink_t[:, hq:hq + 1]), r=[bs_, b_sink], w=[bs_])
                    k.op("dve", lambda e: e.reduce_sum(out=s_[:, 2:3], in_=Pm[i][:], axis=mybir.AxisListType.X), r=[b_Pm[i]], w=[bs_])
                    k.op("dve", lambda e: e.tensor_tensor(out=s_[:, 4:5], in0=s_[:, 2:3], in1=s_[:, 3:4], op=ALU.add), r=[bs_], w=[bs_])
                    k.op("dve", lambda e: e.reciprocal(out=s_[:, 5:6], in_=s_[:, 4:5]), r=[bs_], w=[bs_])

                def stC(n):
                    si, hq, c, kvt, mt, g, i = geo(n)
                    pT = 4 + i
                    pvb = bank[pT].bitcast(BF16)
                    for kb in range(3):
                        k.op("pe", lambda e, kb=kb: e.transpose(out=pvb[:, kb * 128:(kb + 1) * 128], in_=Pm[i][:, kb * 128:(kb + 1) * 128], identity=identb), r=[b_Pm[i], b_c16], w=[bkb[pT]])
                    k.op("act", lambda e: e.activation(out=PT[i][:].rearrange("p a b -> p (a b)"), in_=pvb[:, 0:384], func=AF.Identity), r=[bkb[pT]], w=[b_PT[i]])

                def stD(n):
                    si, hq, c, kvt, mt, g, i = geo(n)
                    pO = 6 + i
                    for kb in range(3):
                        k.op("pe", lambda e, kb=kb: e.matmul(bank[pO][:, 0:128], lhsT=PT[i][:, kb, :], rhs=v_tok[:, kvt[kb], g * 128:(g + 1) * 128], start=(kb == 0), stop=(kb == 2)), r=[b_PT[i], b_v], w=[bkb[pO]])
                    k.op("dve", lambda e: e.tensor_scalar(out=ao[si][:, hq * 128:(hq + 1) * 128], in0=bank[pO][:, 0:128], scalar1=st[i][:, 5:6], scalar2=None, op0=ALU.mult), r=[bkb[pO], b_st[i]], w=[b_ao[si]])
                    if hq == 15:
                        k.op("dve", lambda e: e.tensor_tensor(out=aosq[:], in0=ao[si][:], in1=ao[si][:], op=ALU.mult), r=[b_ao[si]], w=[b_aosq])
                        k.op("dve", lambda e: e.reduce_sum(out=sq[:, 16:17], in_=aosq[:], axis=mybir.AxisListType.X), r=[b_aosq], w=[b_sq])
                        k.op("dve", lambda e: e.tensor_scalar(out=sq[:, 17:18], in0=sq[:, 16:17], scalar1=1.0 / 2048, scalar2=RMS_EPS, op0=ALU.mult, op1=ALU.add), r=[b_sq], w=[b_sq])
                        k.op("act", lambda e: e.activation(out=sq[:, 18:19], in_=sq[:, 17:18], func=AF.Ln), r=[b_sq], w=[b_sq])
                        k.op("act", lambda e: e.activation(out=sq[:, 18:19], in_=sq[:, 18:19], func=AF.Exp, scale=-0.5), r=[b_sq], w=[b_sq])
                        k.op("dve", lambda e: e.tensor_scalar(out=aon[:], in0=ao[si][:], scalar1=sq[:, 18:19], scalar2=None, op0=ALU.mult), r=[b_ao[si], b_sq], w=[b_aon])
                        k.dma("sp", mix_d[c * 128:(c + 1) * 128, 0:2048], aon[:], r=[b_aon])

                stA(0)
                for n in range(len(items)):
                    if n + 1 < len(items):
                        stA(n + 1)
                    stB(n)
                    if not nomod:
                        ada_unit(64 + pr * 32 + n)
                        flush_epi()
                    stC(n)
                    stD(n)
                    if p2_tasks and n % 4 == 3:
                        p2_tasks.pop(0)()
            k.barrier()

        while p2_tasks:
            p2_tasks.pop(0)()
        k.barrier()
        p2s.close()
        if "hb" in dbg:
            k.dma("sp", dbg_out["hb"], hb_d, is_out=True)
        ast_.close()
        k.barrier()
        gst2 = ExitStack()
        alloc_group(gst2, "b", False)
        with ExitStack() as ps_:
            BT = sb("BT", [128, 4, 3, 128], BF16, stack=ps_); b_BT = Buf("BT")
            CT = sb("CT", [128, 4, 3, 128], BF16, stack=ps_); b_CT = Buf("CT")
            cacc = sb("cacc", [128, 3, 128], stack=ps_); b_cacc = Buf("cacc")
            gz = sb("gz", [128, 3, 2048], BF16, stack=ps_); b_gz = Buf("gz")
            zT = sb("zT", [128, 3 * SW], BF16, stack=ps_); b_zT = Buf("zT")
            R = [sb("R%d" % i, [128, 16, 128], stack=ps_) for i in range(2)]; b_R = [Buf("R0"), Buf("R1")]
            dec = [sb("dec%d" % i, [128, 512], stack=ps_) for i in range(2)]; b_dec = [Buf("dec0"), Buf("dec1")]
            Mt = [sb("Mt%d" % i, [128, 4, 128], BF16, stack=ps_) for i in range(2)]; b_Mt = [Buf("Mt0"), Buf("Mt1")]
            cbm = [sb("cbm%d" % i, [128, 4, 128], stack=ps_) for i in range(2)]; b_cbm = [Buf("cbF"), Buf("cbB")]
            xdt = [sb("xdt%d" % i, [128, 2048], BF16, stack=ps_) for i in range(2)]; b_xdt = [Buf("xdtf"), Buf("xdtb")]
            hin = [sb("hin%d" % i, [128, 2048], BF16, stack=ps_) for i in range(2)]; b_hin = [Buf("hinf"), Buf("hinb")]
            yacc = sb("yacc", [128, 2048], stack=ps_); b_yacc = Buf("yacc")
            ytmp = sb("ytmp", [128, 512], stack=ps_); b_ytmp = Buf("ytmp")
            ysq = sb("ysq", [128, 8], stack=ps_); b_ysq = Buf("ysq")
            yn = sb("yn", [128, 2048], BF16, stack=ps_); b_yn = Buf("yn")
            xs_tok = xsL[0]; B_tok = BtL[0]; b_xs0 = b_xs[0]; b_Bt0 = b_Bt[0]; b_hTg0 = b_hTg[0]
            for G in OWNG:
                ns = len(G)
                for t_ in group_hT_tasks(G, 0):
                    t_()
                gemm(hT_fn(ns), ns * SW, KC, w_in, CX, 16, lambda blk, ps, pbuf, G=G: conv_epi(blk, ps, pbuf, G, xs_tok, b_xs0, blk * 128), [b_hTg0], two_phase=True)
                gemm(hT_fn(ns), ns * SW, KC, w_in, CB, 4, lambda blk, ps, pbuf, G=G: conv_epi(16 + blk, ps, pbuf, G, B_tok, b_Bt0, blk * 128, feat=(BT[:, blk, :, :], b_BT, cacc, b_cacc)), [b_hTg0], two_phase=True)
                gemm(hT_fn(ns), ns * SW, KC, w_in, CC, 4, lambda blk, ps, pbuf, G=G: conv_epi(20 + blk, ps, pbuf, G, None, None, 0, feat=(CT[:, blk, :, :], b_CT, cacc, b_cacc)), [b_hTg0], two_phase=True)

                def epi_z(blk, ps, pbuf, ns=ns):
                    k.op("act", lambda e: e.activation(out=zT[:, 0:ns * SW], in_=ps[:, 0:ns * SW], func=AF.Silu), r=[pbuf], w=[b_zT])
                    pvb = bank[3].bitcast(BF16)
                    for si in range(ns):
                        k.op("pe", lambda e, si=si: e.transpose(out=pvb[:, si * 128:(si + 1) * 128], in_=zT[:, si * SW + 2:si * SW + 130], identity=identb), r=[b_zT, b_c16], w=[bkb[3]])
                    k.op("dve", lambda e: e.tensor_copy(out=gz[:, 0:ns, blk * 128:(blk + 1) * 128], in_=V(pvb[:, 0:ns * 128], 128)), r=[bkb[3]], w=[b_gz])
                gemm(hT_fn(ns), ns * SW, KC, w_in, CZ, 16, epi_z, [b_hTg0])
                flush_epi()
                dt_matmuls(G, 0)
                for t_ in chain_dt_tasks(G, 0):
                    t_()
                for si, s in enumerate(G):
                    dts_ = dts[:, 0, si, :]; adt_ = adt[:, 0, si, :]; ex_ = ex[:, 0, si, :]
                    bdts_ = b_dts[0][si]; badt_ = b_adt[0][si]; bex_ = b_ex[0][si]
                    xs3 = V(xs_tok[:, si, :], 64)
                    if s == 0:
                        dump("dts", dts_, [bdts_]); dump("ex", ex_, [bex_]); dump("xs", xs_tok[:, 0, :], [b_xs0], BF16)
                        dump("Btok", B_tok[:, 0, :], [b_Bt0], BF16); dump("BT", BT[:, :, 0, :], [b_BT], BF16); dump("CT", CT[:, :, 0, :], [b_CT], BF16)
                        dump("gz", gz[:, 0, :], [b_gz], BF16); dump("hf", Hf[:], [b_Hf])
                    k.dma("sp", hin[1][:], hb_d[s], w=[b_hin[1]])
                    k.op("act", lambda e: e.activation(out=hin[0][:], in_=Hf[:], func=AF.Identity), r=[b_Hf], w=[b_hin[0]])
                    for d in range(2):
                        k.op("dve", lambda e, d=d: e.tensor_tensor(out=V(xdt[d][:], 64), in0=xs3, in1=bc_hp(dts_[:, d * 32:(d + 1) * 32]), op=ALU.mult), r=[b_xs0, bdts_], w=[b_xdt[d]])
                    for g in range(4):
                        k.op("pe", lambda e, g=g: e.matmul(bank[3][:, g * 128:(g + 1) * 128], lhsT=BT[:, g, si, :], rhs=CT[:, g, si, :], start=True, stop=True), r=[b_BT, b_CT], w=[bkb[3]])
                    for d, M in ((0, Uin), (1, Lin)):
                        k.op("dve", lambda e, d=d, M=M: e.tensor_tensor(out=cbm[d][:], in0=V(bank[3][:, :], 128), in1=M.unsqueeze(1).broadcast_to([128, 4, 128]), op=ALU.mult), r=[bkb[3], b_cm], w=[b_cbm[d]])
                    for hh in range(2):
                        for d, M in ((0, Uin), (1, Lin)):
                            k.op("dve", lambda e, d=d, M=M: e.tensor_tensor(out=R[d][:], in0=M.unsqueeze(1).broadcast_to([128, 16, 128]), in1=adt_[:, d * 32 + hh * 16:d * 32 + hh * 16 + 16].unsqueeze(2).broadcast_to([128, 16, 128]), op=ALU.mult), r=[b_cm, badt_], w=[b_R[d]])
                        for q4 in range(4):
                            h0 = hh * 16 + q4 * 4
                            g = h0 // 8
                            for d, M2 in ((0, Lst), (1, Ust)):
                                pb = 2 + d
                                k.op("pe", lambda e, d=d, M2=M2, pb=pb: e.matmul(bank[pb][:, :], lhsT=M2, rhs=R[d][:, q4 * 4:(q4 + 1) * 4, :].rearrange("p a b -> p (a b)"), start=True, stop=True), r=[b_cm, b_R[d]], w=[bkb[pb]])
                                k.op("act", lambda e, d=d, pb=pb: e.activation(out=dec[d][:], in_=bank[pb][:, :], func=AF.Exp), r=[bkb[pb]], w=[b_dec[d]])
                                k.op("dve", lambda e, d=d, g=g: e.tensor_tensor(out=Mt[d][:], in0=V(dec[d][:], 128), in1=cbm[d][:, g, :].unsqueeze(1).broadcast_to([128, 4, 128]), op=ALU.mult), r=[b_dec[d], b_cbm[d]], w=[b_Mt[d]])
                            for i4 in range(4):
                                h = h0 + i4
                                pb = 4 + ((h % 16) // 8)
                                o = bank[pb][:, (h % 8) * 64:(h % 8) * 64 + 64]
                                k.op("pe", lambda e, o=o, i4=i4, h=h: e.matmul(o, lhsT=Mt[0][:, i4, :], rhs=xdt[0][:, h * 64:(h + 1) * 64], start=True, stop=False), r=[b_Mt[0], b_xdt[0]], w=[bkb[pb]])
                                k.op("pe", lambda e, o=o, i4=i4, h=h: e.matmul(o, lhsT=Mt[1][:, i4, :], rhs=xdt[1][:, h * 64:(h + 1) * 64], start=False, stop=True), r=[b_Mt[1], b_xdt[1]], w=[bkb[pb]])
                        for gg in range(2):
                            g = hh * 2 + gg
                            ys = yacc[:, g * 512:(g + 1) * 512]
                            k.op("dve", lambda e, g=g, ys=ys: e.tensor_tensor(out=V(ys, 64), in0=V(xs_tok[:, si, g * 512:(g + 1) * 512], 64), in1=bc_hp(hp[:, 128 + g * 8:128 + g * 8 + 8], 8), op=ALU.mult), r=[b_xs0, b_hp], w=[b_yacc])
                            k.op("dve", lambda e, gg=gg, ys=ys: e.tensor_tensor(out=ys, in0=ys, in1=bank[4 + gg][:, :], op=ALU.add), r=[bkb[4 + gg]], w=[b_yacc])
                            if s == 0:
                                dump("yd%d" % g, ys, [b_yacc])
                            for d in range(2):
                                pb = 2 + d
                                k.op("pe", lambda e, d=d, g=g, pb=pb: e.matmul(bank[pb][:, :], lhsT=CT[:, g, si, :], rhs=hin[d][:, g * 512:(g + 1) * 512], start=True, stop=True), r=[b_CT, b_hin[d]], w=[bkb[pb]])
                                k.op("dve", lambda e, d=d, g=g, pb=pb: e.tensor_tensor(out=V(ytmp[:], 64), in0=V(bank[pb][:, :], 64), in1=bc_hp(ex_[:, d * 32 + g * 8:d * 32 + g * 8 + 8], 8), op=ALU.mult), r=[bkb[pb], bex_], w=[b_ytmp])
                                k.op("dve", lambda e, ys=ys: e.tensor_tensor(out=ys, in0=ys, in1=ytmp[:], op=ALU.add), r=[b_ytmp], w=[b_yacc])
                    if s == 0:
                        dump("ypre", yacc[:], [b_yacc]); dump("hinb", hin[1][:], [b_hin[1]], BF16)
                    k.op("dve", lambda e: e.tensor_tensor(out=yacc[:], in0=yacc[:], in1=gz[:, si, :], op=ALU.mult), r=[b_gz], w=[b_yacc])
                    rsc = R[0][:].rearrange("p a b -> p (a b)")
                    k.op("dve", lambda e: e.tensor_tensor(out=rsc, in0=yacc[:], in1=yacc[:], op=ALU.mult), r=[b_yacc], w=[b_R[0]])
                    k.op("dve", lambda e: e.reduce_sum(out=ysq[:, 0:4], in_=V(rsc, 512), axis=mybir.AxisListType.X), r=[b_R[0]], w=[b_ysq])
                    k.op("dve", lambda e: e.tensor_scalar(out=ysq[:, 4:8], in0=ysq[:, 0:4], scalar1=1.0 / 512, scalar2=RMS_EPS, op0=ALU.mult, op1=ALU.add), r=[b_ysq], w=[b_ysq])
                    k.op("act", lambda e: e.activation(out=ysq[:, 4:8], in_=ysq[:, 4:8], func=AF.Ln), r=[b_ysq], w=[b_ysq])
                    k.op("act", lambda e: e.activation(out=ysq[:, 4:8], in_=ysq[:, 4:8], func=AF.Exp, scale=-0.5), r=[b_ysq], w=[b_ysq])
                    k.op("dve", lambda e: e.tensor_tensor(out=V(yn[:], 512), in0=V(yacc[:], 512), in1=ysq[:, 4:8].unsqueeze(2).broadcast_to([128, 4, 512]), op=ALU.mult), r=[b_yacc, b_ysq], w=[b_yn])
                    k.dma("sp", mix_d[s * 128:(s + 1) * 128, 2048:4096], yn[:], r=[b_yn])
                    slot_states(si, s, do_f=True, do_b=False, act_f_col=None)
            k.barrier()
        gst2.close()
        k.barrier()
        with ExitStack() as ps_:
            stg_ = sb("mcB_s", [64, 128], stack=ps_); bs_ = Buf("mcB_s")
            k.dma("sp", stg_[:], mod_d[3 * D:5 * D].rearrange("(a b) -> a b", b=128), w=[bs_])
            k.op("pe", lambda e: e.transpose(out=bank[2][:, 0:64], in_=stg_[:], identity=ident[0:64, 0:64]), r=[bs_, b_cm], w=[bkb[2]])
            k.op("dve", lambda e: e.tensor_copy(out=mB[:, 0:64], in_=bank[2][:, 0:64]), r=[bkb[2]], w=[b_mcB])
            k.op("dve", lambda e: e.tensor_scalar(out=mB[:, 32:64], in0=mB[:, 32:64], scalar1=1.0, scalar2=None, op0=ALU.add), r=[b_mcB], w=[b_mcB])
            if "mod" in dbg:
                k.dma("sp", dbg_out["mod"], mod_d, is_out=True)
            k.barrier()
        if "mix" in dbg:
            k.dma("sp", dbg_out["mix"], mix_d, is_out=True)

        if full:
            def store_T(ps, pbuf, stg, b_stg, ost, b_ost, dst_d, half, blk):
                k.op("act", lambda e: e.activation(out=stg[:], in_=ps[:, 0:512], func=AF.Identity), r=[pbuf], w=[b_stg])
                for tt in range(4):
                    k.op("pe", lambda e, tt=tt: e.transpose(out=bank[3][:, tt * 128:(tt + 1) * 128], in_=stg[:, tt * 128:(tt + 1) * 128], identity=ident), r=[b_stg, b_cm], w=[bkb[3]])
                k.op("dve", lambda e: e.tensor_copy(out=ost[:].rearrange("p a b -> p (a b)"), in_=bank[3][:, :]), r=[bkb[3]], w=[b_ost])
                k.dma("sp", dst_d[half * 512:(half + 1) * 512, blk * 128:(blk + 1) * 128].rearrange("(t p) c -> p t c", p=128), ost[:], r=[b_ost])

            with ExitStack() as ps_:
                mixT = sb("mixT", [128, KC, 1024], BF16, stack=ps_); b_mixT = Buf("mixT")
                mt_ = sb("mixtile", [128, D], BF16, stack=ps_); b_mt = Buf("mixtile")
                stg = sb("ostg", [128, 512], stack=ps_); b_stg = Buf("ostg")
                ost = sb("oost", [128, 4, 128], stack=ps_); b_ost = Buf("oost")
                for t in range(8):
                    k.dma("sp", mt_[:], mix_d[t * 128:(t + 1) * 128, :], w=[b_mt])
                    for k0 in range(0, KC, 8):
                        pb = 4 + (k0 // 8) % 2
                        pvb = bank[pb].bitcast(BF16)
                        for j in range(8):
                            kc = k0 + j
                            k.op("pe", lambda e, kc=kc, j=j, pvb=pvb: e.transpose(out=pvb[:, j * 128:(j + 1) * 128], in_=mt_[:, kc * 128:(kc + 1) * 128], identity=identb), r=[b_mt, b_c16], w=[bkb[pb]])
                        for j in range(8):
                            kc = k0 + j
                            nwc = anw[:, 0, kc:kc + 1] if kc < 16 else snw[:, 0, kc - 16:kc - 15]
                            k.op("act", lambda e, kc=kc, j=j, pvb=pvb, nwc=nwc: e.activation(out=mixT[:, kc, t * 128:(t + 1) * 128], in_=pvb[:, j * 128:(j + 1) * 128], func=AF.Identity, scale=nwc), r=[bkb[pb], b_anw, b_snw], w=[b_mixT])
                for half in range(2):
                    gemm(lambda kc, half=half: mixT[:, kc, half * 512:(half + 1) * 512], 512, KC, w_out, 0, 32,
                         lambda blk, ps, pbuf, half=half: store_T(ps, pbuf, stg, b_stg, ost, b_ost, mixr_d, half, blk), [b_mixT])
                flush_epi()
                k.barrier()
            if "mixr" in dbg:
                k.dma("sp", dbg_out["mixr"], mixr_d, is_out=True)

            def ln_phase(stack, tiles, br_d, res_fn, rows3, emit):
                rows = sb("rows", [8, D], stack=stack); b_rows = Buf("rows")
                sel = sb("selr", [8, 8, 128], stack=stack); b_sel = Buf("sel")
                k.dma("sp", sel[:].rearrange("p a b -> p (a b)"), sel_d[:, :], w=[b_sel])
                k.op("dve", lambda e: e.memset(rows[:], 0.0), w=[b_rows])
                k.dma("sp", rows[0:1, :], mod_d[2 * D:3 * D].rearrange("(a b) -> a b", a=1), w=[b_rows])
                k.dma("sp", rows[1:2, :], mod_d[5 * D:6 * D].rearrange("(a b) -> a b", a=1), w=[b_rows])
                for r_, src in ((2, ln1_g), (3, ln1_b), (4, ln2_g), (5, ln2_b)):
                    k.dma("sp", rows[r_:r_ + 1, :], src.rearrange("(a b) -> a b", a=1), w=[b_rows])
                bc3 = sb("bc3", [128, 3, D], stack=stack); b_bc3 = Buf("bc3")
                rt_ = sb("lnr", [128, D], stack=stack); b_rt = Buf("lnr")
                bst = sb("bst", [128, 8, 6], stack=stack); b_bst = Buf("bst")
                mv = sb("mv", [128, 4], stack=stack); b_mv = Buf("mv")
                for j, r_ in enumerate(rows3):
                    for n in range(8):
                        pb = 2 + n % 2
                        k.op("pe", lambda e, r_=r_, n=n, pb=pb: e.matmul(bank[pb][:, :], lhsT=sel[:, r_, :], rhs=rows[:, n * 512:(n + 1) * 512], start=True, stop=True), r=[b_sel, b_rows], w=[bkb[pb]])
                        k.op("act", lambda e, j=j, n=n, pb=pb: e.activation(out=bc3[:, j, n * 512:(n + 1) * 512], in_=bank[pb][:, :], func=AF.Identity), r=[bkb[pb]], w=[b_bc3])
                for t in tiles:
                    k.dma("sp", rt_[:], br_d[t * 128:(t + 1) * 128, :], w=[b_rt])
                    res_fn(t)
                    k.op("dve", lambda e: e.tensor_tensor(out=rt_[:], in0=rt_[:], in1=bc3[:, 0, :], op=ALU.mult), r=[b_bc3], w=[b_rt])
                    k.op("dve", lambda e: e.scalar_tensor_tensor(out=rt_[:], in0=xt[:], scalar=ALPHA, in1=rt_[:], op0=ALU.mult, op1=ALU.add), r=[b_xt], w=[b_rt])
                    for n in range(8):
                        k.op("dve", lambda e, n=n: e.bn_stats(out=bst[:, n, :], in_=rt_[:, n * 512:(n + 1) * 512]), r=[b_rt], w=[b_bst])
                    k.op("dve", lambda e: e.bn_aggr(out=mv[:, 0:2], in_=bst[:].rearrange("p a b -> p (a b)")), r=[b_bst], w=[b_mv])
                    k.op("dve", lambda e: e.tensor_scalar(out=mv[:, 2:3], in0=mv[:, 1:2], scalar1=LN_EPS, scalar2=None, op0=ALU.add), r=[b_mv], w=[b_mv])
                    k.op("act", lambda e: e.activation(out=mv[:, 2:3], in_=mv[:, 2:3], func=AF.Ln), r=[b_mv], w=[b_mv])
                    k.op("act", lambda e: e.activation(out=mv[:, 2:3], in_=mv[:, 2:3], func=AF.Exp, scale=-0.5), r=[b_mv], w=[b_mv])
                    k.op("dve", lambda e: e.tensor_scalar(out=rt_[:], in0=rt_[:], scalar1=mv[:, 0:1], scalar2=mv[:, 2:3], op0=ALU.subtract, op1=ALU.mult), r=[b_mv], w=[b_rt])
                    k.op("dve", lambda e: e.tensor_tensor(out=rt_[:], in0=rt_[:], in1=bc3[:, 1, :], op=ALU.mult), r=[b_bc3], w=[b_rt])
                    k.op("dve", lambda e: e.tensor_tensor(out=rt_[:], in0=rt_[:], in1=bc3[:, 2, :], op=ALU.add), r=[b_bc3], w=[b_rt])
                    emit(t, rt_, b_rt)

            for half in range(2):
                h2s = ExitStack()
                h2T = sb("h2T", [128, KC, 512], BF16, stack=h2s); b_h2T = Buf("h2T")
                with ExitStack() as ps_:
                    def res1(t):
                        k.dma("sp", xt[:], xm[t * 128:(t + 1) * 128, :], w=[b_xt])

                    def emit1(t, rt_, b_rt, half=half):
                        k.dma("sp", x1_d[t * 128:(t + 1) * 128, :], rt_[:], r=[b_rt])
                        tl = t - half * 4
                        transp_mod(rt_, b_rt, 128, lambda kc, src, tl=tl: [(h2T[:, kc, tl * 128:(tl + 1) * 128], src)], b_h2T, mB, b_mcB)
                    ln_phase(ps_, range(half * 4, half * 4 + 4), mixr_d, res1, (0, 2, 3), emit1)
                    k.barrier()
                with ExitStack() as ps_:
                    actT = sb("actT", [128, KCF, 512], BF16, stack=ps_); b_actT = Buf("actT")
                    sg = [sb("sg%d" % i, [128, 512], stack=ps_) for i in range(2)]; b_sg = [Buf("sg0"), Buf("sg1")]
                    stg = sb("fstg", [128, 512], stack=ps_); b_stg = Buf("fstg")
                    ost = sb("fost", [128, 4, 128], stack=ps_); b_ost = Buf("fost")
                    for blk in range(KCF):
                        i = blk % 2
                        gemm(lambda kc: h2T[:, kc, :], 512, KC, w_gate, blk * 128, 1,
                             lambda b_, ps, pbuf, i=i: k.op("act", lambda e: e.activation(out=sg[i][:], in_=ps[:, 0:512], func=AF.Silu), r=[pbuf], w=[b_sg[i]]), [b_h2T])
                        gemm(lambda kc: h2T[:, kc, :], 512, KC, w_up, blk * 128, 1,
                             lambda b_, ps, pbuf, i=i, blk=blk: k.op("dve", lambda e: e.tensor_tensor(out=actT[:, blk, :], in0=sg[i][:], in1=ps[:, 0:512], op=ALU.mult), r=[pbuf, b_sg[i]], w=[b_actT]), [b_h2T])
                    gemm(lambda kc: actT[:, kc, :], 512, KCF, w_down, 0, 32,
                         lambda blk, ps, pbuf, half=half: store_T(ps, pbuf, stg, b_stg, ost, b_ost, ffn_d, half, blk), [b_actT], units=3)
                    flush_epi()
                    k.barrier()
                h2s.close()
                k.barrier()
            if "x1" in dbg:
                k.dma("sp", dbg_out["x1"], x1_d, is_out=True)
            if "ffn" in dbg:
                k.dma("sp", dbg_out["ffn"], ffn_d, is_out=True)

            with ExitStack() as ps_:
                def res2(t):
                    k.dma("sp", xt[:], x1_d[t * 128:(t + 1) * 128, :], w=[b_xt])

                def emit2(t, rt_, b_rt):
                    k.dma("sp", y_out[t * 128:(t + 1) * 128, :], rt_[:], r=[b_rt], is_out=True)
                ln_phase(ps_, range(8), ffn_d, res2, (1, 4, 5), emit2)
                k.barrier()
        k.finish()
        print("instructions emitted:", k.ninst, flush=True)
    nc._in_names = in_names
    return nc


def _consts():
    t = np.arange(128)
    ident = np.eye(128, dtype=np.float32)
    Uin = (t[:, None] <= t[None, :]).astype(np.float32)
    Lin = (t[:, None] >= t[None, :]).astype(np.float32)
    Ust = (t[:, None] < t[None, :]).astype(np.float32)
    Lst = (t[:, None] > t[None, :]).astype(np.float32)
    ones = np.ones((128, 128), np.float32)
    cmat = np.concatenate([ident, Uin, Lin, Ust, Lst, ones], axis=1)
    sel = np.zeros((8, 8, 128), np.float32)
    for r in range(8):
        sel[r, r, :] = 1.0
    invf = (500000.0 ** (-np.arange(0, 32, 2, dtype=np.float32) / 32)).astype(np.float32)
    ropec = np.zeros((32, 2), np.float32)
    ropec[:, 0] = np.concatenate([invf, invf])
    ropec[:16, 1] = -1.0
    ropec[16:, 1] = 1.0
    psw = np.zeros((32, 32), np.float32)
    for m in range(32):
        psw[(m + 16) % 32, m] = 1.0
    return cmat, sel.reshape(8, 1024), ropec, psw


def prep_inputs(inputs):
    x = np.asarray(inputs["x"], np.float32)
    c = np.asarray(inputs["c"], np.float32)
    pos = np.asarray(inputs["positions"]).astype(np.int32)
    cmat, sel, ropec, psw = _consts()
    shared = {"cmat": cmat, "sel": sel, "ropec": ropec, "pswap": psw}
    for nm in ("w_ada", "w_in", "conv_w", "w_out", "w_gate", "w_up", "w_down"):
        shared[nm] = np.ascontiguousarray(np.asarray(inputs[nm], np.float32)[0])
    for nm in ("b_ada", "conv_b", "attn_sink", "a_log_fwd", "a_log_bwd", "dt_bias_fwd", "dt_bias_bwd",
               "ssd_d", "ssd_norm_w", "attn_norm_w", "ln1_g", "ln1_b", "ln2_g", "ln2_b"):
        shared[nm] = np.ascontiguousarray(np.asarray(inputs[nm], np.float32)[0])
    NEG = -30000.0
    qi = np.arange(128)[:, None]
    kj = np.arange(128)[None, :]
    in_maps = []
    for core in range(NCORES):
        b, j = core // 4, core % 4
        m = dict(shared)
        m["xm"] = np.ascontiguousarray(np.roll(x[b], -1024 * j, axis=0))
        xh = np.zeros((NSLOT * 4, D), np.float32)
        hmk = np.zeros((NSLOT, 4), np.float32)
        for s in range(NSLOT):
            t0 = ((8 * j + s) % 32) * 128
            for ii, tt in enumerate((t0 - 2, t0 - 1, t0 + 128, t0 + 129)):
                if 0 <= tt < 4096:
                    xh[4 * s + ii] = x[b, tt]
                    hmk[s, ii] = 1.0
        m["xh"] = xh
        m["hmask"] = np.ascontiguousarray(np.broadcast_to(hmk.reshape(1, -1), (128, NSLOT * 4)))
        pr = np.zeros((10, 128), np.int32)
        for ti, s in ((0, 0), (1, 1), (2, 2), (3, 3), (4, 4), (5, 5), (6, 6), (7, 7), (8, 8), (9, 31)):
            ch = (8 * j + s) % 32
            pr[ti] = pos[b, ch * 128:(ch + 1) * 128]
        m["posr"] = pr.reshape(-1)
        am = np.zeros((3, 128, 384), np.float32)
        prev = np.where(kj >= qi, 0.0, NEG)
        nxt = np.where(kj <= qi, 0.0, NEG)
        for ty in range(3):
            am[ty, :, 0:128] = prev
            am[ty, :, 256:384] = nxt
        if j == 0:
            am[0, :, 0:128] = NEG
        if j == 3:
            am[2, :, 256:384] = NEG
        m["amask"] = am
        af = np.zeros(NSLOT, np.float32)
        ab = np.zeros(NSLOT, np.float32)
        for s in range(NSLOT):
            if s >= 8 and s >= 32 - 8 * j:
                af[s] = 1.0
            if s < 8 or (8 <= s <= 31 - 8 * j):
                ab[s] = 1.0
        m["actf"] = np.ascontiguousarray(np.broadcast_to(af[None, :], (128, NSLOT)))
        m["actb"] = np.ascontiguousarray(np.broadcast_to(ab[None, :], (128, NSLOT)))
        m["cvec"] = np.ascontiguousarray(c[b].reshape(32, 128))
        in_maps.append(m)
    return in_maps


def kernel(**inputs):
    in_maps = prep_inputs(inputs)
    nc = build_nc()
    res = run_bass_kernel_spmd(nc, in_maps, core_ids=list(range(NCORES)))
    out = np.zeros((2, 4096, 4096), np.float32)
    for core in range(NCORES):
        b, j = core // 4, core % 4
        out[b, 1024 * j:1024 * (j + 1)] = res.results[core]["y_out"]
    return out
```

```python
import numpy as np
from contextlib import ExitStack
import concourse.bass as bass
import concourse.mybir as mybir
from concourse.bass_utils import run_bass_kernel_spmd

F32 = mybir.dt.float32
BF16 = mybir.dt.bfloat16
I32 = mybir.dt.int32
AF = mybir.ActivationFunctionType
ALU = mybir.AluOpType

NCORES = 8
D = 4096
KC = 32
DFF = 11008
KCF = 86
NSLOT = 32
SW = 132
GROUPS = [(0, 1, 2), (3, 4, 5), (6, 7, 8), (9, 10, 11), (12, 13, 14), (15, 16, 17),
          (18, 19, 20), (21, 22, 23), (24, 25, 26), (27, 28, 29), (30, 31)]
OWNG = [(0, 1, 2), (3, 4, 5), (6, 7)]
CQ, CK, CV, CZ, CX, CB, CC, CDT = 0, 2048, 2560, 3072, 5120, 7168, 7680, 8192
SCALE = 128.0 ** -0.5
PI = float(np.pi)
ALPHA = 2.0 ** 0.25
LN_EPS = 1e-5
RMS_EPS = 1e-6
KVSLOT = {0: 0, 1: 1, 2: 2, 3: 3, 4: 4, 5: 5, 6: 6, 7: 7, 8: 8, 31: 9}


class Tok:
    __slots__ = ("key", "val")

    def __init__(self, key, val=None):
        self.key, self.val = key, val


class Buf:
    def __init__(self, name):
        self.name = name
        self.w = None
        self.r = {}


class K:
    CE = ("pe", "act", "dve", "pool", "sp")

    def __init__(self, nc, es):
        self.nc = nc
        self.eng = {"pe": nc.tensor, "act": nc.scalar, "dve": nc.vector, "pool": nc.gpsimd, "sp": nc.sync}
        self.sem = {e: es.enter_context(nc.semaphore("s_" + e)) for e in self.eng}
        self.cnt = {e: 0 for e in self.eng}
        self.pend = {e: [] for e in self.eng}
        self.last = {e: None for e in self.eng}
        self.lasttok = {e: None for e in self.eng}
        self.waited = {e: {} for e in self.eng}
        self.dsem = {}
        self.dpos = {}
        for q in ("sp", "pool"):
            self.dsem[q] = []
            for i in range(12):
                key = "d_%s%d" % (q, i)
                self.sem[key] = es.enter_context(nc.semaphore(key))
                self.dsem[q].append([key, 0, None])
            self.dpos[q] = 0
        self.outtoks = []
        self.ninst = 0

    def _resolve(self, tok):
        if tok.val is None:
            e = tok.key
            self.last[e].then_inc(self.sem[e], 1)
            self.cnt[e] += 1
            for t in self.pend[e]:
                t.val = self.cnt[e]
            self.pend[e] = []
        return tok.val

    def _wait(self, e, tok):
        if tok is None:
            return
        if tok.key == e and e == "pe":
            return
        v = self._resolve(tok)
        if self.waited[e].get(tok.key, 0) >= v:
            return
        self.eng[e].wait_ge(self.sem[tok.key], v)
        self.waited[e][tok.key] = v

    def _deps(self, e, r, w):
        for b in r:
            self._wait(e, b.w)
        for b in w:
            self._wait(e, b.w)
            for t in list(b.r.values()):
                self._wait(e, t)

    def _mark(self, tok, r, w):
        for b in r:
            b.r[tok.key] = tok
        for b in w:
            b.w = tok
            b.r = {}

    def op(self, e, fn, r=(), w=()):
        self._deps(e, r, w)
        inst = fn(self.eng[e])
        self.ninst += 1
        self.last[e] = inst
        tok = Tok(e)
        self.pend[e].append(tok)
        self.lasttok[e] = tok
        self._mark(tok, r, w)
        return tok

    def dma(self, q, out, in_, r=(), w=(), is_out=False):
        self._deps(q, r, w)
        slot = self.dsem[q][self.dpos[q] % len(self.dsem[q])]
        self.dpos[q] += 1
        if slot[2] is not None:
            self._wait(q, slot[2])
        slot[1] += 16
        self.eng[q].dma_start(out=out, in_=in_).then_inc(self.sem[slot[0]], 16)
        self.ninst += 1
        tok = Tok(slot[0], slot[1])
        slot[2] = tok
        self._mark(tok, r, w)
        if is_out:
            self.outtoks.append(tok)
        return tok

    def barrier(self):
        toks = [self.lasttok[e] for e in self.CE if self.lasttok[e] is not None]
        for q in ("sp", "pool"):
            toks += [s[2] for s in self.dsem[q] if s[2] is not None]
        for e in self.CE:
            for t in toks:
                self._wait(e, t)

    def finish(self):
        self.barrier()
        for t in self.outtoks:
            self._wait("sp", t)


def build_nc(dbg=()):
    nc = bass.Bass("TRN2", target_bir_lowering=False)

    in_names = []
    stop_after = [d for d in dbg if d.startswith("stop:")]
    stop_after = stop_after[0][5:] if stop_after else None
    nomod = "nomod" in dbg

    def din(name, shape, dt=F32, need=True):
        if not need:
            return None
        in_names.append(name)
        return nc.dram_tensor(name, list(shape), dt, kind="ExternalInput").ap()

    xm = din("xm", [NSLOT * 128, D])
    xh = din("xh", [NSLOT * 4, D])
    hmask = din("hmask", [128, NSLOT * 4])
    posr = din("posr", [10 * 128], I32)
    amask = din("amask", [3, 128, 384])
    actf = din("actf", [128, NSLOT])
    actb = din("actb", [128, NSLOT])
    cmat_d = din("cmat", [128, 6 * 128])
    sel_d = din("sel", [8, 8 * 128])
    ropec = din("ropec", [32, 2])
    psw_d = din("pswap", [32, 32])
    cvec = din("cvec", [32, 128], need=not nomod)
    w_ada = din("w_ada", [D, 6 * D], need=not nomod)
    b_ada = din("b_ada", [6 * D], need=not nomod)
    mod_in = din("mod_in", [6 * D], need=nomod)
    full = stop_after is None
    w_in = din("w_in", [D, 8256])
    conv_w = din("conv_w", [5, 3072])
    conv_b = din("conv_b", [3072])
    attn_sink = din("attn_sink", [16])
    a_log_f = din("a_log_fwd", [32])
    a_log_b = din("a_log_bwd", [32])
    dtb_f = din("dt_bias_fwd", [32])
    dtb_b = din("dt_bias_bwd", [32])
    ssd_d = din("ssd_d", [32])
    ssd_nw = din("ssd_norm_w", [2048])
    attn_nw = din("attn_norm_w", [2048])
    w_out = din("w_out", [D, D], need=full)
    ln1_g = din("ln1_g", [D], need=full)
    ln1_b = din("ln1_b", [D], need=full)
    w_gate = din("w_gate", [D, DFF], need=full)
    w_up = din("w_up", [D, DFF], need=full)
    w_down = din("w_down", [DFF, D], need=full)
    ln2_g = din("ln2_g", [D], need=full)
    ln2_b = din("ln2_b", [D], need=full)
    y_out = nc.dram_tensor("y_out", [1024, D], F32, kind="ExternalOutput").ap()
    mod_d = nc.dram_tensor("mod_d", [6 * D], F32).ap()
    sb_d = nc.dram_tensor("sb_d", [NSLOT, 128, 2048], F32).ap()
    eb_d = nc.dram_tensor("eb_d", [NSLOT, 128, 32], F32).ap()
    hb_d = nc.dram_tensor("hb_d", [8, 128, 2048], BF16).ap()
    mix_d = nc.dram_tensor("mix_d", [1024, D], BF16).ap()
    mixr_d = nc.dram_tensor("mixr_d", [1024, D], F32).ap()
    x1_d = nc.dram_tensor("x1_d", [1024, D], F32).ap()
    ffn_d = nc.dram_tensor("ffn_d", [1024, D], F32).ap()
    dbg_out = {}
    if "hf" in dbg:
        dbg_out["hf"] = nc.dram_tensor("dbg_hf", [128, 2048], F32, kind="ExternalOutput").ap()
    if "kv" in dbg:
        dbg_out["kT"] = nc.dram_tensor("dbg_kT", [128, 4 * 1280], F32, kind="ExternalOutput").ap()
        dbg_out["v"] = nc.dram_tensor("dbg_v", [128, 10 * 512], F32, kind="ExternalOutput").ap()
    for nm, shp, dt in (("mod", [6 * D], F32), ("hb", [8, 128, 2048], BF16), ("mix", [1024, D], BF16),
                        ("mixr", [1024, D], F32), ("x1", [1024, D], F32), ("ffn", [1024, D], F32)):
        if nm in dbg:
            dbg_out[nm] = nc.dram_tensor("dbg_" + nm, shp, dt, kind="ExternalOutput").ap()
    es = ExitStack()
    with es:
        k = K(nc, es)

        used_names = {}
        want_dump = "p3dump" in dbg

        def dump(name, ap, bufs, dt=F32):
            if not want_dump:
                return
            shp = list(ap.shape)
            o = nc.dram_tensor("dmp_" + name, shp, dt, kind="ExternalOutput").ap()
            k.dma("sp", o, ap, r=bufs, is_out=True)

        def sb(name, shape, dt=F32, stack=None):
            n = used_names.get(name, 0)
            used_names[name] = n + 1
            if n:
                name = "%s_r%d" % (name, n)
            return (stack or es).enter_context(nc.sbuf_tensor(name, list(shape), dt))

        psA = es.enter_context(nc.psum_tensor("psA", [128, 2048], F32))
        psB = es.enter_context(nc.psum_tensor("psB", [128, 2048], F32))
        bank = [psA[:, i * 512:(i + 1) * 512] for i in range(4)] + [psB[:, i * 512:(i + 1) * 512] for i in range(4)]
        bkb = [Buf("bank%d" % i) for i in range(8)]

        def V(ap, c):
            return ap.rearrange("p (s c) -> p s c", c=c)

        cmat = sb("cmat_s", [128, 6, 128]); b_cm = Buf("cmat")
        k.dma("sp", cmat[:].rearrange("p a b -> p (a b)"), cmat_d[:, :], w=[b_cm])
        ident, Uin, Lin, Ust, Lst, ones = [cmat[:, i, :] for i in range(6)]
        cb16 = sb("cb16", [128, 2, 128], BF16); b_c16 = Buf("cb16")
        identb, onesb = cb16[:, 0, :], cb16[:, 1, :]
        k.op("dve", lambda e: e.tensor_copy(out=identb, in_=ident), r=[b_cm], w=[b_c16])
        k.op("dve", lambda e: e.tensor_copy(out=onesb, in_=ones), r=[b_cm], w=[b_c16])
        hm, b_hm = sb("hm", [128, NSLOT, 4]), Buf("hm")
        k.dma("sp", hm[:].rearrange("p a b -> p (a b)"), hmask[:, :], w=[b_hm])
        af_t, b_af = sb("af_t", [128, NSLOT]), Buf("af")
        k.dma("sp", af_t[:], actf[:, :], w=[b_af])
        ab_t, b_ab = sb("ab_t", [128, NSLOT]), Buf("ab")
        k.dma("sp", ab_t[:], actb[:, :], w=[b_ab])
        sink_t, b_sink = sb("sink_t", [128, 16]), Buf("sink")
        k.dma("sp", sink_t[:], attn_sink.partition_broadcast(128), w=[b_sink])
        nsink = sb("nsink", [128, 16]); b_nsink = Buf("nsink")
        k.op("dve", lambda e: e.tensor_scalar(out=nsink[:], in0=sink_t[:], scalar1=-1.0, scalar2=None, op0=ALU.mult), r=[b_sink], w=[b_nsink])
        hp = sb("hp", [128, 160]); b_hp = Buf("hp")
        k.dma("sp", hp[:, 0:32], dtb_f.partition_broadcast(128), w=[b_hp])
        k.dma("sp", hp[:, 32:64], dtb_b.partition_broadcast(128), w=[b_hp])
        k.dma("sp", hp[:, 64:96], a_log_f.partition_broadcast(128), w=[b_hp])
        k.dma("sp", hp[:, 96:128], a_log_b.partition_broadcast(128), w=[b_hp])
        k.dma("sp", hp[:, 128:160], ssd_d.partition_broadcast(128), w=[b_hp])
        k.op("act", lambda e: e.activation(out=hp[:, 64:128], in_=hp[:, 64:128], func=AF.Exp), r=[b_hp], w=[b_hp])
        k.op("dve", lambda e: e.tensor_scalar(out=hp[:, 64:128], in0=hp[:, 64:128], scalar1=-1.0, scalar2=None, op0=ALU.mult), r=[b_hp], w=[b_hp])

        def load_cols(name, src2d, R, nb):
            t = sb(name, [128, nb, R]); bt = Buf(name)
            with nc.sbuf_tensor(name + "_s", [R, nb * 128], F32) as stg:
                bs = Buf(name + "_s")
                k.dma("sp", stg[:], src2d, w=[bs])
                per = max(1, 512 // R)
                for b0 in range(0, nb, per):
                    n = min(per, nb - b0)
                    for i in range(n):
                        k.op("pe", lambda e, i=i: e.transpose(out=bank[2][:, i * R:(i + 1) * R], in_=stg[:, (b0 + i) * 128:(b0 + i + 1) * 128], identity=ident[0:R, 0:R]), r=[bs, b_cm], w=[bkb[2]])
                    k.op("dve", lambda e, n=n: e.tensor_copy(out=t[:, b0:b0 + n, :].rearrange("p a b -> p (a b)"), in_=bank[2][:, 0:n * R]), r=[bkb[2]], w=[bt])
                k.barrier()
            return t, bt

        cw, b_cw = load_cols("cw", conv_w[:, :], 5, 24)
        cbc, b_cbc = load_cols("cbc", conv_b.rearrange("(a b) -> a b", b=128), 24, 1)
        anw, b_anw = load_cols("anw", attn_nw.rearrange("(a b) -> a b", b=128), 16, 1)
        snw, b_snw = load_cols("snw", ssd_nw.rearrange("(a b) -> a b", b=128), 16, 1)

        condT = sb("condT", [128, 32], BF16); b_cond = Buf("condT")
        if nomod:
            k.dma("sp", mod_d, mod_in)
            k.barrier()
        else:
            with ExitStack() as ps_:
                cst = sb("cst", [32, 128], stack=ps_); b_cst = Buf("cst")
                k.dma("sp", cst[:], cvec[:, :], w=[b_cst])
                k.op("pe", lambda e: e.transpose(out=bank[2][:, 0:32], in_=cst[:], identity=ident[0:32, 0:32]), r=[b_cst, b_cm], w=[bkb[2]])
                k.op("act", lambda e: e.activation(out=condT[:], in_=bank[2][:, 0:32], func=AF.Silu), r=[bkb[2]], w=[b_cond])
                k.barrier()
        xt = sb("xt", [128, D]); b_xt = Buf("xt")
        NW = 3
        wsl = [sb("wsl%d" % i, [128, 32, 128], BF16) for i in range(NW)]; b_wsl = [Buf("wsl%d" % i) for i in range(NW)]
        wc = [0]
        gc = [0]

        def transp_mod(src_tile, b_src, nrows, dst_fn, b_dst, mc, b_mc):
            for k0 in range(0, KC, 4):
                pb = 6 + (k0 // 4) % 2
                for j in range(4):
                    kc = k0 + j
                    k.op("pe", lambda e, kc=kc, j=j: e.transpose(out=bank[pb][:, j * 128:j * 128 + nrows], in_=src_tile[0:nrows, kc * 128:(kc + 1) * 128], identity=ident[0:nrows, 0:nrows]), r=[b_src, b_cm], w=[bkb[pb]])
                for j in range(4):
                    kc = k0 + j
                    for (dst, src) in dst_fn(kc, bank[pb][:, j * 128:j * 128 + nrows]):
                        k.op("act", lambda e, kc=kc, dst=dst, src=src: e.activation(out=dst, in_=src, func=AF.Identity, scale=mc[:, 32 + kc:33 + kc], bias=mc[:, kc:kc + 1]), r=[bkb[pb], b_mc], w=[b_dst])

        def make_hT(src_rows, nrows, dst_fn, b_dst):
            k.dma("sp", xt[0:nrows, :], src_rows, w=[b_xt])
            transp_mod(xt, b_xt, nrows, dst_fn, b_dst, mA, b_mcA)

        pend_epi = [None]
        bg_hook = [lambda: None]

        def flush_epi():
            if pend_epi[0] is not None:
                f = pend_epi[0]; pend_epi[0] = None
                f()

        def gemm(act_fn, N, nkc, wsrc, col0, nblk, epi, r_act, units=1, two_phase=False):
            flush_epi()
            usz = [nkc // units + (1 if u < nkc % units else 0) for u in range(units)]
            uoff = [sum(usz[:u]) for u in range(units)]
            for blk in range(nblk):
                c0 = col0 + blk * 128
                pb = gc[0] % 2; gc[0] += 1
                for u in range(units):
                    wi = wc[0] % NW; wc[0] += 1
                    per = usz[u]
                    k.dma("pool", wsl[wi][:, 0:per, :], wsrc[uoff[u] * 128:(uoff[u] + per) * 128, c0:c0 + 128].rearrange("(kc p) c -> p kc c", p=128), w=[b_wsl[wi]])
                    for kk in range(per):
                        kc = uoff[u] + kk
                        k.op("pe", lambda e, kc=kc, kk=kk, wi=wi: e.matmul(bank[pb][:, 0:N], lhsT=wsl[wi][:, kk, :], rhs=act_fn(kc), start=(kc == 0), stop=(kc == nkc - 1)), r=[b_wsl[wi]] + r_act, w=[bkb[pb]])
                flush_epi()
                if two_phase:
                    pend_epi[0] = epi(blk, bank[pb], bkb[pb])
                else:
                    pend_epi[0] = (lambda blk=blk, pb=pb: epi(blk, bank[pb], bkb[pb]))
                bg_hook[0]()

        abar = sb("abar", [1, 4, 128]); b_abar = [Buf("abar%d" % i) for i in range(4)]
        amrow = sb("amrow", [1, 4, 128]); b_amrow = [Buf("amrow%d" % i) for i in range(4)]
        adc = [0]

        def ada_unit(u):
            c0 = u * 128
            i = adc[0] % 4; adc[0] += 1
            pb = gc[0] % 2; gc[0] += 1
            wi = wc[0] % NW; wc[0] += 1
            flush_epi()
            k.dma("pool", wsl[wi][:, 0:KC, :], w_ada[:, c0:c0 + 128].rearrange("(kc p) c -> p kc c", p=128), w=[b_wsl[wi]])
            k.dma("sp", abar[0:1, i, :], b_ada[c0:c0 + 128].rearrange("(a b) -> a b", a=1), w=[b_abar[i]])
            for kc in range(KC):
                k.op("pe", lambda e, kc=kc: e.matmul(bank[pb][0:1, 0:128], lhsT=condT[:, kc:kc + 1], rhs=wsl[wi][:, kc, :], start=(kc == 0), stop=False), r=[b_cond, b_wsl[wi]], w=[bkb[pb]])
            k.op("pe", lambda e: e.matmul(bank[pb][0:1, 0:128], lhsT=ones[0:1, 0:1], rhs=abar[0:1, i, :], start=False, stop=True), r=[b_cm, b_abar[i]], w=[bkb[pb]])

            def epi():
                k.op("act", lambda e: e.activation(out=amrow[0:1, i, :], in_=bank[pb][0:1, 0:128], func=AF.Identity), r=[bkb[pb]], w=[b_amrow[i]])
                k.dma("sp", mod_d[c0:c0 + 128].rearrange("(a b) -> a b", a=1), amrow[0:1, i, :], r=[b_amrow[i]])
            pend_epi[0] = epi
            bg_hook[0]()

        if not nomod:
            for u in range(64):
                ada_unit(u)
            flush_epi()
            k.barrier()
        mcA, b_mcA = load_cols("mcA", mod_d[0:2 * D].rearrange("(a b) -> a b", b=128), 64, 1)
        mA = mcA[:, 0, :]
        k.op("dve", lambda e: e.tensor_scalar(out=mA[:, 32:64], in0=mA[:, 32:64], scalar1=1.0, scalar2=None, op0=ALU.add), r=[b_mcA], w=[b_mcA])
        mcB = sb("mcB", [128, 1, 64]); b_mcB = Buf("mcB")
        mB = mcB[:, 0, :]

        Hf = sb("Hf", [128, 2048]); b_Hf = Buf("Hf")
        k.op("dve", lambda e: e.memset(Hf[:], 0.0), w=[b_Hf])
        wdt = sb("wdt", [128, KC, 64], BF16); b_wdt = Buf("wdt")
        k.dma("pool", wdt[:], w_in[:, CDT:CDT + 64].rearrange("(kc p) c -> p kc c", p=128), w=[b_wdt])
        ast_ = ExitStack()
        cosT = sb("cosT", [32, 1280], stack=ast_); sinT = sb("sinT", [32, 1280], stack=ast_); b_rope = Buf("rope")
        psw = sb("psw", [32, 32], stack=ast_); b_psw = Buf("psw")
        k.dma("sp", psw[:], psw_d[:, :], w=[b_psw])
        with ExitStack() as ps_:
            pi_i = sb("pi_i", [32, 1280], I32, stack=ps_); ang = sb("ang", [32, 1280], stack=ps_)
            rt = sb("rt", [32, 1280], stack=ps_); rc = sb("rc", [32, 2], stack=ps_)
            b_pi, b_ang, b_rt, b_rc = Buf("pi"), Buf("ang"), Buf("rt"), Buf("rc")
            k.dma("sp", pi_i[:], posr.partition_broadcast(32), w=[b_pi])
            k.dma("sp", rc[:], ropec[:, :], w=[b_rc])
            k.op("dve", lambda e: e.tensor_copy(out=ang[:], in_=pi_i[:]), r=[b_pi], w=[b_ang])
            k.op("dve", lambda e: e.tensor_scalar(out=ang[:], in0=ang[:], scalar1=rc[:, 0:1], scalar2=None, op0=ALU.mult), r=[b_ang, b_rc], w=[b_ang])
            for (dstT, off) in ((sinT, 0.0), (cosT, PI / 2)):
                k.op("dve", lambda e: e.tensor_scalar(out=rt[:], in0=ang[:], scalar1=off, scalar2=1.0 / (2 * PI), op0=ALU.add, op1=ALU.mult), r=[b_ang], w=[b_rt])
                k.op("dve", lambda e: e.tensor_copy(out=pi_i[:], in_=rt[:]), r=[b_rt], w=[b_pi])
                k.op("dve", lambda e: e.tensor_copy(out=rt[:], in_=pi_i[:]), r=[b_pi], w=[b_rt])
                k.op("dve", lambda e: e.scalar_tensor_tensor(out=rt[:], in0=rt[:], scalar=-2 * PI, in1=ang[:], op0=ALU.mult, op1=ALU.add), r=[b_rt, b_ang], w=[b_rt])
                k.op("dve", lambda e: e.tensor_scalar(out=rt[:], in0=rt[:], scalar1=off, scalar2=PI, op0=ALU.add, op1=ALU.min), r=[b_rt], w=[b_rt])
                k.op("dve", lambda e: e.tensor_scalar(out=rt[:], in0=rt[:], scalar1=-PI, scalar2=None, op0=ALU.max), r=[b_rt], w=[b_rt])
                k.op("act", lambda e, dstT=dstT: e.activation(out=dstT[:], in_=rt[:], func=AF.Sin), r=[b_rt], w=[b_rope])
            k.op("dve", lambda e: e.tensor_scalar(out=sinT[:], in0=sinT[:], scalar1=rc[:, 1:2], scalar2=None, op0=ALU.mult), r=[b_rope, b_rc], w=[b_rope])
            k.barrier()

        qr32 = sb("qr32", [32, 512], stack=ast_); b_qr = Buf("qr32")
        rtmp = sb("rtmp", [32, 2, 512], stack=ast_); b_rtmp = Buf("rtmp")

        def rope_rows(ps, pbuf, src_cols, tab_col0, n, dst, b_dst):
            k.op("act", lambda e: e.activation(out=qr32[:, 0:n], in_=ps[0:32, src_cols:src_cols + n], func=AF.Identity), r=[pbuf], w=[b_qr])
            k.op("pe", lambda e: e.matmul(bank[2][0:32, 0:n], lhsT=psw[:], rhs=qr32[:, 0:n], start=True, stop=True), r=[b_psw, b_qr], w=[bkb[2]])
            k.op("dve", lambda e: e.tensor_tensor(out=rtmp[:, 0, 0:n], in0=qr32[:, 0:n], in1=cosT[:, tab_col0:tab_col0 + n], op=ALU.mult), r=[b_qr, b_rope], w=[b_rtmp])
            k.op("dve", lambda e: e.tensor_tensor(out=rtmp[:, 1, 0:n], in0=bank[2][0:32, 0:n], in1=sinT[:, tab_col0:tab_col0 + n], op=ALU.mult), r=[bkb[2], b_rope], w=[b_rtmp])
            k.op("dve", lambda e: e.tensor_tensor(out=dst, in0=rtmp[:, 0, 0:n], in1=rtmp[:, 1, 0:n], op=ALU.add), r=[b_rtmp], w=[b_dst])

        kT_all = sb("kT_all", [128, 4, 1280], BF16, stack=ast_); b_kT = Buf("kT")
        v_tok = sb("v_tok", [128, 10, 512], BF16, stack=ast_); b_v = Buf("v")

        hTgL = xsL = BtL = ue = dg = dtraw = dts = adt = ex = wv = etfa = xw = sst = vT = None
        b_hTg = [Buf("hTg0"), Buf("hTg1")]; b_xs = [Buf("xs0"), Buf("xs1")]; b_Bt = [Buf("Bt0"), Buf("Bt1")]
        b_ue = [Buf("ue0"), Buf("ue1")]; b_dg = [Buf("dg0"), Buf("dg1")]
        b_dtraw = [Buf("dtraw0"), Buf("dtraw1")]
        b_dts = [[Buf("dts") for _ in range(3)] for _ in range(2)]; b_adt = [[Buf("adt") for _ in range(3)] for _ in range(2)]
        b_ex = [[Buf("ex") for _ in range(3)] for _ in range(2)]; b_wv = [[Buf("wv") for _ in range(3)] for _ in range(2)]
        b_etfa = Buf("etfa"); b_xw = [Buf("xw0"), Buf("xw1")]; b_sst = Buf("sst"); b_vT = Buf("vT")
        uec = [0]

        def alloc_group(gst, tag, p1):
            nonlocal hTgL, xsL, BtL, ue, dg, dtraw, dts, adt, ex, wv, etfa, xw, sst, vT
            npar = 2 if p1 else 1
            hTgL = [sb("hTg" + tag, [128, KC, 3 * SW], BF16, stack=gst) for _ in range(npar)]
            xsL = [sb("xs_tok" + tag, [128, 3, 2048], BF16, stack=gst) for _ in range(npar)]
            BtL = [sb("B_tok" + tag, [128, 3, 512], BF16, stack=gst) for _ in range(npar)]
            ue = [sb("ue%d" % i + tag, [128, 3 * SW], BF16, stack=gst) for i in range(2)]
            dg = [sb("dg%d" % i + tag, [128, 6, 128], BF16, stack=gst) for i in range(2)]
            dtraw = sb("dtraw" + tag, [128, npar, 3, 64], stack=gst)
            dts = sb("dts" + tag, [128, npar, 3, 64], stack=gst)
            adt = sb("adt" + tag, [128, npar, 3, 64], stack=gst)
            ex = sb("ex" + tag, [128, npar, 3, 192], stack=gst)
            wv = sb("wv" + tag, [128, npar, 3, 64], stack=gst)
            etfa = sb("etfa" + tag, [128, 32], stack=gst)
            xw = [sb("xw%d" % i + tag, [128, 2048], BF16, stack=gst) for i in range(2 if p1 else 1)]
            if p1:
                sst = sb("sst" + tag, [128, 1024], stack=gst)
                vT = sb("vT" + tag, [128, 3 * SW], BF16, stack=gst)

        from collections import deque
        bgq = deque()

        def bg_step(n=3):
            for _ in range(n):
                if bgq:
                    bgq.popleft()()

        def bg_drain():
            while bgq:
                bgq.popleft()()

        def hT_tile_tasks(src_rows, nrows, dst_fn, b_dst):
            tasks = [lambda: k.dma("sp", xt[0:nrows, :], src_rows, w=[b_xt])]
            for k0 in range(0, KC, 4):
                def t(k0=k0):
                    pb = 6 + (k0 // 4) % 2
                    for j in range(4):
                        kc = k0 + j
                        k.op("pe", lambda e, kc=kc, j=j: e.transpose(out=bank[pb][:, j * 128:j * 128 + nrows], in_=xt[0:nrows, kc * 128:(kc + 1) * 128], identity=ident[0:nrows, 0:nrows]), r=[b_xt, b_cm], w=[bkb[pb]])
                    for j in range(4):
                        kc = k0 + j
                        for (dst, src) in dst_fn(kc, bank[pb][:, j * 128:j * 128 + nrows]):
                            k.op("act", lambda e, kc=kc, dst=dst, src=src: e.activation(out=dst, in_=src, func=AF.Identity, scale=mA[:, 32 + kc:33 + kc], bias=mA[:, kc:kc + 1]), r=[bkb[pb], b_mcA], w=[b_dst])
                tasks.append(t)
            return tasks

        def group_hT_tasks(slots, par):
            ns = len(slots); s0 = slots[0]
            h = hTgL[par]
            tasks = []
            for si, s in enumerate(slots):
                tasks += hT_tile_tasks(xm[s * 128:(s + 1) * 128, :], 128, lambda kc, src, si=si: [(h[:, kc, si * SW + 2:si * SW + 130], src)], b_hTg[par])

            def hdst(kc, src):
                hv = V(h[:, kc, 0:ns * SW], SW)
                sv = V(src, 4)
                return [(hv[:, :, 0:2], sv[:, :, 0:2]), (hv[:, :, 130:132], sv[:, :, 2:4])]
            tasks += hT_tile_tasks(xh[4 * s0:4 * s0 + 4 * ns, :], 4 * ns, hdst, b_hTg[par])
            return tasks

        def conv_epi(blk_ch, ps, pbuf, slots, dst_tok, b_dsttok, dst_col0, feat=None):
            ns = len(slots); s0 = slots[0]
            i = uec[0] % 2; uec[0] += 1
            u = ue[i]
            k.op("act", lambda e: e.activation(out=u[:, 0:ns * SW], in_=ps[:, 0:ns * SW], func=AF.Identity), r=[pbuf], w=[b_ue[i]])
            u3 = V(u[:, 0:ns * SW], SW)
            k.op("dve", lambda e: e.tensor_tensor(out=u3[:, :, 0:2], in0=u3[:, :, 0:2], in1=hm[:, s0:s0 + ns, 0:2], op=ALU.mult), r=[b_hm, b_ue[i]], w=[b_ue[i]])
            k.op("dve", lambda e: e.tensor_tensor(out=u3[:, :, 130:132], in0=u3[:, :, 130:132], in1=hm[:, s0:s0 + ns, 2:4], op=ALU.mult), r=[b_hm, b_ue[i]], w=[b_ue[i]])
            if dst_tok is not None:
                for j in range(5):
                    k.op("dve", lambda e, j=j: e.tensor_scalar(out=dg[i][:, j, :], in0=identb, scalar1=cw[:, blk_ch, j:j + 1], scalar2=None, op0=ALU.mult), r=[b_c16, b_cw], w=[b_dg[i]])
                k.op("dve", lambda e: e.tensor_scalar(out=dg[i][:, 5, :], in0=identb, scalar1=cbc[:, 0, blk_ch:blk_ch + 1], scalar2=None, op0=ALU.mult), r=[b_c16, b_cbc], w=[b_dg[i]])
                pass

            def late():
                if dst_tok is not None:
                    pc = 3
                    for si in range(ns):
                        o = bank[pc][:, si * 128:(si + 1) * 128]
                        for j in range(5):
                            k.op("pe", lambda e, j=j, si=si, o=o: e.matmul(o, lhsT=u[:, si * SW + j:si * SW + j + 128], rhs=dg[i][:, j, :], start=(j == 0), stop=False), r=[b_ue[i], b_dg[i]], w=[bkb[pc]])
                        k.op("pe", lambda e, o=o: e.matmul(o, lhsT=onesb, rhs=dg[i][:, 5, :], start=False, stop=True), r=[b_c16, b_dg[i]], w=[bkb[pc]])
                    k.op("act", lambda e: e.activation(out=dst_tok[:, 0:ns, dst_col0:dst_col0 + 128], in_=V(bank[pc][:, 0:ns * 128], 128), func=AF.Silu), r=[bkb[pc]], w=[b_dsttok])
                if feat is not None:
                    dstT, b_dstT, acc, b_acc = feat
                    k.op("dve", lambda e: e.tensor_scalar(out=acc[:, 0:ns, :], in0=u3[:, :, 0:128], scalar1=cw[:, blk_ch, 0:1], scalar2=None, op0=ALU.mult), r=[b_ue[i], b_cw], w=[b_acc])
                    for j in range(1, 5):
                        k.op("dve", lambda e, j=j: e.scalar_tensor_tensor(out=acc[:, 0:ns, :], in0=u3[:, :, j:j + 128], scalar=cw[:, blk_ch, j:j + 1], in1=acc[:, 0:ns, :], op0=ALU.mult, op1=ALU.add), r=[b_ue[i], b_cw, b_acc], w=[b_acc])
                    k.op("act", lambda e: e.activation(out=dstT[:, 0:ns, :], in_=acc[:, 0:ns, :], func=AF.Silu, bias=cbc[:, 0, blk_ch:blk_ch + 1]), r=[b_acc, b_cbc], w=[b_dstT])

            return late

        def hT_fn(ns, par=0):
            return lambda kc: hTgL[par][:, kc, 0:ns * SW]

        def dt_matmuls(slots, par):
            ns = len(slots)
            for si in range(ns):
                for kc in range(KC):
                    k.op("pe", lambda e, kc=kc, si=si: e.matmul(bank[2][:, si * 64:(si + 1) * 64], lhsT=hTgL[par][:, kc, si * SW + 2:si * SW + 130], rhs=wdt[:, kc, :], start=(kc == 0), stop=(kc == KC - 1)), r=[b_hTg[par], b_wdt], w=[bkb[2]])
            k.op("dve", lambda e: e.tensor_copy(out=dtraw[:, par, 0:ns, :], in_=V(bank[2][:, 0:ns * 64], 64)), r=[bkb[2]], w=[b_dtraw[par]])

        def chain_dt_tasks(slots, par):
            ns = len(slots)
            T = []
            rng = range(ns)
            T.append(lambda: [k.op("dve", lambda e, si=si: e.tensor_tensor(out=dts[:, par, si, :], in0=dtraw[:, par, si, :], in1=hp[:, 0:64], op=ALU.add), r=[b_dtraw[par], b_hp], w=[b_dts[par][si]]) for si in rng])
            T.append(lambda: [k.op("act", lambda e, si=si: e.activation(out=dts[:, par, si, :], in_=dts[:, par, si, :], func=AF.Exp), r=[b_dts[par][si]], w=[b_dts[par][si]]) for si in rng])
            T.append(lambda: [k.op("act", lambda e, si=si: e.activation(out=dts[:, par, si, :], in_=dts[:, par, si, :], func=AF.Ln, bias=1.0), r=[b_dts[par][si]], w=[b_dts[par][si]]) for si in rng])
            T.append(lambda: [k.op("dve", lambda e, si=si: e.tensor_tensor(out=adt[:, par, si, :], in0=dts[:, par, si, :], in1=hp[:, 64:128], op=ALU.mult), r=[b_dts[par][si], b_hp], w=[b_adt[par][si]]) for si in rng])

            def emats():
                for si in rng:
                    pb = 2 + si % 2
                    for (c0, n, M, a0) in ((0, 32, Uin, 0), (32, 32, Lin, 32), (64, 32, Lst, 0), (96, 32, Ust, 32), (128, 64, ones, 0)):
                        k.op("pe", lambda e, c0=c0, n=n, M=M, a0=a0, si=si, pb=pb: e.matmul(bank[pb][:, c0:c0 + n], lhsT=M, rhs=adt[:, par, si, a0:a0 + n], start=True, stop=True), r=[b_cm, b_adt[par][si]], w=[bkb[pb]])
                    k.op("act", lambda e, si=si, pb=pb: e.activation(out=ex[:, par, si, :], in_=bank[pb][:, 0:192], func=AF.Exp), r=[bkb[pb]], w=[b_ex[par][si]])
            T.append(emats)
            T.append(lambda: [k.op("dve", lambda e, si=si: e.tensor_tensor(out=wv[:, par, si, :], in0=dts[:, par, si, :], in1=ex[:, par, si, 64:128], op=ALU.mult), r=[b_dts[par][si], b_ex[par][si]], w=[b_wv[par][si]]) for si in rng])
            return T

        def bc_hp(ap32, nh=32):
            return ap32.unsqueeze(2).broadcast_to([128, nh, 64])

        def slot_states(si, s, do_f, do_b, act_f_col=None, par=0):
            xs3 = V(xsL[par][:, si, :], 64)
            wv_ = wv[:, par, si, :]; ex_ = ex[:, par, si, :]
            bwv = b_wv[par][si]; bex = b_ex[par][si]; B_tok = BtL[par]; bBt = b_Bt[par]; bxs = b_xs[par]
            if do_f:
                k.op("dve", lambda e: e.tensor_tensor(out=V(xw[0][:], 64), in0=xs3, in1=bc_hp(wv_[:, 0:32]), op=ALU.mult), r=[bxs, bwv], w=[b_xw[0]])
                if act_f_col is not None:
                    k.op("dve", lambda e: e.tensor_scalar(out=etfa[:, 0:32], in0=ex_[:, 128:160], scalar1=act_f_col, scalar2=None, op0=ALU.mult), r=[bex, b_af], w=[b_etfa])
                    ecol = etfa[:, 0:32]
                else:
                    ecol = ex_[:, 128:160]
                for g in range(4):
                    pb = 4 + g % 2
                    k.op("pe", lambda e, g=g, pb=pb: e.matmul(bank[pb][:, :], lhsT=B_tok[:, si, g * 128:(g + 1) * 128], rhs=xw[0][:, g * 512:(g + 1) * 512], start=True, stop=True), r=[bBt, b_xw[0]], w=[bkb[pb]])
                    hg = V(Hf[:, g * 512:(g + 1) * 512], 64)
                    k.op("dve", lambda e, g=g, hg=hg: e.tensor_tensor(out=hg, in0=hg, in1=bc_hp(ecol[:, g * 8:(g + 1) * 8], 8), op=ALU.mult), r=[bex, b_etfa], w=[b_Hf])
                    if act_f_col is not None:
                        k.op("dve", lambda e, g=g, pb=pb: e.scalar_tensor_tensor(out=Hf[:, g * 512:(g + 1) * 512], in0=bank[pb][:, :], scalar=act_f_col, in1=Hf[:, g * 512:(g + 1) * 512], op0=ALU.mult, op1=ALU.add), r=[bkb[pb], b_af], w=[b_Hf])
                    else:
                        k.op("dve", lambda e, g=g, pb=pb: e.tensor_tensor(out=Hf[:, g * 512:(g + 1) * 512], in0=Hf[:, g * 512:(g + 1) * 512], in1=bank[pb][:, :], op=ALU.add), r=[bkb[pb]], w=[b_Hf])
            if do_b:
                k.op("dve", lambda e: e.tensor_tensor(out=V(xw[1][:], 64), in0=xs3, in1=bc_hp(wv_[:, 32:64]), op=ALU.mult), r=[bxs, bwv], w=[b_xw[1]])
                for g in range(4):
                    pb = 4 + g % 2
                    k.op("pe", lambda e, g=g, pb=pb: e.matmul(bank[pb][:, :], lhsT=B_tok[:, si, g * 128:(g + 1) * 128], rhs=xw[1][:, g * 512:(g + 1) * 512], start=True, stop=True), r=[bBt, b_xw[1]], w=[bkb[pb]])
                    k.op("act", lambda e, g=g, pb=pb: e.activation(out=sst[:, (g % 2) * 512:(g % 2 + 1) * 512], in_=bank[pb][:, :], func=AF.Identity), r=[bkb[pb]], w=[b_sst])
                    if g % 2 == 1:
                        k.dma("sp", sb_d[s][:, (g - 1) * 512:(g + 1) * 512], sst[:], r=[b_sst])
                k.dma("sp", eb_d[s], ex_[:, 160:192], r=[bex])

        gst1 = ExitStack()
        alloc_group(gst1, "a", True)
        bg_hook[0] = lambda: bg_step(3)

        def chain_all_tasks(G, par):
            T = chain_dt_tasks(G, par)
            for si, s in enumerate(G):
                T.append(lambda si=si, s=s: slot_states(si, s, do_f=(s >= 8), do_b=False, act_f_col=af_t[:, s:s + 1], par=par))
                T.append(lambda si=si, s=s: slot_states(si, s, do_f=False, do_b=True, par=par))
            return T

        for t_ in group_hT_tasks(GROUPS[0], 0):
            t_()
        for gi, G in enumerate(GROUPS):
            ns = len(G)
            par = gi % 2
            bg_drain()
            ta = group_hT_tasks(GROUPS[gi + 1], 1 - par) if gi + 1 < len(GROUPS) else []
            tb = chain_all_tasks(GROUPS[gi - 1], 1 - par) if gi >= 1 else []
            while ta or tb:
                if ta:
                    bgq.append(ta.pop(0))
                    if ta:
                        bgq.append(ta.pop(0))
                if tb:
                    bgq.append(tb.pop(0))
            xs_, bxs_, Bt_, bBt_ = xsL[par], b_xs[par], BtL[par], b_Bt[par]
            gemm(hT_fn(ns, par), ns * SW, KC, w_in, CX, 16, lambda blk, ps, pbuf, G=G, xs_=xs_, bxs_=bxs_: conv_epi(blk, ps, pbuf, G, xs_, bxs_, blk * 128), [b_hTg[par]], two_phase=True)
            gemm(hT_fn(ns, par), ns * SW, KC, w_in, CB, 4, lambda blk, ps, pbuf, G=G, Bt_=Bt_, bBt_=bBt_: conv_epi(16 + blk, ps, pbuf, G, Bt_, bBt_, blk * 128), [b_hTg[par]], two_phase=True)
            kvs = [(si, s) for si, s in enumerate(G) if s in KVSLOT]
            if kvs:
                def epi_k(blk, ps, pbuf, kvs=kvs):
                    for si, s in kvs:
                        ti = KVSLOT[s]
                        k.op("act", lambda e, si=si, ti=ti: e.activation(out=kT_all[:, blk, ti * 128:(ti + 1) * 128], in_=ps[:, si * SW + 2:si * SW + 130], func=AF.Identity), r=[pbuf], w=[b_kT])
                        rope_rows(ps, pbuf, si * SW + 2, ti * 128, 128, kT_all[0:32, blk, ti * 128:(ti + 1) * 128], b_kT)
                gemm(hT_fn(ns, par), ns * SW, KC, w_in, CK, 4, epi_k, [b_hTg[par]])

                def epi_v(blk, ps, pbuf, kvs=kvs, ns=ns):
                    k.op("act", lambda e: e.activation(out=vT[:, 0:ns * SW], in_=ps[:, 0:ns * SW], func=AF.Identity), r=[pbuf], w=[b_vT])
                    pvb = bank[3].bitcast(BF16)
                    for si, s in kvs:
                        k.op("pe", lambda e, si=si: e.transpose(out=pvb[:, si * 128:(si + 1) * 128], in_=vT[:, si * SW + 2:si * SW + 130], identity=identb), r=[b_vT, b_c16], w=[bkb[3]])
                    for si, s in kvs:
                        ti = KVSLOT[s]
                        k.op("dve", lambda e, si=si, ti=ti: e.tensor_copy(out=v_tok[:, ti, blk * 128:(blk + 1) * 128], in_=pvb[:, si * 128:(si + 1) * 128]), r=[bkb[3]], w=[b_v])
                gemm(hT_fn(ns, par), ns * SW, KC, w_in, CV, 4, epi_v, [b_hTg[par]])
            flush_epi()
            dt_matmuls(G, par)
        bg_drain()
        for t_ in chain_all_tasks(GROUPS[-1], (len(GROUPS) - 1) % 2):
            t_()
        bg_hook[0] = lambda: None
        if "hf" in dbg:
            k.dma("sp", dbg_out["hf"], Hf[:], r=[b_Hf], is_out=True)
        if "kv" in dbg:
            with ExitStack() as ps_:
                t1 = sb("dbgt1", [128, 4 * 1280], stack=ps_); t2 = sb("dbgt2", [128, 10 * 512], stack=ps_)
                bt1, bt2 = Buf("t1"), Buf("t2")
                k.op("dve", lambda e: e.tensor_copy(out=t1[:], in_=kT_all[:].rearrange("p a b -> p (a b)")), r=[b_kT], w=[bt1])
                k.op("dve", lambda e: e.tensor_copy(out=t2[:], in_=v_tok[:].rearrange("p a b -> p (a b)")), r=[b_v], w=[bt2])
                k.dma("sp", dbg_out["kT"], t1[:], r=[bt1], is_out=True)
                k.dma("sp", dbg_out["v"], t2[:], r=[bt2], is_out=True)
                k.barrier()

        gst1.close()
        k.barrier()
        p2s = ExitStack()
        Hb = sb("Hb", [128, 2048], stack=p2s); b_Hb = Buf("Hb")
        sbt = [sb("sbt%d" % i, [128, 2048], stack=p2s) for i in range(2)]; b_sbt = [Buf("sbt0"), Buf("sbt1")]
        ebt = [sb("ebt%d" % i, [128, 32], stack=p2s) for i in range(2)]; b_ebt = [Buf("ebt0"), Buf("ebt1")]
        hsv = [sb("hsv%d" % i, [128, 2048], BF16, stack=p2s) for i in range(2)]; b_hsv = [Buf("hsv0"), Buf("hsv1")]
        k.op("dve", lambda e: e.memset(Hb[:], 0.0), w=[b_Hb])

        def p2_step(n_):
            s = 31 - n_
            i = n_ % 2
            k.dma("sp", sbt[i][:], sb_d[s], w=[b_sbt[i]])
            k.dma("sp", ebt[i][:], eb_d[s], w=[b_ebt[i]])
            if s <= 7:
                k.op("act", lambda e: e.activation(out=hsv[i][:], in_=Hb[:], func=AF.Identity), r=[b_Hb], w=[b_hsv[i]])
                k.dma("sp", hb_d[s], hsv[i][:], r=[b_hsv[i]])
            k.op("dve", lambda e: e.tensor_scalar(out=ebt[i][:], in0=ebt[i][:], scalar1=ab_t[:, s:s + 1], scalar2=None, op0=ALU.mult), r=[b_ab, b_ebt[i]], w=[b_ebt[i]])
            k.op("dve", lambda e: e.tensor_tensor(out=V(Hb[:], 64), in0=V(Hb[:], 64), in1=bc_hp(ebt[i][:]), op=ALU.mult), r=[b_ebt[i], b_Hb], w=[b_Hb])
            k.op("dve", lambda e: e.scalar_tensor_tensor(out=Hb[:], in0=sbt[i][:], scalar=ab_t[:, s:s + 1], in1=Hb[:], op0=ALU.mult, op1=ALU.add), r=[b_sbt[i], b_ab, b_Hb], w=[b_Hb])
        p2_tasks = [(lambda n_=n_: p2_step(n_)) for n_ in range(32)]

        with ExitStack() as ps_:
            hTa = sb("hTa", [128, KC, 256], BF16, stack=ps_); b_hTa = Buf("hTa")
            qTg = sb("qTg", [128, 16, 256], BF16, stack=ps_); b_qT = Buf("qTg")
            amb = sb("amb", [128, 3, 384], BF16, stack=ps_); b_amb = Buf("amb")
            k.dma("pool", amb[:], amask.rearrange("a p c -> p a c"), w=[b_amb])
            ao = [sb("ao%d" % i, [128, 2048], stack=ps_) for i in range(2)]; b_ao = [Buf("ao0"), Buf("ao1")]
            aon = sb("aon", [128, 2048], BF16, stack=ps_); b_aon = Buf("aon")
            Pm = [sb("Pm%d" % i, [128, 384], BF16, stack=ps_) for i in range(2)]; b_Pm = [Buf("Pm0"), Buf("Pm1")]
            PT = [sb("PT%d" % i, [128, 3, 128], BF16, stack=ps_) for i in range(2)]; b_PT = [Buf("PT0"), Buf("PT1")]
            st = [sb("ast%d" % i, [128, 8], stack=ps_) for i in range(2)]; b_st = [Buf("ast0"), Buf("ast1")]
            sq = sb("asq", [128, 20], stack=ps_); b_sq = Buf("asq")
            aosq = sb("aosq", [128, 2048], stack=ps_); b_aosq = Buf("aosq")
            pc = [0]
            for pr in range(4):
                for si in range(2):
                    s = pr * 2 + si
                    make_hT(xm[s * 128:(s + 1) * 128, :], 128, lambda kc, src, si=si: [(hTa[:, kc, si * 128:(si + 1) * 128], src)], b_hTa)

                def epi_q(blk, ps, pbuf, pr=pr):
                    k.op("act", lambda e: e.activation(out=qTg[:, blk, :], in_=ps[:, 0:256], func=AF.Identity), r=[pbuf], w=[b_qT])
                    rope_rows(ps, pbuf, 0, pr * 256, 256, qTg[0:32, blk, :], b_qT)
                p2cnt = [0]

                def p2_hook():
                    p2cnt[0] += 1
                    if p2cnt[0] % 2 == 0 and p2_tasks:
                        p2_tasks.pop(0)()
                bg_hook[0] = p2_hook
                gemm(lambda kc: hTa[:, kc, :], 256, KC, w_in, CQ, 16, epi_q, [b_hTa])
                bg_hook[0] = lambda: None
                flush_epi()
                items = [(si, hq) for si in range(2) for hq in range(16)]

                def geo(n):
                    si, hq = items[n]
                    c = pr * 2 + si
                    kvt = [9 if c == 0 else c - 1, c, c + 1]
                    mt = 0 if c == 0 else (2 if c == 7 else 1)
                    return si, hq, c, kvt, mt, hq // 4, n % 2

                def stA(n):
                    si, hq, c, kvt, mt, g, i = geo(n)
                    pS = 2 + i
                    k.op("pe", lambda e: e.matmul(bank[pS][:, 0:384], lhsT=identb, rhs=amb[:, mt, :], start=True, stop=False), r=[b_c16, b_amb], w=[bkb[pS]])
                    for kb in range(3):
                        k.op("pe", lambda e, kb=kb: e.matmul(bank[pS][:, kb * 128:(kb + 1) * 128], lhsT=qTg[:, hq, si * 128:(si + 1) * 128], rhs=kT_all[:, g, kvt[kb] * 128:(kvt[kb] + 1) * 128], start=False, stop=(kb == 2)), r=[b_qT, b_kT], w=[bkb[pS]])

                def stB(n):
                    si, hq, c, kvt, mt, g, i = geo(n)
                    pS = 2 + i
                    s_ = st[i]; bs_ = b_st[i]
                    k.op("dve", lambda e: e.reduce_max(out=s_[:, 0:1], in_=bank[pS][:, 0:384], axis=mybir.AxisListType.X), r=[bkb[pS]], w=[bs_])
                    k.op("dve", lambda e: e.tensor_scalar(out=s_[:, 1:2], in0=s_[:, 0:1], scalar1=-SCALE, scalar2=None, op0=ALU.mult), r=[bs_], w=[bs_])
                    k.op("dve", lambda e: e.tensor_scalar(out=s_[:, 1:2], in0=s_[:, 1:2], scalar1=nsink[:, hq:hq + 1], scalar2=None, op0=ALU.min), r=[bs_, b_nsink], w=[bs_])
                    k.op("act", lambda e: e.activation(out=Pm[i][:], in_=bank[pS][:, 0:384], func=AF.Exp, scale=SCALE, bias=s_[:, 1:2]), r=[bkb[pS], bs_], w=[b_Pm[i]])
                    k.op("act", lambda e: e.activation(out=s_[:, 3:4], in_=s_[:, 1:2], func=AF.Exp, bias=sink_t[:, hq:hq + 1]), r=[bs_, b_sink], w=[bs_])
                    k.op("dve", lambda e: e.reduce_sum(out=s_[:, 2:3], in_=Pm[i][:], axis=mybir.AxisListType.X), r=[b_Pm[i]], w=[bs_])
                    k.op("dve", lambda e: e.tensor_tensor(out=s_[:, 4:5], in0=s_[:, 2:3], in1=s_[:, 3:4], op=ALU.add), r=[bs_], w=[bs_])
                    k.op("dve", lambda e: e.reciprocal(out=s_[:, 5:6], in_=s_[:, 4:5]), r=[bs_], w=[bs_])

                def stC(n):
                    si, hq, c, kvt, mt, g, i = geo(n)
                    pT = 4 + i
                    pvb = bank[pT].bitcast(BF16)
                    for kb in range(3):
                        k.op("pe", lambda e, kb=kb: e.transpose(out=pvb[:, kb * 128:(kb + 1) * 128], in_=Pm[i][:, kb * 128:(kb + 1) * 128], identity=identb), r=[b_Pm[i], b_c16], w=[bkb[pT]])
                    k.op("act", lambda e: e.activation(out=PT[i][:].rearrange("p a b -> p (a b)"), in_=pvb[:, 0:384], func=AF.Identity), r=[bkb[pT]], w=[b_PT[i]])

                def stD(n):
                    si, hq, c, kvt, mt, g, i = geo(n)
                    pO = 6 + i
                    for kb in range(3):
                        k.op("pe", lambda e, kb=kb: e.matmul(bank[pO][:, 0:128], lhsT=PT[i][:, kb, :], rhs=v_tok[:, kvt[kb], g * 128:(g + 1) * 128], start=(kb == 0), stop=(kb == 2)), r=[b_PT[i], b_v], w=[bkb[pO]])
                    k.op("dve", lambda e: e.tensor_scalar(out=ao[si][:, hq * 128:(hq + 1) * 128], in0=bank[pO][:, 0:128], scalar1=st[i][:, 5:6], scalar2=None, op0=ALU.mult), r=[bkb[pO], b_st[i]], w=[b_ao[si]])
                    if hq == 15:
                        k.op("dve", lambda e: e.tensor_tensor(out=aosq[:], in0=ao[si][:], in1=ao[si][:], op=ALU.mult), r=[b_ao[si]], w=[b_aosq])
                        k.op("dve", lambda e: e.reduce_sum(out=sq[:, 16:17], in_=aosq[:], axis=mybir.AxisListType.X), r=[b_aosq], w=[b_sq])
                        k.op("dve", lambda e: e.tensor_scalar(out=sq[:, 17:18], in0=sq[:, 16:17], scalar1=1.0 / 2048, scalar2=RMS_EPS, op0=ALU.mult, op1=ALU.add), r=[b_sq], w=[b_sq])
                        k.op("act", lambda e: e.activation(out=sq[:, 18:19], in_=sq[:, 17:18], func=AF.Ln), r=[b_sq], w=[b_sq])
                        k.op("act", lambda e: e.activation(out=sq[:, 18:19], in_=sq[:, 18:19], func=AF.Exp, scale=-0.5), r=[b_sq], w=[b_sq])
                        k.op("dve", lambda e: e.tensor_scalar(out=aon[:], in0=ao[si][:], scalar1=sq[:, 18:19], scalar2=None, op0=ALU.mult), r=[b_ao[si], b_sq], w=[b_aon])
                        k.dma("sp", mix_d[c * 128:(c + 1) * 128, 0:2048], aon[:], r=[b_aon])

                stA(0)
                for n in range(len(items)):
                    if n + 1 < len(items):
                        stA(n + 1)
                    stB(n)
                    if not nomod:
                        ada_unit(64 + pr * 32 + n)
                        flush_epi()
                    stC(n)
                    stD(n)
            k.barrier()

        while p2_tasks:
            p2_tasks.pop(0)()
        k.barrier()
        p2s.close()
        if "hb" in dbg:
            k.dma("sp", dbg_out["hb"], hb_d, is_out=True)
        ast_.close()
        k.barrier()
        gst2 = ExitStack()
        alloc_group(gst2, "b", False)
        with ExitStack() as ps_:
            BT = sb("BT", [128, 4, 3, 128], BF16, stack=ps_); b_BT = Buf("BT")
            CT = sb("CT", [128, 4, 3, 128], BF16, stack=ps_); b_CT = Buf("CT")
            cacc = sb("cacc", [128, 3, 128], stack=ps_); b_cacc = Buf("cacc")
            gz = sb("gz", [128, 3, 2048], BF16, stack=ps_); b_gz = Buf("gz")
            zT = sb("zT", [128, 3 * SW], BF16, stack=ps_); b_zT = Buf("zT")
            R = [sb("R%d" % i, [128, 16, 128], stack=ps_) for i in range(2)]; b_R = [Buf("R0"), Buf("R1")]
            dec = [sb("dec%d" % i, [128, 512], stack=ps_) for i in range(2)]; b_dec = [Buf("dec0"), Buf("dec1")]
            Mt = [sb("Mt%d" % i, [128, 4, 128], BF16, stack=ps_) for i in range(2)]; b_Mt = [Buf("Mt0"), Buf("Mt1")]
            cbm = [sb("cbm%d" % i, [128, 4, 128], stack=ps_) for i in range(2)]; b_cbm = [Buf("cbF"), Buf("cbB")]
            xdt = [sb("xdt%d" % i, [128, 2048], BF16, stack=ps_) for i in range(2)]; b_xdt = [Buf("xdtf"), Buf("xdtb")]
            hin = [sb("hin%d" % i, [128, 2048], BF16, stack=ps_) for i in range(2)]; b_hin = [Buf("hinf"), Buf("hinb")]
            yacc = sb("yacc", [128, 2048], stack=ps_); b_yacc = Buf("yacc")
            ytmp = sb("ytmp", [128, 512], stack=ps_); b_ytmp = Buf("ytmp")
            ysq = sb("ysq", [128, 8], stack=ps_); b_ysq = Buf("ysq")
            yn = sb("yn", [128, 2048], BF16, stack=ps_); b_yn = Buf("yn")
            xs_tok = xsL[0]; B_tok = BtL[0]; b_xs0 = b_xs[0]; b_Bt0 = b_Bt[0]; b_hTg0 = b_hTg[0]
            for G in OWNG:
                ns = len(G)
                for t_ in group_hT_tasks(G, 0):
                    t_()
                gemm(hT_fn(ns), ns * SW, KC, w_in, CX, 16, lambda blk, ps, pbuf, G=G: conv_epi(blk, ps, pbuf, G, xs_tok, b_xs0, blk * 128), [b_hTg0], two_phase=True)
                gemm(hT_fn(ns), ns * SW, KC, w_in, CB, 4, lambda blk, ps, pbuf, G=G: conv_epi(16 + blk, ps, pbuf, G, B_tok, b_Bt0, blk * 128, feat=(BT[:, blk, :, :], b_BT, cacc, b_cacc)), [b_hTg0], two_phase=True)
                gemm(hT_fn(ns), ns * SW, KC, w_in, CC, 4, lambda blk, ps, pbuf, G=G: conv_epi(20 + blk, ps, pbuf, G, None, None, 0, feat=(CT[:, blk, :, :], b_CT, cacc, b_cacc)), [b_hTg0], two_phase=True)

                def epi_z(blk, ps, pbuf, ns=ns):
                    k.op("act", lambda e: e.activation(out=zT[:, 0:ns * SW], in_=ps[:, 0:ns * SW], func=AF.Silu), r=[pbuf], w=[b_zT])
                    pvb = bank[3].bitcast(BF16)
                    for si in range(ns):
                        k.op("pe", lambda e, si=si: e.transpose(out=pvb[:, si * 128:(si + 1) * 128], in_=zT[:, si * SW + 2:si * SW + 130], identity=identb), r=[b_zT, b_c16], w=[bkb[3]])
                    k.op("dve", lambda e: e.tensor_copy(out=gz[:, 0:ns, blk * 128:(blk + 1) * 128], in_=V(pvb[:, 0:ns * 128], 128)), r=[bkb[3]], w=[b_gz])
                gemm(hT_fn(ns), ns * SW, KC, w_in, CZ, 16, epi_z, [b_hTg0])
                flush_epi()
                dt_matmuls(G, 0)
                for t_ in chain_dt_tasks(G, 0):
                    t_()
                for si, s in enumerate(G):
                    dts_ = dts[:, 0, si, :]; adt_ = adt[:, 0, si, :]; ex_ = ex[:, 0, si, :]
                    bdts_ = b_dts[0][si]; badt_ = b_adt[0][si]; bex_ = b_ex[0][si]
                    xs3 = V(xs_tok[:, si, :], 64)
                    if s == 0:
                        dump("dts", dts_, [bdts_]); dump("ex", ex_, [bex_]); dump("xs", xs_tok[:, 0, :], [b_xs0], BF16)
                        dump("Btok", B_tok[:, 0, :], [b_Bt0], BF16); dump("BT", BT[:, :, 0, :], [b_BT], BF16); dump("CT", CT[:, :, 0, :], [b_CT], BF16)
                        dump("gz", gz[:, 0, :], [b_gz], BF16); dump("hf", Hf[:], [b_Hf])
                    k.dma("sp", hin[1][:], hb_d[s], w=[b_hin[1]])
                    k.op("act", lambda e: e.activation(out=hin[0][:], in_=Hf[:], func=AF.Identity), r=[b_Hf], w=[b_hin[0]])
                    for d in range(2):
                        k.op("dve", lambda e, d=d: e.tensor_tensor(out=V(xdt[d][:], 64), in0=xs3, in1=bc_hp(dts_[:, d * 32:(d + 1) * 32]), op=ALU.mult), r=[b_xs0, bdts_], w=[b_xdt[d]])
                    for g in range(4):
                        k.op("pe", lambda e, g=g: e.matmul(bank[3][:, g * 128:(g + 1) * 128], lhsT=BT[:, g, si, :], rhs=CT[:, g, si, :], start=True, stop=True), r=[b_BT, b_CT], w=[bkb[3]])
                    for d, M in ((0, Uin), (1, Lin)):
                        k.op("dve", lambda e, d=d, M=M: e.tensor_tensor(out=cbm[d][:], in0=V(bank[3][:, :], 128), in1=M.unsqueeze(1).broadcast_to([128, 4, 128]), op=ALU.mult), r=[bkb[3], b_cm], w=[b_cbm[d]])
                    for hh in range(2):
                        for d, M in ((0, Uin), (1, Lin)):
                            k.op("dve", lambda e, d=d, M=M: e.tensor_tensor(out=R[d][:], in0=M.unsqueeze(1).broadcast_to([128, 16, 128]), in1=adt_[:, d * 32 + hh * 16:d * 32 + hh * 16 + 16].unsqueeze(2).broadcast_to([128, 16, 128]), op=ALU.mult), r=[b_cm, badt_], w=[b_R[d]])
                        for q4 in range(4):
                            h0 = hh * 16 + q4 * 4
                            g = h0 // 8
                            for d, M2 in ((0, Lst), (1, Ust)):
                                pb = 2 + d
                                k.op("pe", lambda e, d=d, M2=M2, pb=pb: e.matmul(bank[pb][:, :], lhsT=M2, rhs=R[d][:, q4 * 4:(q4 + 1) * 4, :].rearrange("p a b -> p (a b)"), start=True, stop=True), r=[b_cm, b_R[d]], w=[bkb[pb]])
                                k.op("act", lambda e, d=d, pb=pb: e.activation(out=dec[d][:], in_=bank[pb][:, :], func=AF.Exp), r=[bkb[pb]], w=[b_dec[d]])
                                k.op("dve", lambda e, d=d, g=g: e.tensor_tensor(out=Mt[d][:], in0=V(dec[d][:], 128), in1=cbm[d][:, g, :].unsqueeze(1).broadcast_to([128, 4, 128]), op=ALU.mult), r=[b_dec[d], b_cbm[d]], w=[b_Mt[d]])
                            for i4 in range(4):
                                h = h0 + i4
                                pb = 4 + ((h % 16) // 8)
                                o = bank[pb][:, (h % 8) * 64:(h % 8) * 64 + 64]
                                k.op("pe", lambda e, o=o, i4=i4, h=h: e.matmul(o, lhsT=Mt[0][:, i4, :], rhs=xdt[0][:, h * 64:(h + 1) * 64], start=True, stop=False), r=[b_Mt[0], b_xdt[0]], w=[bkb[pb]])
                                k.op("pe", lambda e, o=o, i4=i4, h=h: e.matmul(o, lhsT=Mt[1][:, i4, :], rhs=xdt[1][:, h * 64:(h + 1) * 64], start=False, stop=True), r=[b_Mt[1], b_xdt[1]], w=[bkb[pb]])
                        for gg in range(2):
                            g = hh * 2 + gg
                            ys = yacc[:, g * 512:(g + 1) * 512]
                            k.op("dve", lambda e, g=g, ys=ys: e.tensor_tensor(out=V(ys, 64), in0=V(xs_tok[:, si, g * 512:(g + 1) * 512], 64), in1=bc_hp(hp[:, 128 + g * 8:128 + g * 8 + 8], 8), op=ALU.mult), r=[b_xs0, b_hp], w=[b_yacc])
                            k.op("dve", lambda e, gg=gg, ys=ys: e.tensor_tensor(out=ys, in0=ys, in1=bank[4 + gg][:, :], op=ALU.add), r=[bkb[4 + gg]], w=[b_yacc])
                            if s == 0:
                                dump("yd%d" % g, ys, [b_yacc])
                            for d in range(2):
                                pb = 2 + d
                                k.op("pe", lambda e, d=d, g=g, pb=pb: e.matmul(bank[pb][:, :], lhsT=CT[:, g, si, :], rhs=hin[d][:, g * 512:(g + 1) * 512], start=True, stop=True), r=[b_CT, b_hin[d]], w=[bkb[pb]])
                                k.op("dve", lambda e, d=d, g=g, pb=pb: e.tensor_tensor(out=V(ytmp[:], 64), in0=V(bank[pb][:, :], 64), in1=bc_hp(ex_[:, d * 32 + g * 8:d * 32 + g * 8 + 8], 8), op=ALU.mult), r=[bkb[pb], bex_], w=[b_ytmp])
                                k.op("dve", lambda e, ys=ys: e.tensor_tensor(out=ys, in0=ys, in1=ytmp[:], op=ALU.add), r=[b_ytmp], w=[b_yacc])
                    if s == 0:
                        dump("ypre", yacc[:], [b_yacc]); dump("hinb", hin[1][:], [b_hin[1]], BF16)
                    k.op("dve", lambda e: e.tensor_tensor(out=yacc[:], in0=yacc[:], in1=gz[:, si, :], op=ALU.mult), r=[b_gz], w=[b_yacc])
                    rsc = R[0][:].rearrange("p a b -> p (a b)")
                    k.op("dve", lambda e: e.tensor_tensor(out=rsc, in0=yacc[:], in1=yacc[:], op=ALU.mult), r=[b_yacc], w=[b_R[0]])
                    k.op("dve", lambda e: e.reduce_sum(out=ysq[:, 0:4], in_=V(rsc, 512), axis=mybir.AxisListType.X), r=[b_R[0]], w=[b_ysq])
                    k.op("dve", lambda e: e.tensor_scalar(out=ysq[:, 4:8], in0=ysq[:, 0:4], scalar1=1.0 / 512, scalar2=RMS_EPS, op0=ALU.mult, op1=ALU.add), r=[b_ysq], w=[b_ysq])
                    k.op("act", lambda e: e.activation(out=ysq[:, 4:8], in_=ysq[:, 4:8], func=AF.Ln), r=[b_ysq], w=[b_ysq])
                    k.op("act", lambda e: e.activation(out=ysq[:, 4:8], in_=ysq[:, 4:8], func=AF.Exp, scale=-0.5), r=[b_ysq], w=[b_ysq])
                    k.op("dve", lambda e: e.tensor_tensor(out=V(yn[:], 512), in0=V(yacc[:], 512), in1=ysq[:, 4:8].unsqueeze(2).broadcast_to([128, 4, 512]), op=ALU.mult), r=[b_yacc, b_ysq], w=[b_yn])
                    k.dma("sp", mix_d[s * 128:(s + 1) * 128, 2048:4096], yn[:], r=[b_yn])
                    slot_states(si, s, do_f=True, do_b=False, act_f_col=None)
            k.barrier()
        gst2.close()
        k.barrier()
        with ExitStack() as ps_:
            stg_ = sb("mcB_s", [64, 128], stack=ps_); bs_ = Buf("mcB_s")
            k.dma("sp", stg_[:], mod_d[3 * D:5 * D].rearrange("(a b) -> a b", b=128), w=[bs_])
            k.op("pe", lambda e: e.transpose(out=bank[2][:, 0:64], in_=stg_[:], identity=ident[0:64, 0:64]), r=[bs_, b_cm], w=[bkb[2]])
            k.op("dve", lambda e: e.tensor_copy(out=mB[:, 0:64], in_=bank[2][:, 0:64]), r=[bkb[2]], w=[b_mcB])
            k.op("dve", lambda e: e.tensor_scalar(out=mB[:, 32:64], in0=mB[:, 32:64], scalar1=1.0, scalar2=None, op0=ALU.add), r=[b_mcB], w=[b_mcB])
            if "mod" in dbg:
                k.dma("sp", dbg_out["mod"], mod_d, is_out=True)
            k.barrier()
        if "mix" in dbg:
            k.dma("sp", dbg_out["mix"], mix_d, is_out=True)

        if full:
            def store_T(ps, pbuf, stg, b_stg, ost, b_ost, dst_d, half, blk):
                k.op("act", lambda e: e.activation(out=stg[:], in_=ps[:, 0:512], func=AF.Identity), r=[pbuf], w=[b_stg])
                for tt in range(4):
                    k.op("pe", lambda e, tt=tt: e.transpose(out=bank[3][:, tt * 128:(tt + 1) * 128], in_=stg[:, tt * 128:(tt + 1) * 128], identity=ident), r=[b_stg, b_cm], w=[bkb[3]])
                k.op("dve", lambda e: e.tensor_copy(out=ost[:].rearrange("p a b -> p (a b)"), in_=bank[3][:, :]), r=[bkb[3]], w=[b_ost])
                k.dma("sp", dst_d[half * 512:(half + 1) * 512, blk * 128:(blk + 1) * 128].rearrange("(t p) c -> p t c", p=128), ost[:], r=[b_ost])

            with ExitStack() as ps_:
                mixT = sb("mixT", [128, KC, 1024], BF16, stack=ps_); b_mixT = Buf("mixT")
                mt_ = sb("mixtile", [128, D], BF16, stack=ps_); b_mt = Buf("mixtile")
                stg = sb("ostg", [128, 512], stack=ps_); b_stg = Buf("ostg")
                ost = sb("oost", [128, 4, 128], stack=ps_); b_ost = Buf("oost")
                for t in range(8):
                    k.dma("sp", mt_[:], mix_d[t * 128:(t + 1) * 128, :], w=[b_mt])
                    for k0 in range(0, KC, 8):
                        pb = 4 + (k0 // 8) % 2
                        pvb = bank[pb].bitcast(BF16)
                        for j in range(8):
                            kc = k0 + j
                            k.op("pe", lambda e, kc=kc, j=j, pvb=pvb: e.transpose(out=pvb[:, j * 128:(j + 1) * 128], in_=mt_[:, kc * 128:(kc + 1) * 128], identity=identb), r=[b_mt, b_c16], w=[bkb[pb]])
                        for j in range(8):
                            kc = k0 + j
                            nwc = anw[:, 0, kc:kc + 1] if kc < 16 else snw[:, 0, kc - 16:kc - 15]
                            k.op("act", lambda e, kc=kc, j=j, pvb=pvb, nwc=nwc: e.activation(out=mixT[:, kc, t * 128:(t + 1) * 128], in_=pvb[:, j * 128:(j + 1) * 128], func=AF.Identity, scale=nwc), r=[bkb[pb], b_anw, b_snw], w=[b_mixT])
                for half in range(2):
                    gemm(lambda kc, half=half: mixT[:, kc, half * 512:(half + 1) * 512], 512, KC, w_out, 0, 32,
                         lambda blk, ps, pbuf, half=half: store_T(ps, pbuf, stg, b_stg, ost, b_ost, mixr_d, half, blk), [b_mixT])
                flush_epi()
                k.barrier()
            if "mixr" in dbg:
                k.dma("sp", dbg_out["mixr"], mixr_d, is_out=True)

            def ln_phase(stack, tiles, br_d, res_fn, rows3, emit):
                rows = sb("rows", [8, D], stack=stack); b_rows = Buf("rows")
                sel = sb("selr", [8, 8, 128], stack=stack); b_sel = Buf("sel")
                k.dma("sp", sel[:].rearrange("p a b -> p (a b)"), sel_d[:, :], w=[b_sel])
                k.op("dve", lambda e: e.memset(rows[:], 0.0), w=[b_rows])
                k.dma("sp", rows[0:1, :], mod_d[2 * D:3 * D].rearrange("(a b) -> a b", a=1), w=[b_rows])
                k.dma("sp", rows[1:2, :], mod_d[5 * D:6 * D].rearrange("(a b) -> a b", a=1), w=[b_rows])
                for r_, src in ((2, ln1_g), (3, ln1_b), (4, ln2_g), (5, ln2_b)):
                    k.dma("sp", rows[r_:r_ + 1, :], src.rearrange("(a b) -> a b", a=1), w=[b_rows])
                bc3 = sb("bc3", [128, 3, D], stack=stack); b_bc3 = Buf("bc3")
                rt_ = sb("lnr", [128, D], stack=stack); b_rt = Buf("lnr")
                bst = sb("bst", [128, 8, 6], stack=stack); b_bst = Buf("bst")
                mv = sb("mv", [128, 4], stack=stack); b_mv = Buf("mv")
                for j, r_ in enumerate(rows3):
                    for n in range(8):
                        pb = 2 + n % 2
                        k.op("pe", lambda e, r_=r_, n=n, pb=pb: e.matmul(bank[pb][:, :], lhsT=sel[:, r_, :], rhs=rows[:, n * 512:(n + 1) * 512], start=True, stop=True), r=[b_sel, b_rows], w=[bkb[pb]])
                        k.op("act", lambda e, j=j, n=n, pb=pb: e.activation(out=bc3[:, j, n * 512:(n + 1) * 512], in_=bank[pb][:, :], func=AF.Identity), r=[bkb[pb]], w=[b_bc3])
                for t in tiles:
                    k.dma("sp", rt_[:], br_d[t * 128:(t + 1) * 128, :], w=[b_rt])
                    res_fn(t)
                    k.op("dve", lambda e: e.tensor_tensor(out=rt_[:], in0=rt_[:], in1=bc3[:, 0, :], op=ALU.mult), r=[b_bc3], w=[b_rt])
                    k.op("dve", lambda e: e.scalar_tensor_tensor(out=rt_[:], in0=xt[:], scalar=ALPHA, in1=rt_[:], op0=ALU.mult, op1=ALU.add), r=[b_xt], w=[b_rt])
                    for n in range(8):
                        k.op("dve", lambda e, n=n: e.bn_stats(out=bst[:, n, :], in_=rt_[:, n * 512:(n + 1) * 512]), r=[b_rt], w=[b_bst])
                    k.op("dve", lambda e: e.bn_aggr(out=mv[:, 0:2], in_=bst[:].rearrange("p a b -> p (a b)")), r=[b_bst], w=[b_mv])
                    k.op("dve", lambda e: e.tensor_scalar(out=mv[:, 2:3], in0=mv[:, 1:2], scalar1=LN_EPS, scalar2=None, op0=ALU.add), r=[b_mv], w=[b_mv])
                    k.op("act", lambda e: e.activation(out=mv[:, 2:3], in_=mv[:, 2:3], func=AF.Ln), r=[b_mv], w=[b_mv])
                    k.op("act", lambda e: e.activation(out=mv[:, 2:3], in_=mv[:, 2:3], func=AF.Exp, scale=-0.5), r=[b_mv], w=[b_mv])
                    k.op("dve", lambda e: e.tensor_scalar(out=rt_[:], in0=rt_[:], scalar1=mv[:, 0:1], scalar2=mv[:, 2:3], op0=ALU.subtract, op1=ALU.mult), r=[b_mv], w=[b_rt])
                    k.op("dve", lambda e: e.tensor_tensor(out=rt_[:], in0=rt_[:], in1=bc3[:, 1, :], op=ALU.mult), r=[b_bc3], w=[b_rt])
                    k.op("dve", lambda e: e.tensor_tensor(out=rt_[:], in0=rt_[:], in1=bc3[:, 2, :], op=ALU.add), r=[b_bc3], w=[b_rt])
                    emit(t, rt_, b_rt)

            for half in range(2):
                h2s = ExitStack()
                h2T = sb("h2T", [128, KC, 512], BF16, stack=h2s); b_h2T = Buf("h2T")
                with ExitStack() as ps_:
                    def res1(t):
                        k.dma("sp", xt[:], xm[t * 128:(t + 1) * 128, :], w=[b_xt])

                    def emit1(t, rt_, b_rt, half=half):
                        k.dma("sp", x1_d[t * 128:(t + 1) * 128, :], rt_[:], r=[b_rt])
                        tl = t - half * 4
                        transp_mod(rt_, b_rt, 128, lambda kc, src, tl=tl: [(h2T[:, kc, tl * 128:(tl + 1) * 128], src)], b_h2T, mB, b_mcB)
                    ln_phase(ps_, range(half * 4, half * 4 + 4), mixr_d, res1, (0, 2, 3), emit1)
                    k.barrier()
                with ExitStack() as ps_:
                    actT = sb("actT", [128, KCF, 512], BF16, stack=ps_); b_actT = Buf("actT")
                    sg = [sb("sg%d" % i, [128, 512], stack=ps_) for i in range(2)]; b_sg = [Buf("sg0"), Buf("sg1")]
                    stg = sb("fstg", [128, 512], stack=ps_); b_stg = Buf("fstg")
                    ost = sb("fost", [128, 4, 128], stack=ps_); b_ost = Buf("fost")
                    for blk in range(KCF):
                        i = blk % 2
                        gemm(lambda kc: h2T[:, kc, :], 512, KC, w_gate, blk * 128, 1,
                             lambda b_, ps, pbuf, i=i: k.op("act", lambda e: e.activation(out=sg[i][:], in_=ps[:, 0:512], func=AF.Silu), r=[pbuf], w=[b_sg[i]]), [b_h2T])
                        gemm(lambda kc: h2T[:, kc, :], 512, KC, w_up, blk * 128, 1,
                             lambda b_, ps, pbuf, i=i, blk=blk: k.op("dve", lambda e: e.tensor_tensor(out=actT[:, blk, :], in0=sg[i][:], in1=ps[:, 0:512], op=ALU.mult), r=[pbuf, b_sg[i]], w=[b_actT]), [b_h2T])
                    gemm(lambda kc: actT[:, kc, :], 512, KCF, w_down, 0, 32,
                         lambda blk, ps, pbuf, half=half: store_T(ps, pbuf, stg, b_stg, ost, b_ost, ffn_d, half, blk), [b_actT], units=3)
                    flush_epi()
                    k.barrier()
                h2s.close()
                k.barrier()
            if "x1" in dbg:
                k.dma("sp", dbg_out["x1"], x1_d, is_out=True)
            if "ffn" in dbg:
                k.dma("sp", dbg_out["ffn"], ffn_d, is_out=True)

            with ExitStack() as ps_:
                def res2(t):
                    k.dma("sp", xt[:], x1_d[t * 128:(t + 1) * 128, :], w=[b_xt])

                def emit2(t, rt_, b_rt):
                    k.dma("sp", y_out[t * 128:(t + 1) * 128, :], rt_[:], r=[b_rt], is_out=True)
                ln_phase(ps_, range(8), ffn_d, res2, (1, 4, 5), emit2)
                k.barrier()
        k.finish()
        print("instructions emitted:", k.ninst, flush=True)
    nc._in_names = in_names
    return nc


def _consts():
    t = np.arange(128)
    ident = np.eye(128, dtype=np.float32)
    Uin = (t[:, None] <= t[None, :]).astype(np.float32)
    Lin = (t[:, None] >= t[None, :]).astype(np.float32)
    Ust = (t[:, None] < t[None, :]).astype(np.float32)
    Lst = (t[:, None] > t[None, :]).astype(np.float32)
    ones = np.ones((128, 128), np.float32)
    cmat = np.concatenate([ident, Uin, Lin, Ust, Lst, ones], axis=1)
    sel = np.zeros((8, 8, 128), np.float32)
    for r in range(8):
        sel[r, r, :] = 1.0
    invf = (500000.0 ** (-np.arange(0, 32, 2, dtype=np.float32) / 32)).astype(np.float32)
    ropec = np.zeros((32, 2), np.float32)
    ropec[:, 0] = np.concatenate([invf, invf])
    ropec[:16, 1] = -1.0
    ropec[16:, 1] = 1.0
    psw = np.zeros((32, 32), np.float32)
    for m in range(32):
        psw[(m + 16) % 32, m] = 1.0
    return cmat, sel.reshape(8, 1024), ropec, psw


def prep_inputs(inputs):
    x = np.asarray(inputs["x"], np.float32)
    c = np.asarray(inputs["c"], np.float32)
    pos = np.asarray(inputs["positions"]).astype(np.int32)
    cmat, sel, ropec, psw = _consts()
    shared = {"cmat": cmat, "sel": sel, "ropec": ropec, "pswap": psw}
    for nm in ("w_ada", "w_in", "conv_w", "w_out", "w_gate", "w_up", "w_down"):
        shared[nm] = np.ascontiguousarray(np.asarray(inputs[nm], np.float32)[0])
    for nm in ("b_ada", "conv_b", "attn_sink", "a_log_fwd", "a_log_bwd", "dt_bias_fwd", "dt_bias_bwd",
               "ssd_d", "ssd_norm_w", "attn_norm_w", "ln1_g", "ln1_b", "ln2_g", "ln2_b"):
        shared[nm] = np.ascontiguousarray(np.asarray(inputs[nm], np.float32)[0])
    NEG = -30000.0
    qi = np.arange(128)[:, None]
    kj = np.arange(128)[None, :]
    in_maps = []
    for core in range(NCORES):
        b, j = core // 4, core % 4
        m = dict(shared)
        m["xm"] = np.ascontiguousarray(np.roll(x[b], -1024 * j, axis=0))
        xh = np.zeros((NSLOT * 4, D), np.float32)
        hmk = np.zeros((NSLOT, 4), np.float32)
        for s in range(NSLOT):
            t0 = ((8 * j + s) % 32) * 128
            for ii, tt in enumerate((t0 - 2, t0 - 1, t0 + 128, t0 + 129)):
                if 0 <= tt < 4096:
                    xh[4 * s + ii] = x[b, tt]
                    hmk[s, ii] = 1.0
        m["xh"] = xh
        m["hmask"] = np.ascontiguousarray(np.broadcast_to(hmk.reshape(1, -1), (128, NSLOT * 4)))
        pr = np.zeros((10, 128), np.int32)
        for ti, s in ((0, 0), (1, 1), (2, 2), (3, 3), (4, 4), (5, 5), (6, 6), (7, 7), (8, 8), (9, 31)):
            ch = (8 * j + s) % 32
            pr[ti] = pos[b, ch * 128:(ch + 1) * 128]
        m["posr"] = pr.reshape(-1)
        am = np.zeros((3, 128, 384), np.float32)
        prev = np.where(kj >= qi, 0.0, NEG)
        nxt = np.where(kj <= qi, 0.0, NEG)
        for ty in range(3):
            am[ty, :, 0:128] = prev
            am[ty, :, 256:384] = nxt
        if j == 0:
            am[0, :, 0:128] = NEG
        if j == 3:
            am[2, :, 256:384] = NEG
        m["amask"] = am
        af = np.zeros(NSLOT, np.float32)
        ab = np.zeros(NSLOT, np.float32)
        for s in range(NSLOT):
            if s >= 8 and s >= 32 - 8 * j:
                af[s] = 1.0
            if s < 8 or (8 <= s <= 31 - 8 * j):
                ab[s] = 1.0
        m["actf"] = np.ascontiguousarray(np.broadcast_to(af[None, :], (128, NSLOT)))
        m["actb"] = np.ascontiguousarray(np.broadcast_to(ab[None, :], (128, NSLOT)))
        m["cvec"] = np.ascontiguousarray(c[b].reshape(32, 128))
        in_maps.append(m)
    return in_maps


def kernel(**inputs):
    in_maps = prep_inputs(inputs)
    nc = build_nc()
    res = run_bass_kernel_spmd(nc, in_maps, core_ids=list(range(NCORES)))
    out = np.zeros((2, 4096, 4096), np.float32)
    for core in range(NCORES):
        b, j = core // 4, core % 4
        out[b, 1024 * j:1024 * (j + 1)] = res.results[core]["y_out"]
    return out
```

```python
import numpy as np
from contextlib import ExitStack
import concourse.bass as bass
import concourse.mybir as mybir
from concourse.bass_utils import run_bass_kernel_spmd

F32 = mybir.dt.float32
BF16 = mybir.dt.bfloat16
I32 = mybir.dt.int32
AF = mybir.ActivationFunctionType
ALU = mybir.AluOpType

NCORES = 8
D = 4096
KC = 32
DFF = 11008
KCF = 86
NSLOT = 32
SW = 132
GROUPS = [(0, 1, 2), (3, 4, 5), (6, 7, 8), (9, 10, 11), (12, 13, 14), (15, 16, 17),
          (18, 19, 20), (21, 22, 23), (24, 25, 26), (27, 28, 29), (30, 31)]
OWNG = [(0, 1, 2), (3, 4, 5), (6, 7)]
CQ, CK, CV, CZ, CX, CB, CC, CDT = 0, 2048, 2560, 3072, 5120, 7168, 7680, 8192
SCALE = 128.0 ** -0.5
PI = float(np.pi)
ALPHA = 2.0 ** 0.25
LN_EPS = 1e-5
RMS_EPS = 1e-6
KVSLOT = {0: 0, 1: 1, 2: 2, 3: 3, 4: 4, 5: 5, 6: 6, 7: 7, 8: 8, 31: 9}


class Tok:
    __slots__ = ("key", "val")

    def __init__(self, key, val=None):
        self.key, self.val = key, val


class Buf:
    def __init__(self, name):
        self.name = name
        self.w = None
        self.r = {}


class K:
    CE = ("pe", "act", "dve", "pool", "sp")

    def __init__(self, nc, es):
        self.nc = nc
        self.eng = {"pe": nc.tensor, "act": nc.scalar, "dve": nc.vector, "pool": nc.gpsimd, "sp": nc.sync}
        self.sem = {e: es.enter_context(nc.semaphore("s_" + e)) for e in self.eng}
        self.cnt = {e: 0 for e in self.eng}
        self.pend = {e: [] for e in self.eng}
        self.last = {e: None for e in self.eng}
        self.lasttok = {e: None for e in self.eng}
        self.waited = {e: {} for e in self.eng}
        self.dsem = {}
        self.dpos = {}
        for q in ("sp", "pool"):
            self.dsem[q] = []
            for i in range(24):
                key = "d_%s%d" % (q, i)
                self.sem[key] = es.enter_context(nc.semaphore(key))
                self.dsem[q].append([key, 0, None])
            self.dpos[q] = 0
        self.outtoks = []
        self.ninst = 0

    def _resolve(self, tok):
        if tok.val is None:
            e = tok.key
            self.last[e].then_inc(self.sem[e], 1)
            self.cnt[e] += 1
            for t in self.pend[e]:
                t.val = self.cnt[e]
            self.pend[e] = []
        return tok.val

    def _wait(self, e, tok):
        if tok is None:
            return
        if tok.key == e and e == "pe":
            return
        v = self._resolve(tok)
        if self.waited[e].get(tok.key, 0) >= v:
            return
        self.eng[e].wait_ge(self.sem[tok.key], v)
        self.waited[e][tok.key] = v

    def _deps(self, e, r, w):
        for b in r:
            self._wait(e, b.w)
        for b in w:
            self._wait(e, b.w)
            for t in list(b.r.values()):
                self._wait(e, t)

    def _mark(self, tok, r, w):
        for b in r:
            b.r[tok.key] = tok
        for b in w:
            b.w = tok
            b.r = {}

    def op(self, e, fn, r=(), w=()):
        self._deps(e, r, w)
        inst = fn(self.eng[e])
        self.ninst += 1
        self.last[e] = inst
        tok = Tok(e)
        self.pend[e].append(tok)
        self.lasttok[e] = tok
        self._mark(tok, r, w)
        return tok

    def dma(self, q, out, in_, r=(), w=(), is_out=False):
        self._deps(q, r, w)
        slot = self.dsem[q][self.dpos[q] % len(self.dsem[q])]
        self.dpos[q] += 1
        if slot[2] is not None:
            self._wait(q, slot[2])
        slot[1] += 16
        self.eng[q].dma_start(out=out, in_=in_).then_inc(self.sem[slot[0]], 16)
        self.ninst += 1
        tok = Tok(slot[0], slot[1])
        slot[2] = tok
        self._mark(tok, r, w)
        if is_out:
            self.outtoks.append(tok)
        return tok

    def barrier(self):
        toks = [self.lasttok[e] for e in self.CE if self.lasttok[e] is not None]
        for q in ("sp", "pool"):
            toks += [s[2] for s in self.dsem[q] if s[2] is not None]
        for e in self.CE:
            for t in toks:
                self._wait(e, t)

    def finish(self):
        self.barrier()
        for t in self.outtoks:
            self._wait("sp", t)


def build_nc(dbg=()):
    nc = bass.Bass("TRN2", target_bir_lowering=False)

    in_names = []
    stop_after = [d for d in dbg if d.startswith("stop:")]
    stop_after = stop_after[0][5:] if stop_after else None
    nomod = "nomod" in dbg

    def din(name, shape, dt=F32, need=True):
        if not need:
            return None
        in_names.append(name)
        return nc.dram_tensor(name, list(shape), dt, kind="ExternalInput").ap()

    xm = din("xm", [NSLOT * 128, D])
    xh = din("xh", [NSLOT * 4, D])
    hmask = din("hmask", [128, NSLOT * 4])
    posr = din("posr", [10 * 128], I32)
    amask = din("amask", [3, 128, 384])
    actf = din("actf", [128, NSLOT])
    actb = din("actb", [128, NSLOT])
    cmat_d = din("cmat", [128, 6 * 128])
    sel_d = din("sel", [8, 8 * 128])
    ropec = din("ropec", [32, 2])
    psw_d = din("pswap", [32, 32])
    cvec = din("cvec", [32, 128], need=not nomod)
    w_ada = din("w_ada", [D, 6 * D], need=not nomod)
    b_ada = din("b_ada", [6 * D], need=not nomod)
    mod_in = din("mod_in", [6 * D], need=nomod)
    full = stop_after is None
    w_in = din("w_in", [D, 8256])
    conv_w = din("conv_w", [5, 3072])
    conv_b = din("conv_b", [3072])
    attn_sink = din("attn_sink", [16])
    a_log_f = din("a_log_fwd", [32])
    a_log_b = din("a_log_bwd", [32])
    dtb_f = din("dt_bias_fwd", [32])
    dtb_b = din("dt_bias_bwd", [32])
    ssd_d = din("ssd_d", [32])
    ssd_nw = din("ssd_norm_w", [2048])
    attn_nw = din("attn_norm_w", [2048])
    w_out = din("w_out", [D, D], need=full)
    ln1_g = din("ln1_g", [D], need=full)
    ln1_b = din("ln1_b", [D], need=full)
    w_gate = din("w_gate", [D, DFF], need=full)
    w_up = din("w_up", [D, DFF], need=full)
    w_down = din("w_down", [DFF, D], need=full)
    ln2_g = din("ln2_g", [D], need=full)
    ln2_b = din("ln2_b", [D], need=full)
    y_out = nc.dram_tensor("y_out", [1024, D], F32, kind="ExternalOutput").ap()
    mod_d = nc.dram_tensor("mod_d", [6 * D], F32).ap()
    sb_d = nc.dram_tensor("sb_d", [NSLOT, 128, 2048], F32).ap()
    eb_d = nc.dram_tensor("eb_d", [NSLOT, 128, 32], F32).ap()
    hb_d = nc.dram_tensor("hb_d", [8, 128, 2048], BF16).ap()
    mix_d = nc.dram_tensor("mix_d", [1024, D], BF16).ap()
    mixr_d = nc.dram_tensor("mixr_d", [1024, D], F32).ap()
    x1_d = nc.dram_tensor("x1_d", [1024, D], F32).ap()
    ffn_d = nc.dram_tensor("ffn_d", [1024, D], F32).ap()
    dbg_out = {}
    if "hf" in dbg:
        dbg_out["hf"] = nc.dram_tensor("dbg_hf", [128, 2048], F32, kind="ExternalOutput").ap()
    if "kv" in dbg:
        dbg_out["kT"] = nc.dram_tensor("dbg_kT", [128, 4 * 1280], F32, kind="ExternalOutput").ap()
        dbg_out["v"] = nc.dram_tensor("dbg_v", [128, 10 * 512], F32, kind="ExternalOutput").ap()
    for nm, shp, dt in (("mod", [6 * D], F32), ("hb", [8, 128, 2048], BF16), ("mix", [1024, D], BF16),
                        ("mixr", [1024, D], F32), ("x1", [1024, D], F32), ("ffn", [1024, D], F32)):
        if nm in dbg:
            dbg_out[nm] = nc.dram_tensor("dbg_" + nm, shp, dt, kind="ExternalOutput").ap()
    es = ExitStack()
    with es:
        k = K(nc, es)

        used_names = {}
        want_dump = "p3dump" in dbg

        def dump(name, ap, bufs, dt=F32):
            if not want_dump:
                return
            shp = list(ap.shape)
            o = nc.dram_tensor("dmp_" + name, shp, dt, kind="ExternalOutput").ap()
            k.dma("sp", o, ap, r=bufs, is_out=True)

        def sb(name, shape, dt=F32, stack=None):
            n = used_names.get(name, 0)
            used_names[name] = n + 1
            if n:
                name = "%s_r%d" % (name, n)
            return (stack or es).enter_context(nc.sbuf_tensor(name, list(shape), dt))

        psA = es.enter_context(nc.psum_tensor("psA", [128, 2048], F32))
        psB = es.enter_context(nc.psum_tensor("psB", [128, 2048], F32))
        bank = [psA[:, i * 512:(i + 1) * 512] for i in range(4)] + [psB[:, i * 512:(i + 1) * 512] for i in range(4)]
        bkb = [Buf("bank%d" % i) for i in range(8)]

        def V(ap, c):
            return ap.rearrange("p (s c) -> p s c", c=c)

        cmat = sb("cmat_s", [128, 6, 128]); b_cm = Buf("cmat")
        k.dma("sp", cmat[:].rearrange("p a b -> p (a b)"), cmat_d[:, :], w=[b_cm])
        ident, Uin, Lin, Ust, Lst, ones = [cmat[:, i, :] for i in range(6)]
        cb16 = sb("cb16", [128, 2, 128], BF16); b_c16 = Buf("cb16")
        identb, onesb = cb16[:, 0, :], cb16[:, 1, :]
        k.op("dve", lambda e: e.tensor_copy(out=identb, in_=ident), r=[b_cm], w=[b_c16])
        k.op("dve", lambda e: e.tensor_copy(out=onesb, in_=ones), r=[b_cm], w=[b_c16])
        hm, b_hm = sb("hm", [128, NSLOT, 4]), Buf("hm")
        k.dma("sp", hm[:].rearrange("p a b -> p (a b)"), hmask[:, :], w=[b_hm])
        af_t, b_af = sb("af_t", [128, NSLOT]), Buf("af")
        k.dma("sp", af_t[:], actf[:, :], w=[b_af])
        ab_t, b_ab = sb("ab_t", [128, NSLOT]), Buf("ab")
        k.dma("sp", ab_t[:], actb[:, :], w=[b_ab])
        sink_t, b_sink = sb("sink_t", [128, 16]), Buf("sink")
        k.dma("sp", sink_t[:], attn_sink.partition_broadcast(128), w=[b_sink])
        nsink = sb("nsink", [128, 16]); b_nsink = Buf("nsink")
        k.op("dve", lambda e: e.tensor_scalar(out=nsink[:], in0=sink_t[:], scalar1=-1.0, scalar2=None, op0=ALU.mult), r=[b_sink], w=[b_nsink])
        hp = sb("hp", [128, 160]); b_hp = Buf("hp")
        k.dma("sp", hp[:, 0:32], dtb_f.partition_broadcast(128), w=[b_hp])
        k.dma("sp", hp[:, 32:64], dtb_b.partition_broadcast(128), w=[b_hp])
        k.dma("sp", hp[:, 64:96], a_log_f.partition_broadcast(128), w=[b_hp])
        k.dma("sp", hp[:, 96:128], a_log_b.partition_broadcast(128), w=[b_hp])
        k.dma("sp", hp[:, 128:160], ssd_d.partition_broadcast(128), w=[b_hp])
        k.op("act", lambda e: e.activation(out=hp[:, 64:128], in_=hp[:, 64:128], func=AF.Exp), r=[b_hp], w=[b_hp])
        k.op("dve", lambda e: e.tensor_scalar(out=hp[:, 64:128], in0=hp[:, 64:128], scalar1=-1.0, scalar2=None, op0=ALU.mult), r=[b_hp], w=[b_hp])

        def load_cols(name, src2d, R, nb):
            t = sb(name, [128, nb, R]); bt = Buf(name)
            with nc.sbuf_tensor(name + "_s", [R, nb * 128], F32) as stg:
                bs = Buf(name + "_s")
                k.dma("sp", stg[:], src2d, w=[bs])
                per = max(1, 512 // R)
                for b0 in range(0, nb, per):
                    n = min(per, nb - b0)
                    for i in range(n):
                        k.op("pe", lambda e, i=i: e.transpose(out=bank[2][:, i * R:(i + 1) * R], in_=stg[:, (b0 + i) * 128:(b0 + i + 1) * 128], identity=ident[0:R, 0:R]), r=[bs, b_cm], w=[bkb[2]])
                    k.op("dve", lambda e, n=n: e.tensor_copy(out=t[:, b0:b0 + n, :].rearrange("p a b -> p (a b)"), in_=bank[2][:, 0:n * R]), r=[bkb[2]], w=[bt])
                k.barrier()
            return t, bt

        cw, b_cw = load_cols("cw", conv_w[:, :], 5, 24)
        cbc, b_cbc = load_cols("cbc", conv_b.rearrange("(a b) -> a b", b=128), 24, 1)
        anw, b_anw = load_cols("anw", attn_nw.rearrange("(a b) -> a b", b=128), 16, 1)
        snw, b_snw = load_cols("snw", ssd_nw.rearrange("(a b) -> a b", b=128), 16, 1)

        condT = sb("condT", [128, 32], BF16); b_cond = Buf("condT")
        if nomod:
            k.dma("sp", mod_d, mod_in)
            k.barrier()
        else:
            with ExitStack() as ps_:
                cst = sb("cst", [32, 128], stack=ps_); b_cst = Buf("cst")
                k.dma("sp", cst[:], cvec[:, :], w=[b_cst])
                k.op("pe", lambda e: e.transpose(out=bank[2][:, 0:32], in_=cst[:], identity=ident[0:32, 0:32]), r=[b_cst, b_cm], w=[bkb[2]])
                k.op("act", lambda e: e.activation(out=condT[:], in_=bank[2][:, 0:32], func=AF.Silu), r=[bkb[2]], w=[b_cond])
                k.barrier()
        xt = sb("xt", [128, D]); b_xt = Buf("xt")
        NW = 3
        wsl = [sb("wsl%d" % i, [128, 32, 128], BF16) for i in range(NW)]; b_wsl = [Buf("wsl%d" % i) for i in range(NW)]
        wc = [0]
        gc = [0]

        def transp_mod(src_tile, b_src, nrows, dst_fn, b_dst, mc, b_mc):
            for k0 in range(0, KC, 4):
                pb = 6 + (k0 // 4) % 2
                for j in range(4):
                    kc = k0 + j
                    k.op("pe", lambda e, kc=kc, j=j: e.transpose(out=bank[pb][:, j * 128:j * 128 + nrows], in_=src_tile[0:nrows, kc * 128:(kc + 1) * 128], identity=ident[0:nrows, 0:nrows]), r=[b_src, b_cm], w=[bkb[pb]])
                for j in range(4):
                    kc = k0 + j
                    for (dst, src) in dst_fn(kc, bank[pb][:, j * 128:j * 128 + nrows]):
                        k.op("act", lambda e, kc=kc, dst=dst, src=src: e.activation(out=dst, in_=src, func=AF.Identity, scale=mc[:, 32 + kc:33 + kc], bias=mc[:, kc:kc + 1]), r=[bkb[pb], b_mc], w=[b_dst])

        def make_hT(src_rows, nrows, dst_fn, b_dst):
            k.dma("sp", xt[0:nrows, :], src_rows, w=[b_xt])
            transp_mod(xt, b_xt, nrows, dst_fn, b_dst, mA, b_mcA)

        pend_epi = [None]
        bg_hook = [lambda: None]

        def flush_epi():
            if pend_epi[0] is not None:
                f = pend_epi[0]; pend_epi[0] = None
                f()

        def gemm(act_fn, N, nkc, wsrc, col0, nblk, epi, r_act, units=1, two_phase=False):
            flush_epi()
            usz = [nkc // units + (1 if u < nkc % units else 0) for u in range(units)]
            uoff = [sum(usz[:u]) for u in range(units)]
            for blk in range(nblk):
                c0 = col0 + blk * 128
                pb = gc[0] % 2; gc[0] += 1
                for u in range(units):
                    wi = wc[0] % NW; wc[0] += 1
                    per = usz[u]
                    k.dma("pool", wsl[wi][:, 0:per, :], wsrc[uoff[u] * 128:(uoff[u] + per) * 128, c0:c0 + 128].rearrange("(kc p) c -> p kc c", p=128), w=[b_wsl[wi]])
                    for kk in range(per):
                        kc = uoff[u] + kk
                        k.op("pe", lambda e, kc=kc, kk=kk, wi=wi: e.matmul(bank[pb][:, 0:N], lhsT=wsl[wi][:, kk, :], rhs=act_fn(kc), start=(kc == 0), stop=(kc == nkc - 1)), r=[b_wsl[wi]] + r_act, w=[bkb[pb]])
                flush_epi()
                if two_phase:
                    pend_epi[0] = epi(blk, bank[pb], bkb[pb])
                else:
                    pend_epi[0] = (lambda blk=blk, pb=pb: epi(blk, bank[pb], bkb[pb]))
                bg_hook[0]()

        abar = sb("abar", [1, 4, 128]); b_abar = [Buf("abar%d" % i) for i in range(4)]
        amrow = sb("amrow", [1, 4, 128]); b_amrow = [Buf("amrow%d" % i) for i in range(4)]
        adc = [0]

        def ada_unit(u):
            c0 = u * 128
            i = adc[0] % 4; adc[0] += 1
            pb = gc[0] % 2; gc[0] += 1
            wi = wc[0] % NW; wc[0] += 1
            flush_epi()
            k.dma("pool", wsl[wi][:, 0:KC, :], w_ada[:, c0:c0 + 128].rearrange("(kc p) c -> p kc c", p=128), w=[b_wsl[wi]])
            k.dma("sp", abar[0:1, i, :], b_ada[c0:c0 + 128].rearrange("(a b) -> a b", a=1), w=[b_abar[i]])
            for kc in range(KC):
                k.op("pe", lambda e, kc=kc: e.matmul(bank[pb][0:1, 0:128], lhsT=condT[:, kc:kc + 1], rhs=wsl[wi][:, kc, :], start=(kc == 0), stop=False), r=[b_cond, b_wsl[wi]], w=[bkb[pb]])
            k.op("pe", lambda e: e.matmul(bank[pb][0:1, 0:128], lhsT=ones[0:1, 0:1], rhs=abar[0:1, i, :], start=False, stop=True), r=[b_cm, b_abar[i]], w=[bkb[pb]])

            def epi():
                k.op("act", lambda e: e.activation(out=amrow[0:1, i, :], in_=bank[pb][0:1, 0:128], func=AF.Identity), r=[bkb[pb]], w=[b_amrow[i]])
                k.dma("sp", mod_d[c0:c0 + 128].rearrange("(a b) -> a b", a=1), amrow[0:1, i, :], r=[b_amrow[i]])
            pend_epi[0] = epi
            bg_hook[0]()

        if not nomod:
            for u in range(64):
                ada_unit(u)
            flush_epi()
            k.barrier()
        mcA, b_mcA = load_cols("mcA", mod_d[0:2 * D].rearrange("(a b) -> a b", b=128), 64, 1)
        mA = mcA[:, 0, :]
        k.op("dve", lambda e: e.tensor_scalar(out=mA[:, 32:64], in0=mA[:, 32:64], scalar1=1.0, scalar2=None, op0=ALU.add), r=[b_mcA], w=[b_mcA])
        mcB = sb("mcB", [128, 1, 64]); b_mcB = Buf("mcB")
        mB = mcB[:, 0, :]

        Hf = sb("Hf", [128, 2048]); b_Hf = Buf("Hf")
        k.op("dve", lambda e: e.memset(Hf[:], 0.0), w=[b_Hf])
        wdt = sb("wdt", [128, KC, 64], BF16); b_wdt = Buf("wdt")
        k.dma("pool", wdt[:], w_in[:, CDT:CDT + 64].rearrange("(kc p) c -> p kc c", p=128), w=[b_wdt])
        ast_ = ExitStack()
        cosT = sb("cosT", [32, 1280], stack=ast_); sinT = sb("sinT", [32, 1280], stack=ast_); b_rope = Buf("rope")
        psw = sb("psw", [32, 32], stack=ast_); b_psw = Buf("psw")
        k.dma("sp", psw[:], psw_d[:, :], w=[b_psw])
        with ExitStack() as ps_:
            pi_i = sb("pi_i", [32, 1280], I32, stack=ps_); ang = sb("ang", [32, 1280], stack=ps_)
            rt = sb("rt", [32, 1280], stack=ps_); rc = sb("rc", [32, 2], stack=ps_)
            b_pi, b_ang, b_rt, b_rc = Buf("pi"), Buf("ang"), Buf("rt"), Buf("rc")
            k.dma("sp", pi_i[:], posr.partition_broadcast(32), w=[b_pi])
            k.dma("sp", rc[:], ropec[:, :], w=[b_rc])
            k.op("dve", lambda e: e.tensor_copy(out=ang[:], in_=pi_i[:]), r=[b_pi], w=[b_ang])
            k.op("dve", lambda e: e.tensor_scalar(out=ang[:], in0=ang[:], scalar1=rc[:, 0:1], scalar2=None, op0=ALU.mult), r=[b_ang, b_rc], w=[b_ang])
            for (dstT, off) in ((sinT, 0.0), (cosT, PI / 2)):
                k.op("dve", lambda e: e.tensor_scalar(out=rt[:], in0=ang[:], scalar1=off, scalar2=1.0 / (2 * PI), op0=ALU.add, op1=ALU.mult), r=[b_ang], w=[b_rt])
                k.op("dve", lambda e: e.tensor_copy(out=pi_i[:], in_=rt[:]), r=[b_rt], w=[b_pi])
                k.op("dve", lambda e: e.tensor_copy(out=rt[:], in_=pi_i[:]), r=[b_pi], w=[b_rt])
                k.op("dve", lambda e: e.scalar_tensor_tensor(out=rt[:], in0=rt[:], scalar=-2 * PI, in1=ang[:], op0=ALU.mult, op1=ALU.add), r=[b_rt, b_ang], w=[b_rt])
                k.op("dve", lambda e: e.tensor_scalar(out=rt[:], in0=rt[:], scalar1=off, scalar2=PI, op0=ALU.add, op1=ALU.min), r=[b_rt], w=[b_rt])
                k.op("dve", lambda e: e.tensor_scalar(out=rt[:], in0=rt[:], scalar1=-PI, scalar2=None, op0=ALU.max), r=[b_rt], w=[b_rt])
                k.op("act", lambda e, dstT=dstT: e.activation(out=dstT[:], in_=rt[:], func=AF.Sin), r=[b_rt], w=[b_rope])
            k.op("dve", lambda e: e.tensor_scalar(out=sinT[:], in0=sinT[:], scalar1=rc[:, 1:2], scalar2=None, op0=ALU.mult), r=[b_rope, b_rc], w=[b_rope])
            k.barrier()

        qr32 = sb("qr32", [32, 512], stack=ast_); b_qr = Buf("qr32")
        rtmp = sb("rtmp", [32, 2, 512], stack=ast_); b_rtmp = Buf("rtmp")

        def rope_rows(ps, pbuf, src_cols, tab_col0, n, dst, b_dst):
            k.op("act", lambda e: e.activation(out=qr32[:, 0:n], in_=ps[0:32, src_cols:src_cols + n], func=AF.Identity), r=[pbuf], w=[b_qr])
            k.op("pe", lambda e: e.matmul(bank[2][0:32, 0:n], lhsT=psw[:], rhs=qr32[:, 0:n], start=True, stop=True), r=[b_psw, b_qr], w=[bkb[2]])
            k.op("dve", lambda e: e.tensor_tensor(out=rtmp[:, 0, 0:n], in0=qr32[:, 0:n], in1=cosT[:, tab_col0:tab_col0 + n], op=ALU.mult), r=[b_qr, b_rope], w=[b_rtmp])
            k.op("dve", lambda e: e.tensor_tensor(out=rtmp[:, 1, 0:n], in0=bank[2][0:32, 0:n], in1=sinT[:, tab_col0:tab_col0 + n], op=ALU.mult), r=[bkb[2], b_rope], w=[b_rtmp])
            k.op("dve", lambda e: e.tensor_tensor(out=dst, in0=rtmp[:, 0, 0:n], in1=rtmp[:, 1, 0:n], op=ALU.add), r=[b_rtmp], w=[b_dst])

        kT_all = sb("kT_all", [128, 4, 1280], BF16, stack=ast_); b_kT = Buf("kT")
        v_tok = sb("v_tok", [128, 10, 512], BF16, stack=ast_); b_v = Buf("v")

        hTgL = xsL = BtL = ue = dg = dtraw = dts = adt = ex = wv = etfa = xw = sst = vT = None
        b_hTg = [Buf("hTg0"), Buf("hTg1")]; b_xs = [Buf("xs0"), Buf("xs1")]; b_Bt = [Buf("Bt0"), Buf("Bt1")]
        b_ue = [Buf("ue0"), Buf("ue1")]; b_dg = [Buf("dg0"), Buf("dg1")]
        b_dtraw = [Buf("dtraw0"), Buf("dtraw1")]
        b_dts = [[Buf("dts") for _ in range(3)] for _ in range(2)]; b_adt = [[Buf("adt") for _ in range(3)] for _ in range(2)]
        b_ex = [[Buf("ex") for _ in range(3)] for _ in range(2)]; b_wv = [[Buf("wv") for _ in range(3)] for _ in range(2)]
        b_etfa = Buf("etfa"); b_xw = [Buf("xw0"), Buf("xw1")]; b_sst = Buf("sst"); b_vT = Buf("vT")
        uec = [0]

        def alloc_group(gst, tag, p1):
            nonlocal hTgL, xsL, BtL, ue, dg, dtraw, dts, adt, ex, wv, etfa, xw, sst, vT
            npar = 2 if p1 else 1
            hTgL = [sb("hTg" + tag, [128, KC, 3 * SW], BF16, stack=gst) for _ in range(npar)]
            xsL = [sb("xs_tok" + tag, [128, 3, 2048], BF16, stack=gst) for _ in range(npar)]
            BtL = [sb("B_tok" + tag, [128, 3, 512], BF16, stack=gst) for _ in range(npar)]
            ue = [sb("ue%d" % i + tag, [128, 3 * SW], BF16, stack=gst) for i in range(2)]
            dg = [sb("dg%d" % i + tag, [128, 6, 128], BF16, stack=gst) for i in range(2)]
            dtraw = sb("dtraw" + tag, [128, npar, 3, 64], stack=gst)
            dts = sb("dts" + tag, [128, npar, 3, 64], stack=gst)
            adt = sb("adt" + tag, [128, npar, 3, 64], stack=gst)
            ex = sb("ex" + tag, [128, npar, 3, 192], stack=gst)
            wv = sb("wv" + tag, [128, npar, 3, 64], stack=gst)
            etfa = sb("etfa" + tag, [128, 32], stack=gst)
            xw = [sb("xw%d" % i + tag, [128, 2048], BF16, stack=gst) for i in range(2 if p1 else 1)]
            if p1:
                sst = sb("sst" + tag, [128, 1024], stack=gst)
                vT = sb("vT" + tag, [128, 3 * SW], BF16, stack=gst)

        from collections import deque
        bgq = deque()

        def bg_step(n=3):
            for _ in range(n):
                if bgq:
                    bgq.popleft()()

        def bg_drain():
            while bgq:
                bgq.popleft()()

        def hT_tile_tasks(src_rows, nrows, dst_fn, b_dst):
            tasks = [lambda: k.dma("sp", xt[0:nrows, :], src_rows, w=[b_xt])]
            for k0 in range(0, KC, 4):
                def t(k0=k0):
                    pb = 6 + (k0 // 4) % 2
                    for j in range(4):
                        kc = k0 + j
                        k.op("pe", lambda e, kc=kc, j=j: e.transpose(out=bank[pb][:, j * 128:j * 128 + nrows], in_=xt[0:nrows, kc * 128:(kc + 1) * 128], identity=ident[0:nrows, 0:nrows]), r=[b_xt, b_cm], w=[bkb[pb]])
                    for j in range(4):
                        kc = k0 + j
                        for (dst, src) in dst_fn(kc, bank[pb][:, j * 128:j * 128 + nrows]):
                            k.op("act", lambda e, kc=kc, dst=dst, src=src: e.activation(out=dst, in_=src, func=AF.Identity, scale=mA[:, 32 + kc:33 + kc], bias=mA[:, kc:kc + 1]), r=[bkb[pb], b_mcA], w=[b_dst])
                tasks.append(t)
            return tasks

        def group_hT_tasks(slots, par):
            ns = len(slots); s0 = slots[0]
            h = hTgL[par]
            tasks = []
            for si, s in enumerate(slots):
                tasks += hT_tile_tasks(xm[s * 128:(s + 1) * 128, :], 128, lambda kc, src, si=si: [(h[:, kc, si * SW + 2:si * SW + 130], src)], b_hTg[par])

            def hdst(kc, src):
                hv = V(h[:, kc, 0:ns * SW], SW)
                sv = V(src, 4)
                return [(hv[:, :, 0:2], sv[:, :, 0:2]), (hv[:, :, 130:132], sv[:, :, 2:4])]
            tasks += hT_tile_tasks(xh[4 * s0:4 * s0 + 4 * ns, :], 4 * ns, hdst, b_hTg[par])
            return tasks

        def conv_epi(blk_ch, ps, pbuf, slots, dst_tok, b_dsttok, dst_col0, feat=None):
            ns = len(slots); s0 = slots[0]
            i = uec[0] % 2; uec[0] += 1
            u = ue[i]
            k.op("act", lambda e: e.activation(out=u[:, 0:ns * SW], in_=ps[:, 0:ns * SW], func=AF.Identity), r=[pbuf], w=[b_ue[i]])
            u3 = V(u[:, 0:ns * SW], SW)
            k.op("dve", lambda e: e.tensor_tensor(out=u3[:, :, 0:2], in0=u3[:, :, 0:2], in1=hm[:, s0:s0 + ns, 0:2], op=ALU.mult), r=[b_hm, b_ue[i]], w=[b_ue[i]])
            k.op("dve", lambda e: e.tensor_tensor(out=u3[:, :, 130:132], in0=u3[:, :, 130:132], in1=hm[:, s0:s0 + ns, 2:4], op=ALU.mult), r=[b_hm, b_ue[i]], w=[b_ue[i]])
            if dst_tok is not None:
                for j in range(5):
                    k.op("dve", lambda e, j=j: e.tensor_scalar(out=dg[i][:, j, :], in0=identb, scalar1=cw[:, blk_ch, j:j + 1], scalar2=None, op0=ALU.mult), r=[b_c16, b_cw], w=[b_dg[i]])
                k.op("dve", lambda e: e.tensor_scalar(out=dg[i][:, 5, :], in0=identb, scalar1=cbc[:, 0, blk_ch:blk_ch + 1], scalar2=None, op0=ALU.mult), r=[b_c16, b_cbc], w=[b_dg[i]])
                pass

            def late():
                if dst_tok is not None:
                    pc = 3
                    for si in range(ns):
                        o = bank[pc][:, si * 128:(si + 1) * 128]
                        for j in range(5):
                            k.op("pe", lambda e, j=j, si=si, o=o: e.matmul(o, lhsT=u[:, si * SW + j:si * SW + j + 128], rhs=dg[i][:, j, :], start=(j == 0), stop=False), r=[b_ue[i], b_dg[i]], w=[bkb[pc]])
                        k.op("pe", lambda e, o=o: e.matmul(o, lhsT=onesb, rhs=dg[i][:, 5, :], start=False, stop=True), r=[b_c16, b_dg[i]], w=[bkb[pc]])
                    k.op("act", lambda e: e.activation(out=dst_tok[:, 0:ns, dst_col0:dst_col0 + 128], in_=V(bank[pc][:, 0:ns * 128], 128), func=AF.Silu), r=[bkb[pc]], w=[b_dsttok])
                if feat is not None:
                    dstT, b_dstT, acc, b_acc = feat
                    k.op("dve", lambda e: e.tensor_scalar(out=acc[:, 0:ns, :], in0=u3[:, :, 0:128], scalar1=cw[:, blk_ch, 0:1], scalar2=None, op0=ALU.mult), r=[b_ue[i], b_cw], w=[b_acc])
                    for j in range(1, 5):
                        k.op("dve", lambda e, j=j: e.scalar_tensor_tensor(out=acc[:, 0:ns, :], in0=u3[:, :, j:j + 128], scalar=cw[:, blk_ch, j:j + 1], in1=acc[:, 0:ns, :], op0=ALU.mult, op1=ALU.add), r=[b_ue[i], b_cw, b_acc], w=[b_acc])
                    k.op("act", lambda e: e.activation(out=dstT[:, 0:ns, :], in_=acc[:, 0:ns, :], func=AF.Silu, bias=cbc[:, 0, blk_ch:blk_ch + 1]), r=[b_acc, b_cbc], w=[b_dstT])

            return late

        def hT_fn(ns, par=0):
            return lambda kc: hTgL[par][:, kc, 0:ns * SW]

        def dt_matmuls(slots, par):
            ns = len(slots)
            for si in range(ns):
                for kc in range(KC):
                    k.op("pe", lambda e, kc=kc, si=si: e.matmul(bank[2][:, si * 64:(si + 1) * 64], lhsT=hTgL[par][:, kc, si * SW + 2:si * SW + 130], rhs=wdt[:, kc, :], start=(kc == 0), stop=(kc == KC - 1)), r=[b_hTg[par], b_wdt], w=[bkb[2]])
            k.op("dve", lambda e: e.tensor_copy(out=dtraw[:, par, 0:ns, :], in_=V(bank[2][:, 0:ns * 64], 64)), r=[bkb[2]], w=[b_dtraw[par]])

        def chain_dt_tasks(slots, par):
            ns = len(slots)
            T = []
            rng = range(ns)
            T.append(lambda: [k.op("dve", lambda e, si=si: e.tensor_tensor(out=dts[:, par, si, :], in0=dtraw[:, par, si, :], in1=hp[:, 0:64], op=ALU.add), r=[b_dtraw[par], b_hp], w=[b_dts[par][si]]) for si in rng])
            T.append(lambda: [k.op("act", lambda e, si=si: e.activation(out=dts[:, par, si, :], in_=dts[:, par, si, :], func=AF.Exp), r=[b_dts[par][si]], w=[b_dts[par][si]]) for si in rng])
            T.append(lambda: [k.op("act", lambda e, si=si: e.activation(out=dts[:, par, si, :], in_=dts[:, par, si, :], func=AF.Ln, bias=1.0), r=[b_dts[par][si]], w=[b_dts[par][si]]) for si in rng])
            T.append(lambda: [k.op("dve", lambda e, si=si: e.tensor_tensor(out=adt[:, par, si, :], in0=dts[:, par, si, :], in1=hp[:, 64:128], op=ALU.mult), r=[b_dts[par][si], b_hp], w=[b_adt[par][si]]) for si in rng])

            def emats():
                for si in rng:
                    pb = 2 + si % 2
                    for (c0, n, M, a0) in ((0, 32, Uin, 0), (32, 32, Lin, 32), (64, 32, Lst, 0), (96, 32, Ust, 32), (128, 64, ones, 0)):
                        k.op("pe", lambda e, c0=c0, n=n, M=M, a0=a0, si=si, pb=pb: e.matmul(bank[pb][:, c0:c0 + n], lhsT=M, rhs=adt[:, par, si, a0:a0 + n], start=True, stop=True), r=[b_cm, b_adt[par][si]], w=[bkb[pb]])
                    k.op("act", lambda e, si=si, pb=pb: e.activation(out=ex[:, par, si, :], in_=bank[pb][:, 0:192], func=AF.Exp), r=[bkb[pb]], w=[b_ex[par][si]])
            T.append(emats)
            T.append(lambda: [k.op("dve", lambda e, si=si: e.tensor_tensor(out=wv[:, par, si, :], in0=dts[:, par, si, :], in1=ex[:, par, si, 64:128], op=ALU.mult), r=[b_dts[par][si], b_ex[par][si]], w=[b_wv[par][si]]) for si in rng])
            return T

        def bc_hp(ap32, nh=32):
            return ap32.unsqueeze(2).broadcast_to([128, nh, 64])

        def slot_states(si, s, do_f, do_b, act_f_col=None, par=0):
            xs3 = V(xsL[par][:, si, :], 64)
            wv_ = wv[:, par, si, :]; ex_ = ex[:, par, si, :]
            bwv = b_wv[par][si]; bex = b_ex[par][si]; B_tok = BtL[par]; bBt = b_Bt[par]; bxs = b_xs[par]
            if do_f:
                k.op("dve", lambda e: e.tensor_tensor(out=V(xw[0][:], 64), in0=xs3, in1=bc_hp(wv_[:, 0:32]), op=ALU.mult), r=[bxs, bwv], w=[b_xw[0]])
                if act_f_col is not None:
                    k.op("dve", lambda e: e.tensor_scalar(out=etfa[:, 0:32], in0=ex_[:, 128:160], scalar1=act_f_col, scalar2=None, op0=ALU.mult), r=[bex, b_af], w=[b_etfa])
                    ecol = etfa[:, 0:32]
                else:
                    ecol = ex_[:, 128:160]
                for g in range(4):
                    pb = 4 + g % 2
                    k.op("pe", lambda e, g=g, pb=pb: e.matmul(bank[pb][:, :], lhsT=B_tok[:, si, g * 128:(g + 1) * 128], rhs=xw[0][:, g * 512:(g + 1) * 512], start=True, stop=True), r=[bBt, b_xw[0]], w=[bkb[pb]])
                    hg = V(Hf[:, g * 512:(g + 1) * 512], 64)
                    k.op("dve", lambda e, g=g, hg=hg: e.tensor_tensor(out=hg, in0=hg, in1=bc_hp(ecol[:, g * 8:(g + 1) * 8], 8), op=ALU.mult), r=[bex, b_etfa], w=[b_Hf])
                    if act_f_col is not None:
                        k.op("dve", lambda e, g=g, pb=pb: e.scalar_tensor_tensor(out=Hf[:, g * 512:(g + 1) * 512], in0=bank[pb][:, :], scalar=act_f_col, in1=Hf[:, g * 512:(g + 1) * 512], op0=ALU.mult, op1=ALU.add), r=[bkb[pb], b_af], w=[b_Hf])
                    else:
                        k.op("dve", lambda e, g=g, pb=pb: e.tensor_tensor(out=Hf[:, g * 512:(g + 1) * 512], in0=Hf[:, g * 512:(g + 1) * 512], in1=bank[pb][:, :], op=ALU.add), r=[bkb[pb]], w=[b_Hf])
            if do_b:
                k.op("dve", lambda e: e.tensor_tensor(out=V(xw[1][:], 64), in0=xs3, in1=bc_hp(wv_[:, 32:64]), op=ALU.mult), r=[bxs, bwv], w=[b_xw[1]])
                for g in range(4):
                    pb = 4 + g % 2
                    k.op("pe", lambda e, g=g, pb=pb: e.matmul(bank[pb][:, :], lhsT=B_tok[:, si, g * 128:(g + 1) * 128], rhs=xw[1][:, g * 512:(g + 1) * 512], start=True, stop=True), r=[bBt, b_xw[1]], w=[bkb[pb]])
                    k.op("act", lambda e, g=g, pb=pb: e.activation(out=sst[:, (g % 2) * 512:(g % 2 + 1) * 512], in_=bank[pb][:, :], func=AF.Identity), r=[bkb[pb]], w=[b_sst])
                    if g % 2 == 1:
                        k.dma("sp", sb_d[s][:, (g - 1) * 512:(g + 1) * 512], sst[:], r=[b_sst])
                k.dma("sp", eb_d[s], ex_[:, 160:192], r=[bex])

        gst1 = ExitStack()
        alloc_group(gst1, "a", True)
        bg_hook[0] = lambda: bg_step(4)

        def chain_all_tasks(G, par):
            T = chain_dt_tasks(G, par)
            for si, s in enumerate(G):
                T.append(lambda si=si, s=s: slot_states(si, s, do_f=(s >= 8), do_b=False, act_f_col=af_t[:, s:s + 1], par=par))
                T.append(lambda si=si, s=s: slot_states(si, s, do_f=False, do_b=True, par=par))
            return T

        for t_ in group_hT_tasks(GROUPS[0], 0):
            t_()
        for gi, G in enumerate(GROUPS):
            ns = len(G)
            par = gi % 2
            bg_drain()
            ta = group_hT_tasks(GROUPS[gi + 1], 1 - par) if gi + 1 < len(GROUPS) else []
            tb = chain_all_tasks(GROUPS[gi - 1], 1 - par) if gi >= 1 else []
            while ta or tb:
                if ta:
                    bgq.append(ta.pop(0))
                    if ta:
                        bgq.append(ta.pop(0))
                if tb:
                    bgq.append(tb.pop(0))
            xs_, bxs_, Bt_, bBt_ = xsL[par], b_xs[par], BtL[par], b_Bt[par]
            gemm(hT_fn(ns, par), ns * SW, KC, w_in, CX, 16, lambda blk, ps, pbuf, G=G, xs_=xs_, bxs_=bxs_: conv_epi(blk, ps, pbuf, G, xs_, bxs_, blk * 128), [b_hTg[par]], two_phase=True)
            gemm(hT_fn(ns, par), ns * SW, KC, w_in, CB, 4, lambda blk, ps, pbuf, G=G, Bt_=Bt_, bBt_=bBt_: conv_epi(16 + blk, ps, pbuf, G, Bt_, bBt_, blk * 128), [b_hTg[par]], two_phase=True)
            kvs = [(si, s) for si, s in enumerate(G) if s in KVSLOT]
            if kvs:
                def epi_k(blk, ps, pbuf, kvs=kvs):
                    for si, s in kvs:
                        ti = KVSLOT[s]
                        k.op("act", lambda e, si=si, ti=ti: e.activation(out=kT_all[:, blk, ti * 128:(ti + 1) * 128], in_=ps[:, si * SW + 2:si * SW + 130], func=AF.Identity), r=[pbuf], w=[b_kT])
                        rope_rows(ps, pbuf, si * SW + 2, ti * 128, 128, kT_all[0:32, blk, ti * 128:(ti + 1) * 128], b_kT)
                gemm(hT_fn(ns, par), ns * SW, KC, w_in, CK, 4, epi_k, [b_hTg[par]])

                def epi_v(blk, ps, pbuf, kvs=kvs, ns=ns):
                    k.op("act", lambda e: e.activation(out=vT[:, 0:ns * SW], in_=ps[:, 0:ns * SW], func=AF.Identity), r=[pbuf], w=[b_vT])
                    pvb = bank[3].bitcast(BF16)
                    for si, s in kvs:
                        k.op("pe", lambda e, si=si: e.transpose(out=pvb[:, si * 128:(si + 1) * 128], in_=vT[:, si * SW + 2:si * SW + 130], identity=identb), r=[b_vT, b_c16], w=[bkb[3]])
                    for si, s in kvs:
                        ti = KVSLOT[s]
                        k.op("dve", lambda e, si=si, ti=ti: e.tensor_copy(out=v_tok[:, ti, blk * 128:(blk + 1) * 128], in_=pvb[:, si * 128:(si + 1) * 128]), r=[bkb[3]], w=[b_v])
                gemm(hT_fn(ns, par), ns * SW, KC, w_in, CV, 4, epi_v, [b_hTg[par]])
            flush_epi()
            dt_matmuls(G, par)
        bg_drain()
        for t_ in chain_all_tasks(GROUPS[-1], (len(GROUPS) - 1) % 2):
            t_()
        bg_hook[0] = lambda: None
        if "hf" in dbg:
            k.dma("sp", dbg_out["hf"], Hf[:], r=[b_Hf], is_out=True)
        if "kv" in dbg:
            with ExitStack() as ps_:
                t1 = sb("dbgt1", [128, 4 * 1280], stack=ps_); t2 = sb("dbgt2", [128, 10 * 512], stack=ps_)
                bt1, bt2 = Buf("t1"), Buf("t2")
                k.op("dve", lambda e: e.tensor_copy(out=t1[:], in_=kT_all[:].rearrange("p a b -> p (a b)")), r=[b_kT], w=[bt1])
                k.op("dve", lambda e: e.tensor_copy(out=t2[:], in_=v_tok[:].rearrange("p a b -> p (a b)")), r=[b_v], w=[bt2])
                k.dma("sp", dbg_out["kT"], t1[:], r=[bt1], is_out=True)
                k.dma("sp", dbg_out["v"], t2[:], r=[bt2], is_out=True)
                k.barrier()

        gst1.close()
        k.barrier()
        with ExitStack() as ps_:
            hTa = sb("hTa", [128, KC, 256], BF16, stack=ps_); b_hTa = Buf("hTa")
            qTg = sb("qTg", [128, 16, 256], BF16, stack=ps_); b_qT = Buf("qTg")
            amb = sb("amb", [128, 3, 384], BF16, stack=ps_); b_amb = Buf("amb")
            k.dma("pool", amb[:], amask.rearrange("a p c -> p a c"), w=[b_amb])
            ao = [sb("ao%d" % i, [128, 2048], stack=ps_) for i in range(2)]; b_ao = [Buf("ao0"), Buf("ao1")]
            aon = sb("aon", [128, 2048], BF16, stack=ps_); b_aon = Buf("aon")
            Pm = [sb("Pm%d" % i, [128, 384], BF16, stack=ps_) for i in range(2)]; b_Pm = [Buf("Pm0"), Buf("Pm1")]
            PT = [sb("PT%d" % i, [128, 3, 128], BF16, stack=ps_) for i in range(2)]; b_PT = [Buf("PT0"), Buf("PT1")]
            st = [sb("ast%d" % i, [128, 8], stack=ps_) for i in range(2)]; b_st = [Buf("ast0"), Buf("ast1")]
            sq = sb("asq", [128, 20], stack=ps_); b_sq = Buf("asq")
            aosq = sb("aosq", [128, 2048], stack=ps_); b_aosq = Buf("aosq")
            pc = [0]
            for pr in range(4):
                for si in range(2):
                    s = pr * 2 + si
                    make_hT(xm[s * 128:(s + 1) * 128, :], 128, lambda kc, src, si=si: [(hTa[:, kc, si * 128:(si + 1) * 128], src)], b_hTa)

                def epi_q(blk, ps, pbuf, pr=pr):
                    k.op("act", lambda e: e.activation(out=qTg[:, blk, :], in_=ps[:, 0:256], func=AF.Identity), r=[pbuf], w=[b_qT])
                    rope_rows(ps, pbuf, 0, pr * 256, 256, qTg[0:32, blk, :], b_qT)
                gemm(lambda kc: hTa[:, kc, :], 256, KC, w_in, CQ, 16, epi_q, [b_hTa])
                flush_epi()
                items = [(si, hq) for si in range(2) for hq in range(16)]

                def geo(n):
                    si, hq = items[n]
                    c = pr * 2 + si
                    kvt = [9 if c == 0 else c - 1, c, c + 1]
                    mt = 0 if c == 0 else (2 if c == 7 else 1)
                    return si, hq, c, kvt, mt, hq // 4, n % 2

                def stA(n):
                    si, hq, c, kvt, mt, g, i = geo(n)
                    pS = 2 + i
                    k.op("pe", lambda e: e.matmul(bank[pS][:, 0:384], lhsT=identb, rhs=amb[:, mt, :], start=True, stop=False), r=[b_c16, b_amb], w=[bkb[pS]])
                    for kb in range(3):
                        k.op("pe", lambda e, kb=kb: e.matmul(bank[pS][:, kb * 128:(kb + 1) * 128], lhsT=qTg[:, hq, si * 128:(si + 1) * 128], rhs=kT_all[:, g, kvt[kb] * 128:(kvt[kb] + 1) * 128], start=False, stop=(kb == 2)), r=[b_qT, b_kT], w=[bkb[pS]])

                def stB(n):
                    si, hq, c, kvt, mt, g, i = geo(n)
                    pS = 2 + i
                    s_ = st[i]; bs_ = b_st[i]
                    k.op("dve", lambda e: e.reduce_max(out=s_[:, 0:1], in_=bank[pS][:, 0:384], axis=mybir.AxisListType.X), r=[bkb[pS]], w=[bs_])
                    k.op("dve", lambda e: e.tensor_scalar(out=s_[:, 1:2], in0=s_[:, 0:1], scalar1=-SCALE, scalar2=None, op0=ALU.mult), r=[bs_], w=[bs_])
                    k.op("dve", lambda e: e.tensor_scalar(out=s_[:, 1:2], in0=s_[:, 1:2], scalar1=nsink[:, hq:hq + 1], scalar2=None, op0=ALU.min), r=[bs_, b_nsink], w=[bs_])
                    k.op("act", lambda e: e.activation(out=Pm[i][:], in_=bank[pS][:, 0:384], func=AF.Exp, scale=SCALE, bias=s_[:, 1:2]), r=[bkb[pS], bs_], w=[b_Pm[i]])
                    k.op("act", lambda e: e.activation(out=s_[:, 3:4], in_=s_[:, 1:2], func=AF.Exp, bias=sink_t[:, hq:hq + 1]), r=[bs_, b_sink], w=[bs_])
                    k.op("dve", lambda e: e.reduce_sum(out=s_[:, 2:3], in_=Pm[i][:], axis=mybir.AxisListType.X), r=[b_Pm[i]], w=[bs_])
                    k.op("dve", lambda e: e.tensor_tensor(out=s_[:, 4:5], in0=s_[:, 2:3], in1=s_[:, 3:4], op=ALU.add), r=[bs_], w=[bs_])
                    k.op("dve", lambda e: e.reciprocal(out=s_[:, 5:6], in_=s_[:, 4:5]), r=[bs_], w=[bs_])

                def stC(n):
                    si, hq, c, kvt, mt, g, i = geo(n)
                    pT = 4 + i
                    pvb = bank[pT].bitcast(BF16)
                    for kb in range(3):
                        k.op("pe", lambda e, kb=kb: e.transpose(out=pvb[:, kb * 128:(kb + 1) * 128], in_=Pm[i][:, kb * 128:(kb + 1) * 128], identity=identb), r=[b_Pm[i], b_c16], w=[bkb[pT]])
                    k.op("act", lambda e: e.activation(out=PT[i][:].rearrange("p a b -> p (a b)"), in_=pvb[:, 0:384], func=AF.Identity), r=[bkb[pT]], w=[b_PT[i]])

                def stD(n):
                    si, hq, c, kvt, mt, g, i = geo(n)
                    pO = 6 + i
                    for kb in range(3):
                        k.op("pe", lambda e, kb=kb: e.matmul(bank[pO][:, 0:128], lhsT=PT[i][:, kb, :], rhs=v_tok[:, kvt[kb], g * 128:(g + 1) * 128], start=(kb == 0), stop=(kb == 2)), r=[b_PT[i], b_v], w=[bkb[pO]])
                    k.op("dve", lambda e: e.tensor_scalar(out=ao[si][:, hq * 128:(hq + 1) * 128], in0=bank[pO][:, 0:128], scalar1=st[i][:, 5:6], scalar2=None, op0=ALU.mult), r=[bkb[pO], b_st[i]], w=[b_ao[si]])
                    if hq == 15:
                        k.op("dve", lambda e: e.tensor_tensor(out=aosq[:], in0=ao[si][:], in1=ao[si][:], op=ALU.mult), r=[b_ao[si]], w=[b_aosq])
                        k.op("dve", lambda e: e.reduce_sum(out=sq[:, 16:17], in_=aosq[:], axis=mybir.AxisListType.X), r=[b_aosq], w=[b_sq])
                        k.op("dve", lambda e: e.tensor_scalar(out=sq[:, 17:18], in0=sq[:, 16:17], scalar1=1.0 / 2048, scalar2=RMS_EPS, op0=ALU.mult, op1=ALU.add), r=[b_sq], w=[b_sq])
                        k.op("act", lambda e: e.activation(out=sq[:, 18:19], in_=sq[:, 17:18], func=AF.Ln), r=[b_sq], w=[b_sq])
                        k.op("act", lambda e: e.activation(out=sq[:, 18:19], in_=sq[:, 18:19], func=AF.Exp, scale=-0.5), r=[b_sq], w=[b_sq])
                        k.op("dve", lambda e: e.tensor_scalar(out=aon[:], in0=ao[si][:], scalar1=sq[:, 18:19], scalar2=None, op0=ALU.mult), r=[b_ao[si], b_sq], w=[b_aon])
                        k.dma("sp", mix_d[c * 128:(c + 1) * 128, 0:2048], aon[:], r=[b_aon])

                stA(0)
                for n in range(len(items)):
                    if n + 1 < len(items):
                        stA(n + 1)
                    stB(n)
                    if not nomod:
                        ada_unit(64 + pr * 32 + n)
                        flush_epi()
                    stC(n)
                    stD(n)
            k.barrier()

        ast_.close()
        k.barrier()
        k.barrier()
        with ExitStack() as ps_:
            Hb = sb("Hb", [128, 2048], stack=ps_); b_Hb = Buf("Hb")
            sbt = [sb("sbt%d" % i, [128, 2048], stack=ps_) for i in range(2)]; b_sbt = [Buf("sbt0"), Buf("sbt1")]
            ebt = [sb("ebt%d" % i, [128, 32], stack=ps_) for i in range(2)]; b_ebt = [Buf("ebt0"), Buf("ebt1")]
            hsv = [sb("hsv%d" % i, [128, 2048], BF16, stack=ps_) for i in range(2)]; b_hsv = [Buf("hsv0"), Buf("hsv1")]
            k.op("dve", lambda e: e.memset(Hb[:], 0.0), w=[b_Hb])
            for n_, s in enumerate(range(31, -1, -1)):
                i = n_ % 2
                k.dma("sp", sbt[i][:], sb_d[s], w=[b_sbt[i]])
                k.dma("sp", ebt[i][:], eb_d[s], w=[b_ebt[i]])
                if s <= 7:
                    k.op("act", lambda e, i=i: e.activation(out=hsv[i][:], in_=Hb[:], func=AF.Identity), r=[b_Hb], w=[b_hsv[i]])
                    k.dma("sp", hb_d[s], hsv[i][:], r=[b_hsv[i]])
                k.op("dve", lambda e, i=i, s=s: e.tensor_scalar(out=ebt[i][:], in0=ebt[i][:], scalar1=ab_t[:, s:s + 1], scalar2=None, op0=ALU.mult), r=[b_ab, b_ebt[i]], w=[b_ebt[i]])
                k.op("dve", lambda e, i=i: e.tensor_tensor(out=V(Hb[:], 64), in0=V(Hb[:], 64), in1=bc_hp(ebt[i][:]), op=ALU.mult), r=[b_ebt[i], b_Hb], w=[b_Hb])
                k.op("dve", lambda e, i=i, s=s: e.scalar_tensor_tensor(out=Hb[:], in0=sbt[i][:], scalar=ab_t[:, s:s + 1], in1=Hb[:], op0=ALU.mult, op1=ALU.add), r=[b_sbt[i], b_ab, b_Hb], w=[b_Hb])
            k.barrier()
        if "hb" in dbg:
            k.dma("sp", dbg_out["hb"], hb_d, is_out=True)

        gst2 = ExitStack()
        alloc_group(gst2, "b", False)
        with ExitStack() as ps_:
            BT = sb("BT", [128, 4, 3, 128], BF16, stack=ps_); b_BT = Buf("BT")
            CT = sb("CT", [128, 4, 3, 128], BF16, stack=ps_); b_CT = Buf("CT")
            cacc = sb("cacc", [128, 3, 128], stack=ps_); b_cacc = Buf("cacc")
            gz = sb("gz", [128, 3, 2048], BF16, stack=ps_); b_gz = Buf("gz")
            zT = sb("zT", [128, 3 * SW], BF16, stack=ps_); b_zT = Buf("zT")
            R = [sb("R%d" % i, [128, 16, 128], stack=ps_) for i in range(2)]; b_R = [Buf("R0"), Buf("R1")]
            dec = [sb("dec%d" % i, [128, 512], stack=ps_) for i in range(2)]; b_dec = [Buf("dec0"), Buf("dec1")]
            Mt = [sb("Mt%d" % i, [128, 4, 128], BF16, stack=ps_) for i in range(2)]; b_Mt = [Buf("Mt0"), Buf("Mt1")]
            cbm = [sb("cbm%d" % i, [128, 4, 128], stack=ps_) for i in range(2)]; b_cbm = [Buf("cbF"), Buf("cbB")]
            xdt = [sb("xdt%d" % i, [128, 2048], BF16, stack=ps_) for i in range(2)]; b_xdt = [Buf("xdtf"), Buf("xdtb")]
            hin = [sb("hin%d" % i, [128, 2048], BF16, stack=ps_) for i in range(2)]; b_hin = [Buf("hinf"), Buf("hinb")]
            yacc = sb("yacc", [128, 2048], stack=ps_); b_yacc = Buf("yacc")
            ytmp = sb("ytmp", [128, 512], stack=ps_); b_ytmp = Buf("ytmp")
            ysq = sb("ysq", [128, 8], stack=ps_); b_ysq = Buf("ysq")
            yn = sb("yn", [128, 2048], BF16, stack=ps_); b_yn = Buf("yn")
            xs_tok = xsL[0]; B_tok = BtL[0]; b_xs0 = b_xs[0]; b_Bt0 = b_Bt[0]; b_hTg0 = b_hTg[0]
            for G in OWNG:
                ns = len(G)
                for t_ in group_hT_tasks(G, 0):
                    t_()
                gemm(hT_fn(ns), ns * SW, KC, w_in, CX, 16, lambda blk, ps, pbuf, G=G: conv_epi(blk, ps, pbuf, G, xs_tok, b_xs0, blk * 128), [b_hTg0], two_phase=True)
                gemm(hT_fn(ns), ns * SW, KC, w_in, CB, 4, lambda blk, ps, pbuf, G=G: conv_epi(16 + blk, ps, pbuf, G, B_tok, b_Bt0, blk * 128, feat=(BT[:, blk, :, :], b_BT, cacc, b_cacc)), [b_hTg0], two_phase=True)
                gemm(hT_fn(ns), ns * SW, KC, w_in, CC, 4, lambda blk, ps, pbuf, G=G: conv_epi(20 + blk, ps, pbuf, G, None, None, 0, feat=(CT[:, blk, :, :], b_CT, cacc, b_cacc)), [b_hTg0], two_phase=True)

                def epi_z(blk, ps, pbuf, ns=ns):
                    k.op("act", lambda e: e.activation(out=zT[:, 0:ns * SW], in_=ps[:, 0:ns * SW], func=AF.Silu), r=[pbuf], w=[b_zT])
                    pvb = bank[3].bitcast(BF16)
                    for si in range(ns):
                        k.op("pe", lambda e, si=si: e.transpose(out=pvb[:, si * 128:(si + 1) * 128], in_=zT[:, si * SW + 2:si * SW + 130], identity=identb), r=[b_zT, b_c16], w=[bkb[3]])
                    k.op("dve", lambda e: e.tensor_copy(out=gz[:, 0:ns, blk * 128:(blk + 1) * 128], in_=V(pvb[:, 0:ns * 128], 128)), r=[bkb[3]], w=[b_gz])
                gemm(hT_fn(ns), ns * SW, KC, w_in, CZ, 16, epi_z, [b_hTg0])
                flush_epi()
                dt_matmuls(G, 0)
                for t_ in chain_dt_tasks(G, 0):
                    t_()
                for si, s in enumerate(G):
                    dts_ = dts[:, 0, si, :]; adt_ = adt[:, 0, si, :]; ex_ = ex[:, 0, si, :]
                    bdts_ = b_dts[0][si]; badt_ = b_adt[0][si]; bex_ = b_ex[0][si]
                    xs3 = V(xs_tok[:, si, :], 64)
                    if s == 0:
                        dump("dts", dts_, [bdts_]); dump("ex", ex_, [bex_]); dump("xs", xs_tok[:, 0, :], [b_xs0], BF16)
                        dump("Btok", B_tok[:, 0, :], [b_Bt0], BF16); dump("BT", BT[:, :, 0, :], [b_BT], BF16); dump("CT", CT[:, :, 0, :], [b_CT], BF16)
                        dump("gz", gz[:, 0, :], [b_gz], BF16); dump("hf", Hf[:], [b_Hf])
                    k.dma("sp", hin[1][:], hb_d[s], w=[b_hin[1]])
                    k.op("act", lambda e: e.activation(out=hin[0][:], in_=Hf[:], func=AF.Identity), r=[b_Hf], w=[b_hin[0]])
                    for d in range(2):
                        k.op("dve", lambda e, d=d: e.tensor_tensor(out=V(xdt[d][:], 64), in0=xs3, in1=bc_hp(dts_[:, d * 32:(d + 1) * 32]), op=ALU.mult), r=[b_xs0, bdts_], w=[b_xdt[d]])
                    for g in range(4):
                        k.op("pe", lambda e, g=g: e.matmul(bank[3][:, g * 128:(g + 1) * 128], lhsT=BT[:, g, si, :], rhs=CT[:, g, si, :], start=True, stop=True), r=[b_BT, b_CT], w=[bkb[3]])
                    for d, M in ((0, Uin), (1, Lin)):
                        k.op("dve", lambda e, d=d, M=M: e.tensor_tensor(out=cbm[d][:], in0=V(bank[3][:, :], 128), in1=M.unsqueeze(1).broadcast_to([128, 4, 128]), op=ALU.mult), r=[bkb[3], b_cm], w=[b_cbm[d]])
                    for hh in range(2):
                        for d, M in ((0, Uin), (1, Lin)):
                            k.op("dve", lambda e, d=d, M=M: e.tensor_tensor(out=R[d][:], in0=M.unsqueeze(1).broadcast_to([128, 16, 128]), in1=adt_[:, d * 32 + hh * 16:d * 32 + hh * 16 + 16].unsqueeze(2).broadcast_to([128, 16, 128]), op=ALU.mult), r=[b_cm, badt_], w=[b_R[d]])
                        for q4 in range(4):
                            h0 = hh * 16 + q4 * 4
                            g = h0 // 8
                            for d, M2 in ((0, Lst), (1, Ust)):
                                pb = 2 + d
                                k.op("pe", lambda e, d=d, M2=M2, pb=pb: e.matmul(bank[pb][:, :], lhsT=M2, rhs=R[d][:, q4 * 4:(q4 + 1) * 4, :].rearrange("p a b -> p (a b)"), start=True, stop=True), r=[b_cm, b_R[d]], w=[bkb[pb]])
                                k.op("act", lambda e, d=d, pb=pb: e.activation(out=dec[d][:], in_=bank[pb][:, :], func=AF.Exp), r=[bkb[pb]], w=[b_dec[d]])
                                k.op("dve", lambda e, d=d, g=g: e.tensor_tensor(out=Mt[d][:], in0=V(dec[d][:], 128), in1=cbm[d][:, g, :].unsqueeze(1).broadcast_to([128, 4, 128]), op=ALU.mult), r=[b_dec[d], b_cbm[d]], w=[b_Mt[d]])
                            for i4 in range(4):
                                h = h0 + i4
                                pb = 4 + ((h % 16) // 8)
                                o = bank[pb][:, (h % 8) * 64:(h % 8) * 64 + 64]
                                k.op("pe", lambda e, o=o, i4=i4, h=h: e.matmul(o, lhsT=Mt[0][:, i4, :], rhs=xdt[0][:, h * 64:(h + 1) * 64], start=True, stop=False), r=[b_Mt[0], b_xdt[0]], w=[bkb[pb]])
                                k.op("pe", lambda e, o=o, i4=i4, h=h: e.matmul(o, lhsT=Mt[1][:, i4, :], rhs=xdt[1][:, h * 64:(h + 1) * 64], start=False, stop=True), r=[b_Mt[1], b_xdt[1]], w=[bkb[pb]])
                        for gg in range(2):
                            g = hh * 2 + gg
                            ys = yacc[:, g * 512:(g + 1) * 512]
                            k.op("dve", lambda e, g=g, ys=ys: e.tensor_tensor(out=V(ys, 64), in0=V(xs_tok[:, si, g * 512:(g + 1) * 512], 64), in1=bc_hp(hp[:, 128 + g * 8:128 + g * 8 + 8], 8), op=ALU.mult), r=[b_xs0, b_hp], w=[b_yacc])
                            k.op("dve", lambda e, gg=gg, ys=ys: e.tensor_tensor(out=ys, in0=ys, in1=bank[4 + gg][:, :], op=ALU.add), r=[bkb[4 + gg]], w=[b_yacc])
                            if s == 0:
                                dump("yd%d" % g, ys, [b_yacc])
                            for d in range(2):
                                pb = 2 + d
                                k.op("pe", lambda e, d=d, g=g, pb=pb: e.matmul(bank[pb][:, :], lhsT=CT[:, g, si, :], rhs=hin[d][:, g * 512:(g + 1) * 512], start=True, stop=True), r=[b_CT, b_hin[d]], w=[bkb[pb]])
                                k.op("dve", lambda e, d=d, g=g, pb=pb: e.tensor_tensor(out=V(ytmp[:], 64), in0=V(bank[pb][:, :], 64), in1=bc_hp(ex_[:, d * 32 + g * 8:d * 32 + g * 8 + 8], 8), op=ALU.mult), r=[bkb[pb], bex_], w=[b_ytmp])
                                k.op("dve", lambda e, ys=ys: e.tensor_tensor(out=ys, in0=ys, in1=ytmp[:], op=ALU.add), r=[b_ytmp], w=[b_yacc])
                    if s == 0:
                        dump("ypre", yacc[:], [b_yacc]); dump("hinb", hin[1][:], [b_hin[1]], BF16)
                    k.op("dve", lambda e: e.tensor_tensor(out=yacc[:], in0=yacc[:], in1=gz[:, si, :], op=ALU.mult), r=[b_gz], w=[b_yacc])
                    rsc = R[0][:].rearrange("p a b -> p (a b)")
                    k.op("dve", lambda e: e.tensor_tensor(out=rsc, in0=yacc[:], in1=yacc[:], op=ALU.mult), r=[b_yacc], w=[b_R[0]])
                    k.op("dve", lambda e: e.reduce_sum(out=ysq[:, 0:4], in_=V(rsc, 512), axis=mybir.AxisListType.X), r=[b_R[0]], w=[b_ysq])
                    k.op("dve", lambda e: e.tensor_scalar(out=ysq[:, 4:8], in0=ysq[:, 0:4], scalar1=1.0 / 512, scalar2=RMS_EPS, op0=ALU.mult, op1=ALU.add), r=[b_ysq], w=[b_ysq])
                    k.op("act", lambda e: e.activation(out=ysq[:, 4:8], in_=ysq[:, 4:8], func=AF.Ln), r=[b_ysq], w=[b_ysq])
                    k.op("act", lambda e: e.activation(out=ysq[:, 4:8], in_=ysq[:, 4:8], func=AF.Exp, scale=-0.5), r=[b_ysq], w=[b_ysq])
                    k.op("dve", lambda e: e.tensor_tensor(out=V(yn[:], 512), in0=V(yacc[:], 512), in1=ysq[:, 4:8].unsqueeze(2).broadcast_to([128, 4, 512]), op=ALU.mult), r=[b_yacc, b_ysq], w=[b_yn])
                    k.dma("sp", mix_d[s * 128:(s + 1) * 128, 2048:4096], yn[:], r=[b_yn])
                    slot_states(si, s, do_f=True, do_b=False, act_f_col=None)
            k.barrier()
        gst2.close()
        k.barrier()
        with ExitStack() as ps_:
            stg_ = sb("mcB_s", [64, 128], stack=ps_); bs_ = Buf("mcB_s")
            k.dma("sp", stg_[:], mod_d[3 * D:5 * D].rearrange("(a b) -> a b", b=128), w=[bs_])
            k.op("pe", lambda e: e.transpose(out=bank[2][:, 0:64], in_=stg_[:], identity=ident[0:64, 0:64]), r=[bs_, b_cm], w=[bkb[2]])
            k.op("dve", lambda e: e.tensor_copy(out=mB[:, 0:64], in_=bank[2][:, 0:64]), r=[bkb[2]], w=[b_mcB])
            k.op("dve", lambda e: e.tensor_scalar(out=mB[:, 32:64], in0=mB[:, 32:64], scalar1=1.0, scalar2=None, op0=ALU.add), r=[b_mcB], w=[b_mcB])
            if "mod" in dbg:
                k.dma("sp", dbg_out["mod"], mod_d, is_out=True)
            k.barrier()
        if "mix" in dbg:
            k.dma("sp", dbg_out["mix"], mix_d, is_out=True)

        if full:
            def store_T(ps, pbuf, stg, b_stg, ost, b_ost, dst_d, half, blk):
                k.op("act", lambda e: e.activation(out=stg[:], in_=ps[:, 0:512], func=AF.Identity), r=[pbuf], w=[b_stg])
                for tt in range(4):
                    k.op("pe", lambda e, tt=tt: e.transpose(out=bank[3][:, tt * 128:(tt + 1) * 128], in_=stg[:, tt * 128:(tt + 1) * 128], identity=ident), r=[b_stg, b_cm], w=[bkb[3]])
                k.op("dve", lambda e: e.tensor_copy(out=ost[:].rearrange("p a b -> p (a b)"), in_=bank[3][:, :]), r=[bkb[3]], w=[b_ost])
                k.dma("sp", dst_d[half * 512:(half + 1) * 512, blk * 128:(blk + 1) * 128].rearrange("(t p) c -> p t c", p=128), ost[:], r=[b_ost])

            with ExitStack() as ps_:
                mixT = sb("mixT", [128, KC, 1024], BF16, stack=ps_); b_mixT = Buf("mixT")
                mt_ = sb("mixtile", [128, D], BF16, stack=ps_); b_mt = Buf("mixtile")
                stg = sb("ostg", [128, 512], stack=ps_); b_stg = Buf("ostg")
                ost = sb("oost", [128, 4, 128], stack=ps_); b_ost = Buf("oost")
                for t in range(8):
                    k.dma("sp", mt_[:], mix_d[t * 128:(t + 1) * 128, :], w=[b_mt])
                    for k0 in range(0, KC, 8):
                        pb = 4 + (k0 // 8) % 2
                        pvb = bank[pb].bitcast(BF16)
                        for j in range(8):
                            kc = k0 + j
                            k.op("pe", lambda e, kc=kc, j=j, pvb=pvb: e.transpose(out=pvb[:, j * 128:(j + 1) * 128], in_=mt_[:, kc * 128:(kc + 1) * 128], identity=identb), r=[b_mt, b_c16], w=[bkb[pb]])
                        for j in range(8):
                            kc = k0 + j
                            nwc = anw[:, 0, kc:kc + 1] if kc < 16 else snw[:, 0, kc - 16:kc - 15]
                            k.op("act", lambda e, kc=kc, j=j, pvb=pvb, nwc=nwc: e.activation(out=mixT[:, kc, t * 128:(t + 1) * 128], in_=pvb[:, j * 128:(j + 1) * 128], func=AF.Identity, scale=nwc), r=[bkb[pb], b_anw, b_snw], w=[b_mixT])
                for half in range(2):
                    gemm(lambda kc, half=half: mixT[:, kc, half * 512:(half + 1) * 512], 512, KC, w_out, 0, 32,
                         lambda blk, ps, pbuf, half=half: store_T(ps, pbuf, stg, b_stg, ost, b_ost, mixr_d, half, blk), [b_mixT])
                flush_epi()
                k.barrier()
            if "mixr" in dbg:
                k.dma("sp", dbg_out["mixr"], mixr_d, is_out=True)

            def ln_phase(stack, tiles, br_d, res_fn, rows3, emit):
                rows = sb("rows", [8, D], stack=stack); b_rows = Buf("rows")
                sel = sb("selr", [8, 8, 128], stack=stack); b_sel = Buf("sel")
                k.dma("sp", sel[:].rearrange("p a b -> p (a b)"), sel_d[:, :], w=[b_sel])
                k.op("dve", lambda e: e.memset(rows[:], 0.0), w=[b_rows])
                k.dma("sp", rows[0:1, :], mod_d[2 * D:3 * D].rearrange("(a b) -> a b", a=1), w=[b_rows])
                k.dma("sp", rows[1:2, :], mod_d[5 * D:6 * D].rearrange("(a b) -> a b", a=1), w=[b_rows])
                for r_, src in ((2, ln1_g), (3, ln1_b), (4, ln2_g), (5, ln2_b)):
                    k.dma("sp", rows[r_:r_ + 1, :], src.rearrange("(a b) -> a b", a=1), w=[b_rows])
                bc3 = sb("bc3", [128, 3, D], stack=stack); b_bc3 = Buf("bc3")
                rt_ = sb("lnr", [128, D], stack=stack); b_rt = Buf("lnr")
                bst = sb("bst", [128, 8, 6], stack=stack); b_bst = Buf("bst")
                mv = sb("mv", [128, 4], stack=stack); b_mv = Buf("mv")
                for j, r_ in enumerate(rows3):
                    for n in range(8):
                        pb = 2 + n % 2
                        k.op("pe", lambda e, r_=r_, n=n, pb=pb: e.matmul(bank[pb][:, :], lhsT=sel[:, r_, :], rhs=rows[:, n * 512:(n + 1) * 512], start=True, stop=True), r=[b_sel, b_rows], w=[bkb[pb]])
                        k.op("act", lambda e, j=j, n=n, pb=pb: e.activation(out=bc3[:, j, n * 512:(n + 1) * 512], in_=bank[pb][:, :], func=AF.Identity), r=[bkb[pb]], w=[b_bc3])
                for t in tiles:
                    k.dma("sp", rt_[:], br_d[t * 128:(t + 1) * 128, :], w=[b_rt])
                    res_fn(t)
                    k.op("dve", lambda e: e.tensor_tensor(out=rt_[:], in0=rt_[:], in1=bc3[:, 0, :], op=ALU.mult), r=[b_bc3], w=[b_rt])
                    k.op("dve", lambda e: e.scalar_tensor_tensor(out=rt_[:], in0=xt[:], scalar=ALPHA, in1=rt_[:], op0=ALU.mult, op1=ALU.add), r=[b_xt], w=[b_rt])
                    for n in range(8):
                        k.op("dve", lambda e, n=n: e.bn_stats(out=bst[:, n, :], in_=rt_[:, n * 512:(n + 1) * 512]), r=[b_rt], w=[b_bst])
                    k.op("dve", lambda e: e.bn_aggr(out=mv[:, 0:2], in_=bst[:].rearrange("p a b -> p (a b)")), r=[b_bst], w=[b_mv])
                    k.op("dve", lambda e: e.tensor_scalar(out=mv[:, 2:3], in0=mv[:, 1:2], scalar1=LN_EPS, scalar2=None, op0=ALU.add), r=[b_mv], w=[b_mv])
                    k.op("act", lambda e: e.activation(out=mv[:, 2:3], in_=mv[:, 2:3], func=AF.Ln), r=[b_mv], w=[b_mv])
                    k.op("act", lambda e: e.activation(out=mv[:, 2:3], in_=mv[:, 2:3], func=AF.Exp, scale=-0.5), r=[b_mv], w=[b_mv])
                    k.op("dve", lambda e: e.tensor_scalar(out=rt_[:], in0=rt_[:], scalar1=mv[:, 0:1], scalar2=mv[:, 2:3], op0=ALU.subtract, op1=ALU.mult), r=[b_mv], w=[b_rt])
                    k.op("dve", lambda e: e.tensor_tensor(out=rt_[:], in0=rt_[:], in1=bc3[:, 1, :], op=ALU.mult), r=[b_bc3], w=[b_rt])
                    k.op("dve", lambda e: e.tensor_tensor(out=rt_[:], in0=rt_[:], in1=bc3[:, 2, :], op=ALU.add), r=[b_bc3], w=[b_rt])
                    emit(t, rt_, b_rt)

            for half in range(2):
                h2s = ExitStack()
                h2T = sb("h2T", [128, KC, 512], BF16, stack=h2s); b_h2T = Buf("h2T")
                with ExitStack() as ps_:
                    def res1(t):
                        k.dma("sp", xt[:], xm[t * 128:(t + 1) * 128, :], w=[b_xt])

                    def emit1(t, rt_, b_rt, half=half):
                        k.dma("sp", x1_d[t * 128:(t + 1) * 128, :], rt_[:], r=[b_rt])
                        tl = t - half * 4
                        transp_mod(rt_, b_rt, 128, lambda kc, src, tl=tl: [(h2T[:, kc, tl * 128:(tl + 1) * 128], src)], b_h2T, mB, b_mcB)
                    ln_phase(ps_, range(half * 4, half * 4 + 4), mixr_d, res1, (0, 2, 3), emit1)
                    k.barrier()
                with ExitStack() as ps_:
                    actT = sb("actT", [128, KCF, 512], BF16, stack=ps_); b_actT = Buf("actT")
                    sg = [sb("sg%d" % i, [128, 512], stack=ps_) for i in range(2)]; b_sg = [Buf("sg0"), Buf("sg1")]
                    stg = sb("fstg", [128, 512], stack=ps_); b_stg = Buf("fstg")
                    ost = sb("fost", [128, 4, 128], stack=ps_); b_ost = Buf("fost")
                    for blk in range(KCF):
                        i = blk % 2
                        gemm(lambda kc: h2T[:, kc, :], 512, KC, w_gate, blk * 128, 1,
                             lambda b_, ps, pbuf, i=i: k.op("act", lambda e: e.activation(out=sg[i][:], in_=ps[:, 0:512], func=AF.Silu), r=[pbuf], w=[b_sg[i]]), [b_h2T])
                        gemm(lambda kc: h2T[:, kc, :], 512, KC, w_up, blk * 128, 1,
                             lambda b_, ps, pbuf, i=i, blk=blk: k.op("dve", lambda e: e.tensor_tensor(out=actT[:, blk, :], in0=sg[i][:], in1=ps[:, 0:512], op=ALU.mult), r=[pbuf, b_sg[i]], w=[b_actT]), [b_h2T])
                    gemm(lambda kc: actT[:, kc, :], 512, KCF, w_down, 0, 32,
                         lambda blk, ps, pbuf, half=half: store_T(ps, pbuf, stg, b_stg, ost, b_ost, ffn_d, half, blk), [b_actT], units=3)
                    flush_epi()
                    k.barrier()
                h2s.close()
                k.barrier()
            if "x1" in dbg:
                k.dma("sp", dbg_out["x1"], x1_d, is_out=True)
            if "ffn" in dbg:
                k.dma("sp", dbg_out["ffn"], ffn_d, is_out=True)

            with ExitStack() as ps_:
                def res2(t):
                    k.dma("sp", xt[:], x1_d[t * 128:(t + 1) * 128, :], w=[b_xt])

                def emit2(t, rt_, b_rt):
                    k.dma("sp", y_out[t * 128:(t + 1) * 128, :], rt_[:], r=[b_rt], is_out=True)
                ln_phase(ps_, range(8), ffn_d, res2, (1, 4, 5), emit2)
                k.barrier()
        k.finish()
        print("instructions emitted:", k.ninst, flush=True)
    nc._in_names = in_names
    return nc


def _consts():
    t = np.arange(128)
    ident = np.eye(128, dtype=np.float32)
    Uin = (t[:, None] <= t[None, :]).astype(np.float32)
    Lin = (t[:, None] >= t[None, :]).astype(np.float32)
    Ust = (t[:, None] < t[None, :]).astype(np.float32)
    Lst = (t[:, None] > t[None, :]).astype(np.float32)
    ones = np.ones((128, 128), np.float32)
    cmat = np.concatenate([ident, Uin, Lin, Ust, Lst, ones], axis=1)
    sel = np.zeros((8, 8, 128), np.float32)
    for r in range(8):
        sel[r, r, :] = 1.0
    invf = (500000.0 ** (-np.arange(0, 32, 2, dtype=np.float32) / 32)).astype(np.float32)
    ropec = np.zeros((32, 2), np.float32)
    ropec[:, 0] = np.concatenate([invf, invf])
    ropec[:16, 1] = -1.0
    ropec[16:, 1] = 1.0
    psw = np.zeros((32, 32), np.float32)
    for m in range(32):
        psw[(m + 16) % 32, m] = 1.0
    return cmat, sel.reshape(8, 1024), ropec, psw


def prep_inputs(inputs):
    x = np.asarray(inputs["x"], np.float32)
    c = np.asarray(inputs["c"], np.float32)
    pos = np.asarray(inputs["positions"]).astype(np.int32)
    cmat, sel, ropec, psw = _consts()
    shared = {"cmat": cmat, "sel": sel, "ropec": ropec, "pswap": psw}
    for nm in ("w_ada", "w_in", "conv_w", "w_out", "w_gate", "w_up", "w_down"):
        shared[nm] = np.ascontiguousarray(np.asarray(inputs[nm], np.float32)[0])
    for nm in ("b_ada", "conv_b", "attn_sink", "a_log_fwd", "a_log_bwd", "dt_bias_fwd", "dt_bias_bwd",
               "ssd_d", "ssd_norm_w", "attn_norm_w", "ln1_g", "ln1_b", "ln2_g", "ln2_b"):
        shared[nm] = np.ascontiguousarray(np.asarray(inputs[nm], np.float32)[0])
    NEG = -30000.0
    qi = np.arange(128)[:, None]
    kj = np.arange(128)[None, :]
    in_maps = []
    for core in range(NCORES):
        b, j = core // 4, core % 4
        m = dict(shared)
        m["xm"] = np.ascontiguousarray(np.roll(x[b], -1024 * j, axis=0))
        xh = np.zeros((NSLOT * 4, D), np.float32)
        hmk = np.zeros((NSLOT, 4), np.float32)
        for s in range(NSLOT):
            t0 = ((8 * j + s) % 32) * 128
            for ii, tt in enumerate((t0 - 2, t0 - 1, t0 + 128, t0 + 129)):
                if 0 <= tt < 4096:
                    xh[4 * s + ii] = x[b, tt]
                    hmk[s, ii] = 1.0
        m["xh"] = xh
        m["hmask"] = np.ascontiguousarray(np.broadcast_to(hmk.reshape(1, -1), (128, NSLOT * 4)))
        pr = np.zeros((10, 128), np.int32)
        for ti, s in ((0, 0), (1, 1), (2, 2), (3, 3), (4, 4), (5, 5), (6, 6), (7, 7), (8, 8), (9, 31)):
            ch = (8 * j + s) % 32
            pr[ti] = pos[b, ch * 128:(ch + 1) * 128]
        m["posr"] = pr.reshape(-1)
        am = np.zeros((3, 128, 384), np.float32)
        prev = np.where(kj >= qi, 0.0, NEG)
        nxt = np.where(kj <= qi, 0.0, NEG)
        for ty in range(3):
            am[ty, :, 0:128] = prev
            am[ty, :, 256:384] = nxt
        if j == 0:
            am[0, :, 0:128] = NEG
        if j == 3:
            am[2, :, 256:384] = NEG
        m["amask"] = am
        af = np.zeros(NSLOT, np.float32)
        ab = np.zeros(NSLOT, np.float32)
        for s in range(NSLOT):
            if s >= 8 and s >= 32 - 8 * j:
                af[s] = 1.0
            if s < 8 or (8 <= s <= 31 - 8 * j):
                ab[s] = 1.0
        m["actf"] = np.ascontiguousarray(np.broadcast_to(af[None, :], (128, NSLOT)))
        m["actb"] = np.ascontiguousarray(np.broadcast_to(ab[None, :], (128, NSLOT)))
        m["cvec"] = np.ascontiguousarray(c[b].reshape(32, 128))
        in_maps.append(m)
    return in_maps


def kernel(**inputs):
    in_maps = prep_inputs(inputs)
    nc = build_nc()
    res = run_bass_kernel_spmd(nc, in_maps, core_ids=list(range(NCORES)))
    out = np.zeros((2, 4096, 4096), np.float32)
    for core in range(NCORES):
        b, j = core // 4, core % 4
        out[b, 1024 * j:1024 * (j + 1)] = res.results[core]["y_out"]
    return out
```

```python
import numpy as np
from contextlib import ExitStack
import concourse.bass as bass
import concourse.mybir as mybir
from concourse.bass_utils import run_bass_kernel_spmd

F32 = mybir.dt.float32
BF16 = mybir.dt.bfloat16
I32 = mybir.dt.int32
AF = mybir.ActivationFunctionType
ALU = mybir.AluOpType

NCORES = 8
D = 4096
KC = 32
DFF = 11008
KCF = 86
NSLOT = 32
SW = 132
GROUPS = [(0, 1, 2), (3, 4, 5), (6, 7, 8), (9, 10, 11), (12, 13, 14), (15, 16, 17),
          (18, 19, 20), (21, 22, 23), (24, 25, 26), (27, 28, 29), (30, 31)]
OWNG = [(0, 1, 2), (3, 4, 5), (6, 7)]
CQ, CK, CV, CZ, CX, CB, CC, CDT = 0, 2048, 2560, 3072, 5120, 7168, 7680, 8192
SCALE = 128.0 ** -0.5
PI = float(np.pi)
ALPHA = 2.0 ** 0.25
LN_EPS = 1e-5
RMS_EPS = 1e-6
KVSLOT = {0: 0, 1: 1, 2: 2, 3: 3, 4: 4, 5: 5, 6: 6, 7: 7, 8: 8, 31: 9}


class Tok:
    __slots__ = ("key", "val")

    def __init__(self, key, val=None):
        self.key, self.val = key, val


class Buf:
    def __init__(self, name):
        self.name = name
        self.w = None
        self.r = {}


class K:
    CE = ("pe", "act", "dve", "pool", "sp")

    def __init__(self, nc, es):
        self.nc = nc
        self.eng = {"pe": nc.tensor, "act": nc.scalar, "dve": nc.vector, "pool": nc.gpsimd, "sp": nc.sync}
        self.sem = {e: es.enter_context(nc.semaphore("s_" + e)) for e in self.eng}
        self.cnt = {e: 0 for e in self.eng}
        self.pend = {e: [] for e in self.eng}
        self.last = {e: None for e in self.eng}
        self.lasttok = {e: None for e in self.eng}
        self.waited = {e: {} for e in self.eng}
        self.dsem = {}
        self.dpos = {}
        for q in ("sp", "pool"):
            self.dsem[q] = []
            for i in range(12):
                key = "d_%s%d" % (q, i)
                self.sem[key] = es.enter_context(nc.semaphore(key))
                self.dsem[q].append([key, 0, None])
            self.dpos[q] = 0
        self.outtoks = []
        self.ninst = 0

    def _resolve(self, tok):
        if tok.val is None:
            e = tok.key
            self.last[e].then_inc(self.sem[e], 1)
            self.cnt[e] += 1
            for t in self.pend[e]:
                t.val = self.cnt[e]
            self.pend[e] = []
        return tok.val

    def _wait(self, e, tok):
        if tok is None:
            return
        if tok.key == e and e == "pe":
            return
        v = self._resolve(tok)
        if self.waited[e].get(tok.key, 0) >= v:
            return
        self.eng[e].wait_ge(self.sem[tok.key], v)
        self.waited[e][tok.key] = v

    def _deps(self, e, r, w):
        for b in r:
            self._wait(e, b.w)
        for b in w:
            self._wait(e, b.w)
            for t in list(b.r.values()):
                self._wait(e, t)

    def _mark(self, tok, r, w):
        for b in r:
            b.r[tok.key] = tok
        for b in w:
            b.w = tok
            b.r = {}

    def op(self, e, fn, r=(), w=()):
        self._deps(e, r, w)
        inst = fn(self.eng[e])
        self.ninst += 1
        self.last[e] = inst
        tok = Tok(e)
        self.pend[e].append(tok)
        self.lasttok[e] = tok
        self._mark(tok, r, w)
        return tok

    def dma(self, q, out, in_, r=(), w=(), is_out=False):
        self._deps(q, r, w)
        slot = self.dsem[q][self.dpos[q] % len(self.dsem[q])]
        self.dpos[q] += 1
        if slot[2] is not None:
            self._wait(q, slot[2])
        slot[1] += 16
        self.eng[q].dma_start(out=out, in_=in_).then_inc(self.sem[slot[0]], 16)
        self.ninst += 1
        tok = Tok(slot[0], slot[1])
        slot[2] = tok
        self._mark(tok, r, w)
        if is_out:
            self.outtoks.append(tok)
        return tok

    def barrier(self):
        toks = [self.lasttok[e] for e in self.CE if self.lasttok[e] is not None]
        for q in ("sp", "pool"):
            toks += [s[2] for s in self.dsem[q] if s[2] is not None]
        for e in self.CE:
            for t in toks:
                self._wait(e, t)

    def finish(self):
        self.barrier()
        for t in self.outtoks:
            self._wait("sp", t)


def build_nc(dbg=()):
    nc = bass.Bass("TRN2", target_bir_lowering=False)

    in_names = []
    stop_after = [d for d in dbg if d.startswith("stop:")]
    stop_after = stop_after[0][5:] if stop_after else None
    nomod = "nomod" in dbg

    def din(name, shape, dt=F32, need=True):
        if not need:
            return None
        in_names.append(name)
        return nc.dram_tensor(name, list(shape), dt, kind="ExternalInput").ap()

    xm = din("xm", [NSLOT * 128, D])
    xh = din("xh", [NSLOT * 4, D])
    hmask = din("hmask", [128, NSLOT * 4])
    posr = din("posr", [10 * 128], I32)
    amask = din("amask", [3, 128, 384])
    actf = din("actf", [128, NSLOT])
    actb = din("actb", [128, NSLOT])
    cmat_d = din("cmat", [128, 6 * 128])
    sel_d = din("sel", [8, 8 * 128])
    ropec = din("ropec", [32, 2])
    psw_d = din("pswap", [32, 32])
    cvec = din("cvec", [32, 128], need=not nomod)
    w_ada = din("w_ada", [D, 6 * D], need=not nomod)
    b_ada = din("b_ada", [6 * D], need=not nomod)
    mod_in = din("mod_in", [6 * D], need=nomod)
    full = stop_after is None
    w_in = din("w_in", [D, 8256])
    conv_w = din("conv_w", [5, 3072])
    conv_b = din("conv_b", [3072])
    attn_sink = din("attn_sink", [16])
    a_log_f = din("a_log_fwd", [32])
    a_log_b = din("a_log_bwd", [32])
    dtb_f = din("dt_bias_fwd", [32])
    dtb_b = din("dt_bias_bwd", [32])
    ssd_d = din("ssd_d", [32])
    ssd_nw = din("ssd_norm_w", [2048])
    attn_nw = din("attn_norm_w", [2048])
    w_out = din("w_out", [D, D], need=full)
    ln1_g = din("ln1_g", [D], need=full)
    ln1_b = din("ln1_b", [D], need=full)
    w_gate = din("w_gate", [D, DFF], need=full)
    w_up = din("w_up", [D, DFF], need=full)
    w_down = din("w_down", [DFF, D], need=full)
    ln2_g = din("ln2_g", [D], need=full)
    ln2_b = din("ln2_b", [D], need=full)
    y_out = nc.dram_tensor("y_out", [1024, D], F32, kind="ExternalOutput").ap()
    mod_d = nc.dram_tensor("mod_d", [6 * D], F32).ap()
    sb_d = nc.dram_tensor("sb_d", [NSLOT, 128, 2048], F32).ap()
    eb_d = nc.dram_tensor("eb_d", [NSLOT, 128, 32], F32).ap()
    hb_d = nc.dram_tensor("hb_d", [8, 128, 2048], BF16).ap()
    mix_d = nc.dram_tensor("mix_d", [1024, D], BF16).ap()
    mixr_d = nc.dram_tensor("mixr_d", [1024, D], F32).ap()
    x1_d = nc.dram_tensor("x1_d", [1024, D], F32).ap()
    ffn_d = nc.dram_tensor("ffn_d", [1024, D], F32).ap()
    dbg_out = {}
    if "hf" in dbg:
        dbg_out["hf"] = nc.dram_tensor("dbg_hf", [128, 2048], F32, kind="ExternalOutput").ap()
    if "kv" in dbg:
        dbg_out["kT"] = nc.dram_tensor("dbg_kT", [128, 4 * 1280], F32, kind="ExternalOutput").ap()
        dbg_out["v"] = nc.dram_tensor("dbg_v", [128, 10 * 512], F32, kind="ExternalOutput").ap()
    for nm, shp, dt in (("mod", [6 * D], F32), ("hb", [8, 128, 2048], BF16), ("mix", [1024, D], BF16),
                        ("mixr", [1024, D], F32), ("x1", [1024, D], F32), ("ffn", [1024, D], F32)):
        if nm in dbg:
            dbg_out[nm] = nc.dram_tensor("dbg_" + nm, shp, dt, kind="ExternalOutput").ap()
    es = ExitStack()
    with es:
        k = K(nc, es)

        used_names = {}
        want_dump = "p3dump" in dbg

        def dump(name, ap, bufs, dt=F32):
            if not want_dump:
                return
            shp = list(ap.shape)
            o = nc.dram_tensor("dmp_" + name, shp, dt, kind="ExternalOutput").ap()
            k.dma("sp", o, ap, r=bufs, is_out=True)

        def sb(name, shape, dt=F32, stack=None):
            n = used_names.get(name, 0)
            used_names[name] = n + 1
            if n:
                name = "%s_r%d" % (name, n)
            return (stack or es).enter_context(nc.sbuf_tensor(name, list(shape), dt))

        psA = es.enter_context(nc.psum_tensor("psA", [128, 2048], F32))
        psB = es.enter_context(nc.psum_tensor("psB", [128, 2048], F32))
        bank = [psA[:, i * 512:(i + 1) * 512] for i in range(4)] + [psB[:, i * 512:(i + 1) * 512] for i in range(4)]
        bkb = [Buf("bank%d" % i) for i in range(8)]

        def V(ap, c):
            return ap.rearrange("p (s c) -> p s c", c=c)

        cmat = sb("cmat_s", [128, 6, 128]); b_cm = Buf("cmat")
        k.dma("sp", cmat[:].rearrange("p a b -> p (a b)"), cmat_d[:, :], w=[b_cm])
        ident, Uin, Lin, Ust, Lst, ones = [cmat[:, i, :] for i in range(6)]
        cb16 = sb("cb16", [128, 2, 128], BF16); b_c16 = Buf("cb16")
        identb, onesb = cb16[:, 0, :], cb16[:, 1, :]
        k.op("dve", lambda e: e.tensor_copy(out=identb, in_=ident), r=[b_cm], w=[b_c16])
        k.op("dve", lambda e: e.tensor_copy(out=onesb, in_=ones), r=[b_cm], w=[b_c16])
        hm, b_hm = sb("hm", [128, NSLOT, 4]), Buf("hm")
        k.dma("sp", hm[:].rearrange("p a b -> p (a b)"), hmask[:, :], w=[b_hm])
        af_t, b_af = sb("af_t", [128, NSLOT]), Buf("af")
        k.dma("sp", af_t[:], actf[:, :], w=[b_af])
        ab_t, b_ab = sb("ab_t", [128, NSLOT]), Buf("ab")
        k.dma("sp", ab_t[:], actb[:, :], w=[b_ab])
        sink_t, b_sink = sb("sink_t", [128, 16]), Buf("sink")
        k.dma("sp", sink_t[:], attn_sink.partition_broadcast(128), w=[b_sink])
        nsink = sb("nsink", [128, 16]); b_nsink = Buf("nsink")
        k.op("dve", lambda e: e.tensor_scalar(out=nsink[:], in0=sink_t[:], scalar1=-1.0, scalar2=None, op0=ALU.mult), r=[b_sink], w=[b_nsink])
        hp = sb("hp", [128, 160]); b_hp = Buf("hp")
        k.dma("sp", hp[:, 0:32], dtb_f.partition_broadcast(128), w=[b_hp])
        k.dma("sp", hp[:, 32:64], dtb_b.partition_broadcast(128), w=[b_hp])
        k.dma("sp", hp[:, 64:96], a_log_f.partition_broadcast(128), w=[b_hp])
        k.dma("sp", hp[:, 96:128], a_log_b.partition_broadcast(128), w=[b_hp])
        k.dma("sp", hp[:, 128:160], ssd_d.partition_broadcast(128), w=[b_hp])
        k.op("act", lambda e: e.activation(out=hp[:, 64:128], in_=hp[:, 64:128], func=AF.Exp), r=[b_hp], w=[b_hp])
        k.op("dve", lambda e: e.tensor_scalar(out=hp[:, 64:128], in0=hp[:, 64:128], scalar1=-1.0, scalar2=None, op0=ALU.mult), r=[b_hp], w=[b_hp])

        def load_cols(name, src2d, R, nb):
            t = sb(name, [128, nb, R]); bt = Buf(name)
            with nc.sbuf_tensor(name + "_s", [R, nb * 128], F32) as stg:
                bs = Buf(name + "_s")
                k.dma("sp", stg[:], src2d, w=[bs])
                per = max(1, 512 // R)
                for b0 in range(0, nb, per):
                    n = min(per, nb - b0)
                    for i in range(n):
                        k.op("pe", lambda e, i=i: e.transpose(out=bank[2][:, i * R:(i + 1) * R], in_=stg[:, (b0 + i) * 128:(b0 + i + 1) * 128], identity=ident[0:R, 0:R]), r=[bs, b_cm], w=[bkb[2]])
                    k.op("dve", lambda e, n=n: e.tensor_copy(out=t[:, b0:b0 + n, :].rearrange("p a b -> p (a b)"), in_=bank[2][:, 0:n * R]), r=[bkb[2]], w=[bt])
                k.barrier()
            return t, bt

        cw, b_cw = load_cols("cw", conv_w[:, :], 5, 24)
        cbc, b_cbc = load_cols("cbc", conv_b.rearrange("(a b) -> a b", b=128), 24, 1)
        anw, b_anw = load_cols("anw", attn_nw.rearrange("(a b) -> a b", b=128), 16, 1)
        snw, b_snw = load_cols("snw", ssd_nw.rearrange("(a b) -> a b", b=128), 16, 1)

        condT = sb("condT", [128, 32], BF16); b_cond = Buf("condT")
        if nomod:
            k.dma("sp", mod_d, mod_in)
            k.barrier()
        else:
            with ExitStack() as ps_:
                cst = sb("cst", [32, 128], stack=ps_); b_cst = Buf("cst")
                k.dma("sp", cst[:], cvec[:, :], w=[b_cst])
                k.op("pe", lambda e: e.transpose(out=bank[2][:, 0:32], in_=cst[:], identity=ident[0:32, 0:32]), r=[b_cst, b_cm], w=[bkb[2]])
                k.op("act", lambda e: e.activation(out=condT[:], in_=bank[2][:, 0:32], func=AF.Silu), r=[bkb[2]], w=[b_cond])
                k.barrier()
        xt = sb("xt", [128, D]); b_xt = Buf("xt")
        NW = 4
        wsl = [sb("wsl%d" % i, [128, 32, 128], BF16) for i in range(NW)]; b_wsl = [Buf("wsl%d" % i) for i in range(NW)]
        wc = [0]
        gc = [0]

        def transp_mod(src_tile, b_src, nrows, dst_fn, b_dst, mc, b_mc):
            for k0 in range(0, KC, 4):
                pb = 6 + (k0 // 4) % 2
                for j in range(4):
                    kc = k0 + j
                    k.op("pe", lambda e, kc=kc, j=j: e.transpose(out=bank[pb][:, j * 128:j * 128 + nrows], in_=src_tile[0:nrows, kc * 128:(kc + 1) * 128], identity=ident[0:nrows, 0:nrows]), r=[b_src, b_cm], w=[bkb[pb]])
                for j in range(4):
                    kc = k0 + j
                    for (dst, src) in dst_fn(kc, bank[pb][:, j * 128:j * 128 + nrows]):
                        k.op("act", lambda e, kc=kc, dst=dst, src=src: e.activation(out=dst, in_=src, func=AF.Identity, scale=mc[:, 32 + kc:33 + kc], bias=mc[:, kc:kc + 1]), r=[bkb[pb], b_mc], w=[b_dst])

        def make_hT(src_rows, nrows, dst_fn, b_dst):
            k.dma("sp", xt[0:nrows, :], src_rows, w=[b_xt])
            transp_mod(xt, b_xt, nrows, dst_fn, b_dst, mA, b_mcA)

        pend_epi = [None]
        bg_hook = [lambda: None]

        def flush_epi():
            if pend_epi[0] is not None:
                f = pend_epi[0]; pend_epi[0] = None
                f()

        def gemm(act_fn, N, nkc, wsrc, col0, nblk, epi, r_act, units=1, two_phase=False):
            flush_epi()
            usz = [nkc // units + (1 if u < nkc % units else 0) for u in range(units)]
            uoff = [sum(usz[:u]) for u in range(units)]
            for blk in range(nblk):
                c0 = col0 + blk * 128
                pb = gc[0] % 2; gc[0] += 1
                for u in range(units):
                    wi = wc[0] % NW; wc[0] += 1
                    per = usz[u]
                    k.dma("pool", wsl[wi][:, 0:per, :], wsrc[uoff[u] * 128:(uoff[u] + per) * 128, c0:c0 + 128].rearrange("(kc p) c -> p kc c", p=128), w=[b_wsl[wi]])
                    for kk in range(per):
                        kc = uoff[u] + kk
                        k.op("pe", lambda e, kc=kc, kk=kk, wi=wi: e.matmul(bank[pb][:, 0:N], lhsT=wsl[wi][:, kk, :], rhs=act_fn(kc), start=(kc == 0), stop=(kc == nkc - 1)), r=[b_wsl[wi]] + r_act, w=[bkb[pb]])
                flush_epi()
                if two_phase:
                    pend_epi[0] = epi(blk, bank[pb], bkb[pb])
                else:
                    pend_epi[0] = (lambda blk=blk, pb=pb: epi(blk, bank[pb], bkb[pb]))
                bg_hook[0]()

        abar = sb("abar", [1, 2, 128]); b_abar = [Buf("abar%d" % i) for i in range(4)]
        amrow = sb("amrow", [1, 2, 128]); b_amrow = [Buf("amrow%d" % i) for i in range(4)]
        adc = [0]

        def ada_unit(u):
            c0 = u * 128
            i = adc[0] % 2; adc[0] += 1
            pb = gc[0] % 2; gc[0] += 1
            wi = wc[0] % NW; wc[0] += 1
            flush_epi()
            k.dma("pool", wsl[wi][:, 0:KC, :], w_ada[:, c0:c0 + 128].rearrange("(kc p) c -> p kc c", p=128), w=[b_wsl[wi]])
            k.dma("sp", abar[0:1, i, :], b_ada[c0:c0 + 128].rearrange("(a b) -> a b", a=1), w=[b_abar[i]])
            for kc in range(KC):
                k.op("pe", lambda e, kc=kc: e.matmul(bank[pb][0:1, 0:128], lhsT=condT[:, kc:kc + 1], rhs=wsl[wi][:, kc, :], start=(kc == 0), stop=False), r=[b_cond, b_wsl[wi]], w=[bkb[pb]])
            k.op("pe", lambda e: e.matmul(bank[pb][0:1, 0:128], lhsT=ones[0:1, 0:1], rhs=abar[0:1, i, :], start=False, stop=True), r=[b_cm, b_abar[i]], w=[bkb[pb]])

            def epi():
                k.op("act", lambda e: e.activation(out=amrow[0:1, i, :], in_=bank[pb][0:1, 0:128], func=AF.Identity), r=[bkb[pb]], w=[b_amrow[i]])
                k.dma("sp", mod_d[c0:c0 + 128].rearrange("(a b) -> a b", a=1), amrow[0:1, i, :], r=[b_amrow[i]])
            pend_epi[0] = epi
            bg_hook[0]()

        if not nomod:
            for u in range(64):
                ada_unit(u)
            flush_epi()
            k.barrier()
        mcA, b_mcA = load_cols("mcA", mod_d[0:2 * D].rearrange("(a b) -> a b", b=128), 64, 1)
        mA = mcA[:, 0, :]
        k.op("dve", lambda e: e.tensor_scalar(out=mA[:, 32:64], in0=mA[:, 32:64], scalar1=1.0, scalar2=None, op0=ALU.add), r=[b_mcA], w=[b_mcA])
        mcB = sb("mcB", [128, 1, 64]); b_mcB = Buf("mcB")
        mB = mcB[:, 0, :]

        Hf = sb("Hf", [128, 2048]); b_Hf = Buf("Hf")
        k.op("dve", lambda e: e.memset(Hf[:], 0.0), w=[b_Hf])
        wdt = sb("wdt", [128, KC, 64], BF16); b_wdt = Buf("wdt")
        k.dma("pool", wdt[:], w_in[:, CDT:CDT + 64].rearrange("(kc p) c -> p kc c", p=128), w=[b_wdt])
        ast_ = ExitStack()
        cosT = sb("cosT", [32, 1280], stack=ast_); sinT = sb("sinT", [32, 1280], stack=ast_); b_rope = Buf("rope")
        psw = sb("psw", [32, 32], stack=ast_); b_psw = Buf("psw")
        k.dma("sp", psw[:], psw_d[:, :], w=[b_psw])
        with ExitStack() as ps_:
            pi_i = sb("pi_i", [32, 1280], I32, stack=ps_); ang = sb("ang", [32, 1280], stack=ps_)
            rt = sb("rt", [32, 1280], stack=ps_); rc = sb("rc", [32, 2], stack=ps_)
            b_pi, b_ang, b_rt, b_rc = Buf("pi"), Buf("ang"), Buf("rt"), Buf("rc")
            k.dma("sp", pi_i[:], posr.partition_broadcast(32), w=[b_pi])
            k.dma("sp", rc[:], ropec[:, :], w=[b_rc])
            k.op("dve", lambda e: e.tensor_copy(out=ang[:], in_=pi_i[:]), r=[b_pi], w=[b_ang])
            k.op("dve", lambda e: e.tensor_scalar(out=ang[:], in0=ang[:], scalar1=rc[:, 0:1], scalar2=None, op0=ALU.mult), r=[b_ang, b_rc], w=[b_ang])
            for (dstT, off) in ((sinT, 0.0), (cosT, PI / 2)):
                k.op("dve", lambda e: e.tensor_scalar(out=rt[:], in0=ang[:], scalar1=off, scalar2=1.0 / (2 * PI), op0=ALU.add, op1=ALU.mult), r=[b_ang], w=[b_rt])
                k.op("dve", lambda e: e.tensor_copy(out=pi_i[:], in_=rt[:]), r=[b_rt], w=[b_pi])
                k.op("dve", lambda e: e.tensor_copy(out=rt[:], in_=pi_i[:]), r=[b_pi], w=[b_rt])
                k.op("dve", lambda e: e.scalar_tensor_tensor(out=rt[:], in0=rt[:], scalar=-2 * PI, in1=ang[:], op0=ALU.mult, op1=ALU.add), r=[b_rt, b_ang], w=[b_rt])
                k.op("dve", lambda e: e.tensor_scalar(out=rt[:], in0=rt[:], scalar1=off, scalar2=PI, op0=ALU.add, op1=ALU.min), r=[b_rt], w=[b_rt])
                k.op("dve", lambda e: e.tensor_scalar(out=rt[:], in0=rt[:], scalar1=-PI, scalar2=None, op0=ALU.max), r=[b_rt], w=[b_rt])
                k.op("act", lambda e, dstT=dstT: e.activation(out=dstT[:], in_=rt[:], func=AF.Sin), r=[b_rt], w=[b_rope])
            k.op("dve", lambda e: e.tensor_scalar(out=sinT[:], in0=sinT[:], scalar1=rc[:, 1:2], scalar2=None, op0=ALU.mult), r=[b_rope, b_rc], w=[b_rope])
            k.barrier()

        qr32 = sb("qr32", [32, 256], stack=ast_); b_qr = Buf("qr32")
        rtmp = sb("rtmp", [32, 2, 256], stack=ast_); b_rtmp = Buf("rtmp")

        def rope_rows(ps, pbuf, src_cols, tab_col0, n, dst, b_dst):
            k.op("act", lambda e: e.activation(out=qr32[:, 0:n], in_=ps[0:32, src_cols:src_cols + n], func=AF.Identity), r=[pbuf], w=[b_qr])
            k.op("pe", lambda e: e.matmul(bank[2][0:32, 0:n], lhsT=psw[:], rhs=qr32[:, 0:n], start=True, stop=True), r=[b_psw, b_qr], w=[bkb[2]])
            k.op("dve", lambda e: e.tensor_tensor(out=rtmp[:, 0, 0:n], in0=qr32[:, 0:n], in1=cosT[:, tab_col0:tab_col0 + n], op=ALU.mult), r=[b_qr, b_rope], w=[b_rtmp])
            k.op("dve", lambda e: e.tensor_tensor(out=rtmp[:, 1, 0:n], in0=bank[2][0:32, 0:n], in1=sinT[:, tab_col0:tab_col0 + n], op=ALU.mult), r=[bkb[2], b_rope], w=[b_rtmp])
            k.op("dve", lambda e: e.tensor_tensor(out=dst, in0=rtmp[:, 0, 0:n], in1=rtmp[:, 1, 0:n], op=ALU.add), r=[b_rtmp], w=[b_dst])

        kT_all = sb("kT_all", [128, 4, 1280], BF16, stack=ast_); b_kT = Buf("kT")
        v_tok = sb("v_tok", [128, 10, 512], BF16, stack=ast_); b_v = Buf("v")

        hTgL = xsL = BtL = ue = dg = dtraw = dts = adt = ex = wv = etfa = xw = sst = vT = None
        b_hTg = [Buf("hTg0"), Buf("hTg1")]; b_xs = [Buf("xs0"), Buf("xs1")]; b_Bt = [Buf("Bt0"), Buf("Bt1")]
        b_ue = [Buf("ue0"), Buf("ue1")]; b_dg = [Buf("dg0"), Buf("dg1")]
        b_dtraw = [Buf("dtraw0"), Buf("dtraw1")]
        b_dts = [[Buf("dts") for _ in range(3)] for _ in range(2)]; b_adt = [[Buf("adt") for _ in range(3)] for _ in range(2)]
        b_ex = [[Buf("ex") for _ in range(3)] for _ in range(2)]; b_wv = [[Buf("wv") for _ in range(3)] for _ in range(2)]
        b_etfa = Buf("etfa"); b_xw = [Buf("xw0"), Buf("xw1")]; b_sst = Buf("sst"); b_vT = Buf("vT")
        uec = [0]

        def alloc_group(gst, tag, p1):
            nonlocal hTgL, xsL, BtL, ue, dg, dtraw, dts, adt, ex, wv, etfa, xw, sst, vT
            npar = 2 if p1 else 1
            hTgL = [sb("hTg" + tag, [128, KC, 3 * SW], BF16, stack=gst) for _ in range(npar)]
            xsL = [sb("xs_tok" + tag, [128, 3, 2048], BF16, stack=gst) for _ in range(npar)]
            BtL = [sb("B_tok" + tag, [128, 3, 512], BF16, stack=gst) for _ in range(npar)]
            ue = [sb("ue%d" % i + tag, [128, 3 * SW], BF16, stack=gst) for i in range(2)]
            dg = [sb("dg%d" % i + tag, [128, 6, 128], BF16, stack=gst) for i in range(2)]
            dtraw = sb("dtraw" + tag, [128, npar, 3, 64], stack=gst)
            dts = sb("dts" + tag, [128, npar, 3, 64], stack=gst)
            adt = sb("adt" + tag, [128, npar, 3, 64], stack=gst)
            ex = sb("ex" + tag, [128, npar, 3, 192], stack=gst)
            wv = sb("wv" + tag, [128, npar, 3, 64], stack=gst)
            etfa = sb("etfa" + tag, [128, 32], stack=gst)
            xw = [sb("xw%d" % i + tag, [128, 2048], BF16, stack=gst) for i in range(2 if p1 else 1)]
            if p1:
                sst = sb("sst" + tag, [128, 512], stack=gst)
                vT = sb("vT" + tag, [128, 3 * SW], BF16, stack=gst)

        from collections import deque
        bgq = deque()

        def bg_step(n=3):
            for _ in range(n):
                if bgq:
                    bgq.popleft()()

        def bg_drain():
            while bgq:
                bgq.popleft()()

        def hT_tile_tasks(src_rows, nrows, dst_fn, b_dst):
            tasks = [lambda: k.dma("sp", xt[0:nrows, :], src_rows, w=[b_xt])]
            for k0 in range(0, KC, 4):
                def t(k0=k0):
                    pb = 6 + (k0 // 4) % 2
                    for j in range(4):
                        kc = k0 + j
                        k.op("pe", lambda e, kc=kc, j=j: e.transpose(out=bank[pb][:, j * 128:j * 128 + nrows], in_=xt[0:nrows, kc * 128:(kc + 1) * 128], identity=ident[0:nrows, 0:nrows]), r=[b_xt, b_cm], w=[bkb[pb]])
                    for j in range(4):
                        kc = k0 + j
                        for (dst, src) in dst_fn(kc, bank[pb][:, j * 128:j * 128 + nrows]):
                            k.op("act", lambda e, kc=kc, dst=dst, src=src: e.activation(out=dst, in_=src, func=AF.Identity, scale=mA[:, 32 + kc:33 + kc], bias=mA[:, kc:kc + 1]), r=[bkb[pb], b_mcA], w=[b_dst])
                tasks.append(t)
            return tasks

        def group_hT_tasks(slots, par):
            ns = len(slots); s0 = slots[0]
            h = hTgL[par]
            tasks = []
            for si, s in enumerate(slots):
                tasks += hT_tile_tasks(xm[s * 128:(s + 1) * 128, :], 128, lambda kc, src, si=si: [(h[:, kc, si * SW + 2:si * SW + 130], src)], b_hTg[par])

            def hdst(kc, src):
                hv = V(h[:, kc, 0:ns * SW], SW)
                sv = V(src, 4)
                return [(hv[:, :, 0:2], sv[:, :, 0:2]), (hv[:, :, 130:132], sv[:, :, 2:4])]
            tasks += hT_tile_tasks(xh[4 * s0:4 * s0 + 4 * ns, :], 4 * ns, hdst, b_hTg[par])
            return tasks

        def conv_epi(blk_ch, ps, pbuf, slots, dst_tok, b_dsttok, dst_col0, feat=None):
            ns = len(slots); s0 = slots[0]
            i = uec[0] % 2; uec[0] += 1
            u = ue[i]
            k.op("act", lambda e: e.activation(out=u[:, 0:ns * SW], in_=ps[:, 0:ns * SW], func=AF.Identity), r=[pbuf], w=[b_ue[i]])
            u3 = V(u[:, 0:ns * SW], SW)
            k.op("dve", lambda e: e.tensor_tensor(out=u3[:, :, 0:2], in0=u3[:, :, 0:2], in1=hm[:, s0:s0 + ns, 0:2], op=ALU.mult), r=[b_hm, b_ue[i]], w=[b_ue[i]])
            k.op("dve", lambda e: e.tensor_tensor(out=u3[:, :, 130:132], in0=u3[:, :, 130:132], in1=hm[:, s0:s0 + ns, 2:4], op=ALU.mult), r=[b_hm, b_ue[i]], w=[b_ue[i]])
            if dst_tok is not None:
                for j in range(5):
                    k.op("dve", lambda e, j=j: e.tensor_scalar(out=dg[i][:, j, :], in0=identb, scalar1=cw[:, blk_ch, j:j + 1], scalar2=None, op0=ALU.mult), r=[b_c16, b_cw], w=[b_dg[i]])
                k.op("dve", lambda e: e.tensor_scalar(out=dg[i][:, 5, :], in0=identb, scalar1=cbc[:, 0, blk_ch:blk_ch + 1], scalar2=None, op0=ALU.mult), r=[b_c16, b_cbc], w=[b_dg[i]])
                pass

            def late():
                if dst_tok is not None:
                    pc = 3
                    for si in range(ns):
                        o = bank[pc][:, si * 128:(si + 1) * 128]
                        for j in range(5):
                            k.op("pe", lambda e, j=j, si=si, o=o: e.matmul(o, lhsT=u[:, si * SW + j:si * SW + j + 128], rhs=dg[i][:, j, :], start=(j == 0), stop=False), r=[b_ue[i], b_dg[i]], w=[bkb[pc]])
                        k.op("pe", lambda e, o=o: e.matmul(o, lhsT=onesb, rhs=dg[i][:, 5, :], start=False, stop=True), r=[b_c16, b_dg[i]], w=[bkb[pc]])
                    k.op("act", lambda e: e.activation(out=dst_tok[:, 0:ns, dst_col0:dst_col0 + 128], in_=V(bank[pc][:, 0:ns * 128], 128), func=AF.Silu), r=[bkb[pc]], w=[b_dsttok])
                if feat is not None:
                    dstT, b_dstT, acc, b_acc = feat
                    k.op("dve", lambda e: e.tensor_scalar(out=acc[:, 0:ns, :], in0=u3[:, :, 0:128], scalar1=cw[:, blk_ch, 0:1], scalar2=None, op0=ALU.mult), r=[b_ue[i], b_cw], w=[b_acc])
                    for j in range(1, 5):
                        k.op("dve", lambda e, j=j: e.scalar_tensor_tensor(out=acc[:, 0:ns, :], in0=u3[:, :, j:j + 128], scalar=cw[:, blk_ch, j:j + 1], in1=acc[:, 0:ns, :], op0=ALU.mult, op1=ALU.add), r=[b_ue[i], b_cw, b_acc], w=[b_acc])
                    k.op("act", lambda e: e.activation(out=dstT[:, 0:ns, :], in_=acc[:, 0:ns, :], func=AF.Silu, bias=cbc[:, 0, blk_ch:blk_ch + 1]), r=[b_acc, b_cbc], w=[b_dstT])

            return late

        def hT_fn(ns, par=0):
            return lambda kc: hTgL[par][:, kc, 0:ns * SW]

        def dt_matmuls(slots, par):
            ns = len(slots)
            for si in range(ns):
                for kc in range(KC):
                    k.op("pe", lambda e, kc=kc, si=si: e.matmul(bank[2][:, si * 64:(si + 1) * 64], lhsT=hTgL[par][:, kc, si * SW + 2:si * SW + 130], rhs=wdt[:, kc, :], start=(kc == 0), stop=(kc == KC - 1)), r=[b_hTg[par], b_wdt], w=[bkb[2]])
            k.op("dve", lambda e: e.tensor_copy(out=dtraw[:, par, 0:ns, :], in_=V(bank[2][:, 0:ns * 64], 64)), r=[bkb[2]], w=[b_dtraw[par]])

        def chain_dt_tasks(slots, par):
            ns = len(slots)
            T = []
            rng = range(ns)
            T.append(lambda: [k.op("dve", lambda e, si=si: e.tensor_tensor(out=dts[:, par, si, :], in0=dtraw[:, par, si, :], in1=hp[:, 0:64], op=ALU.add), r=[b_dtraw[par], b_hp], w=[b_dts[par][si]]) for si in rng])
            T.append(lambda: [k.op("act", lambda e, si=si: e.activation(out=dts[:, par, si, :], in_=dts[:, par, si, :], func=AF.Exp), r=[b_dts[par][si]], w=[b_dts[par][si]]) for si in rng])
            T.append(lambda: [k.op("act", lambda e, si=si: e.activation(out=dts[:, par, si, :], in_=dts[:, par, si, :], func=AF.Ln, bias=1.0), r=[b_dts[par][si]], w=[b_dts[par][si]]) for si in rng])
            T.append(lambda: [k.op("dve", lambda e, si=si: e.tensor_tensor(out=adt[:, par, si, :], in0=dts[:, par, si, :], in1=hp[:, 64:128], op=ALU.mult), r=[b_dts[par][si], b_hp], w=[b_adt[par][si]]) for si in rng])

            def emats():
                for si in rng:
                    pb = 2 + si % 2
                    for (c0, n, M, a0) in ((0, 32, Uin, 0), (32, 32, Lin, 32), (64, 32, Lst, 0), (96, 32, Ust, 32), (128, 64, ones, 0)):
                        k.op("pe", lambda e, c0=c0, n=n, M=M, a0=a0, si=si, pb=pb: e.matmul(bank[pb][:, c0:c0 + n], lhsT=M, rhs=adt[:, par, si, a0:a0 + n], start=True, stop=True), r=[b_cm, b_adt[par][si]], w=[bkb[pb]])
                    k.op("act", lambda e, si=si, pb=pb: e.activation(out=ex[:, par, si, :], in_=bank[pb][:, 0:192], func=AF.Exp), r=[bkb[pb]], w=[b_ex[par][si]])
            T.append(emats)
            T.append(lambda: [k.op("dve", lambda e, si=si: e.tensor_tensor(out=wv[:, par, si, :], in0=dts[:, par, si, :], in1=ex[:, par, si, 64:128], op=ALU.mult), r=[b_dts[par][si], b_ex[par][si]], w=[b_wv[par][si]]) for si in rng])
            return T

        def bc_hp(ap32, nh=32):
            return ap32.unsqueeze(2).broadcast_to([128, nh, 64])

        def slot_states(si, s, do_f, do_b, act_f_col=None, par=0):
            xs3 = V(xsL[par][:, si, :], 64)
            wv_ = wv[:, par, si, :]; ex_ = ex[:, par, si, :]
            bwv = b_wv[par][si]; bex = b_ex[par][si]; B_tok = BtL[par]; bBt = b_Bt[par]; bxs = b_xs[par]
            if do_f:
                k.op("dve", lambda e: e.tensor_tensor(out=V(xw[0][:], 64), in0=xs3, in1=bc_hp(wv_[:, 0:32]), op=ALU.mult), r=[bxs, bwv], w=[b_xw[0]])
                if act_f_col is not None:
                    k.op("dve", lambda e: e.tensor_scalar(out=etfa[:, 0:32], in0=ex_[:, 128:160], scalar1=act_f_col, scalar2=None, op0=ALU.mult), r=[bex, b_af], w=[b_etfa])
                    ecol = etfa[:, 0:32]
                else:
                    ecol = ex_[:, 128:160]
                for g in range(4):
                    pb = 4 + g % 2
                    k.op("pe", lambda e, g=g, pb=pb: e.matmul(bank[pb][:, :], lhsT=B_tok[:, si, g * 128:(g + 1) * 128], rhs=xw[0][:, g * 512:(g + 1) * 512], start=True, stop=True), r=[bBt, b_xw[0]], w=[bkb[pb]])
                    hg = V(Hf[:, g * 512:(g + 1) * 512], 64)
                    k.op("dve", lambda e, g=g, hg=hg: e.tensor_tensor(out=hg, in0=hg, in1=bc_hp(ecol[:, g * 8:(g + 1) * 8], 8), op=ALU.mult), r=[bex, b_etfa], w=[b_Hf])
                    if act_f_col is not None:
                        k.op("dve", lambda e, g=g, pb=pb: e.scalar_tensor_tensor(out=Hf[:, g * 512:(g + 1) * 512], in0=bank[pb][:, :], scalar=act_f_col, in1=Hf[:, g * 512:(g + 1) * 512], op0=ALU.mult, op1=ALU.add), r=[bkb[pb], b_af], w=[b_Hf])
                    else:
                        k.op("dve", lambda e, g=g, pb=pb: e.tensor_tensor(out=Hf[:, g * 512:(g + 1) * 512], in0=Hf[:, g * 512:(g + 1) * 512], in1=bank[pb][:, :], op=ALU.add), r=[bkb[pb]], w=[b_Hf])
            if do_b:
                k.op("dve", lambda e: e.tensor_tensor(out=V(xw[1][:], 64), in0=xs3, in1=bc_hp(wv_[:, 32:64]), op=ALU.mult), r=[bxs, bwv], w=[b_xw[1]])
                for g in range(4):
                    pb = 4 + g % 2
                    k.op("pe", lambda e, g=g, pb=pb: e.matmul(bank[pb][:, :], lhsT=B_tok[:, si, g * 128:(g + 1) * 128], rhs=xw[1][:, g * 512:(g + 1) * 512], start=True, stop=True), r=[bBt, b_xw[1]], w=[bkb[pb]])
                    k.op("act", lambda e, g=g, pb=pb: e.activation(out=sst[:, 0:512], in_=bank[pb][:, :], func=AF.Identity), r=[bkb[pb]], w=[b_sst])
                    k.dma("sp", sb_d[s][:, g * 512:(g + 1) * 512], sst[:], r=[b_sst])
                k.dma("sp", eb_d[s], ex_[:, 160:192], r=[bex])

        gst1 = ExitStack()
        alloc_group(gst1, "a", True)
        bg_hook[0] = lambda: bg_step(3)

        def chain_all_tasks(G, par):
            T = chain_dt_tasks(G, par)
            for si, s in enumerate(G):
                T.append(lambda si=si, s=s: slot_states(si, s, do_f=(s >= 8), do_b=False, act_f_col=af_t[:, s:s + 1], par=par))
                T.append(lambda si=si, s=s: slot_states(si, s, do_f=False, do_b=True, par=par))
            return T

        for t_ in group_hT_tasks(GROUPS[0], 0):
            t_()
        for gi, G in enumerate(GROUPS):
            ns = len(G)
            par = gi % 2
            bg_drain()
            ta = group_hT_tasks(GROUPS[gi + 1], 1 - par) if gi + 1 < len(GROUPS) else []
            tb = chain_all_tasks(GROUPS[gi - 1], 1 - par) if gi >= 1 else []
            while ta or tb:
                if ta:
                    bgq.append(ta.pop(0))
                    if ta:
                        bgq.append(ta.pop(0))
                if tb:
                    bgq.append(tb.pop(0))
            xs_, bxs_, Bt_, bBt_ = xsL[par], b_xs[par], BtL[par], b_Bt[par]
            gemm(hT_fn(ns, par), ns * SW, KC, w_in, CX, 16, lambda blk, ps, pbuf, G=G, xs_=xs_, bxs_=bxs_: conv_epi(blk, ps, pbuf, G, xs_, bxs_, blk * 128), [b_hTg[par]], two_phase=True)
            gemm(hT_fn(ns, par), ns * SW, KC, w_in, CB, 4, lambda blk, ps, pbuf, G=G, Bt_=Bt_, bBt_=bBt_: conv_epi(16 + blk, ps, pbuf, G, Bt_, bBt_, blk * 128), [b_hTg[par]], two_phase=True)
            kvs = [(si, s) for si, s in enumerate(G) if s in KVSLOT]
            if kvs:
                def epi_k(blk, ps, pbuf, kvs=kvs):
                    for si, s in kvs:
                        ti = KVSLOT[s]
                        k.op("act", lambda e, si=si, ti=ti: e.activation(out=kT_all[:, blk, ti * 128:(ti + 1) * 128], in_=ps[:, si * SW + 2:si * SW + 130], func=AF.Identity), r=[pbuf], w=[b_kT])
                        rope_rows(ps, pbuf, si * SW + 2, ti * 128, 128, kT_all[0:32, blk, ti * 128:(ti + 1) * 128], b_kT)
                gemm(hT_fn(ns, par), ns * SW, KC, w_in, CK, 4, epi_k, [b_hTg[par]])

                def epi_v(blk, ps, pbuf, kvs=kvs, ns=ns):
                    k.op("act", lambda e: e.activation(out=vT[:, 0:ns * SW], in_=ps[:, 0:ns * SW], func=AF.Identity), r=[pbuf], w=[b_vT])
                    pvb = bank[3].bitcast(BF16)
                    for si, s in kvs:
                        k.op("pe", lambda e, si=si: e.transpose(out=pvb[:, si * 128:(si + 1) * 128], in_=vT[:, si * SW + 2:si * SW + 130], identity=identb), r=[b_vT, b_c16], w=[bkb[3]])
                    for si, s in kvs:
                        ti = KVSLOT[s]
                        k.op("dve", lambda e, si=si, ti=ti: e.tensor_copy(out=v_tok[:, ti, blk * 128:(blk + 1) * 128], in_=pvb[:, si * 128:(si + 1) * 128]), r=[bkb[3]], w=[b_v])
                gemm(hT_fn(ns, par), ns * SW, KC, w_in, CV, 4, epi_v, [b_hTg[par]])
            flush_epi()
            dt_matmuls(G, par)
        bg_drain()
        for t_ in chain_all_tasks(GROUPS[-1], (len(GROUPS) - 1) % 2):
            t_()
        bg_hook[0] = lambda: None
        if "hf" in dbg:
            k.dma("sp", dbg_out["hf"], Hf[:], r=[b_Hf], is_out=True)
        if "kv" in dbg:
            with ExitStack() as ps_:
                t1 = sb("dbgt1", [128, 4 * 1280], stack=ps_); t2 = sb("dbgt2", [128, 10 * 512], stack=ps_)
                bt1, bt2 = Buf("t1"), Buf("t2")
                k.op("dve", lambda e: e.tensor_copy(out=t1[:], in_=kT_all[:].rearrange("p a b -> p (a b)")), r=[b_kT], w=[bt1])
                k.op("dve", lambda e: e.tensor_copy(out=t2[:], in_=v_tok[:].rearrange("p a b -> p (a b)")), r=[b_v], w=[bt2])
                k.dma("sp", dbg_out["kT"], t1[:], r=[bt1], is_out=True)
                k.dma("sp", dbg_out["v"], t2[:], r=[bt2], is_out=True)
                k.barrier()

        gst1.close()
        k.barrier()
        with ExitStack() as ps_:
            hTa = sb("hTa", [128, KC, 256], BF16, stack=ps_); b_hTa = Buf("hTa")
            qTg = sb("qTg", [128, 16, 256], BF16, stack=ps_); b_qT = Buf("qTg")
            amb = sb("amb", [128, 3, 384], BF16, stack=ps_); b_amb = Buf("amb")
            k.dma("pool", amb[:], amask.rearrange("a p c -> p a c"), w=[b_amb])
            ao = [sb("ao%d" % i, [128, 2048], stack=ps_) for i in range(2)]; b_ao = [Buf("ao0"), Buf("ao1")]
            aon = sb("aon", [128, 2048], BF16, stack=ps_); b_aon = Buf("aon")
            Pm = [sb("Pm%d" % i, [128, 384], BF16, stack=ps_) for i in range(2)]; b_Pm = [Buf("Pm0"), Buf("Pm1")]
            PT = [sb("PT%d" % i, [128, 3, 128], BF16, stack=ps_) for i in range(2)]; b_PT = [Buf("PT0"), Buf("PT1")]
            st = [sb("ast%d" % i, [128, 8], stack=ps_) for i in range(2)]; b_st = [Buf("ast0"), Buf("ast1")]
            sq = sb("asq", [128, 20], stack=ps_); b_sq = Buf("asq")
            aosq = sb("aosq", [128, 2048], stack=ps_); b_aosq = Buf("aosq")
            pc = [0]
            for pr in range(4):
                for si in range(2):
                    s = pr * 2 + si
                    make_hT(xm[s * 128:(s + 1) * 128, :], 128, lambda kc, src, si=si: [(hTa[:, kc, si * 128:(si + 1) * 128], src)], b_hTa)

                def epi_q(blk, ps, pbuf, pr=pr):
                    k.op("act", lambda e: e.activation(out=qTg[:, blk, :], in_=ps[:, 0:256], func=AF.Identity), r=[pbuf], w=[b_qT])
                    rope_rows(ps, pbuf, 0, pr * 256, 256, qTg[0:32, blk, :], b_qT)
                gemm(lambda kc: hTa[:, kc, :], 256, KC, w_in, CQ, 16, epi_q, [b_hTa])
                flush_epi()
                items = [(si, hq) for si in range(2) for hq in range(16)]

                def geo(n):
                    si, hq = items[n]
                    c = pr * 2 + si
                    kvt = [9 if c == 0 else c - 1, c, c + 1]
                    mt = 0 if c == 0 else (2 if c == 7 else 1)
                    return si, hq, c, kvt, mt, hq // 4, n % 2

                def stA(n):
                    si, hq, c, kvt, mt, g, i = geo(n)
                    pS = 2 + i
                    k.op("pe", lambda e: e.matmul(bank[pS][:, 0:384], lhsT=identb, rhs=amb[:, mt, :], start=True, stop=False), r=[b_c16, b_amb], w=[bkb[pS]])
                    for kb in range(3):
                        k.op("pe", lambda e, kb=kb: e.matmul(bank[pS][:, kb * 128:(kb + 1) * 128], lhsT=qTg[:, hq, si * 128:(si + 1) * 128], rhs=kT_all[:, g, kvt[kb] * 128:(kvt[kb] + 1) * 128], start=False, stop=(kb == 2)), r=[b_qT, b_kT], w=[bkb[pS]])

                def stB(n):
                    si, hq, c, kvt, mt, g, i = geo(n)
                    pS = 2 + i
                    s_ = st[i]; bs_ = b_st[i]
                    k.op("dve", lambda e: e.reduce_max(out=s_[:, 0:1], in_=bank[pS][:, 0:384], axis=mybir.AxisListType.X), r=[bkb[pS]], w=[bs_])
                    k.op("dve", lambda e: e.tensor_scalar(out=s_[:, 1:2], in0=s_[:, 0:1], scalar1=-SCALE, scalar2=None, op0=ALU.mult), r=[bs_], w=[bs_])
                    k.op("dve", lambda e: e.tensor_scalar(out=s_[:, 1:2], in0=s_[:, 1:2], scalar1=nsink[:, hq:hq + 1], scalar2=None, op0=ALU.min), r=[bs_, b_nsink], w=[bs_])
                    k.op("act", lambda e: e.activation(out=Pm[i][:], in_=bank[pS][:, 0:384], func=AF.Exp, scale=SCALE, bias=s_[:, 1:2]), r=[bkb[pS], bs_], w=[b_Pm[i]])
                    k.op("act", lambda e: e.activation(out=s_[:, 3:4], in_=s_[:, 1:2], func=AF.Exp, bias=sink_t[:, hq:hq + 1]), r=[bs_, b_sink], w=[bs_])
                    k.op("dve", lambda e: e.reduce_sum(out=s_[:, 2:3], in_=Pm[i][:], axis=mybir.AxisListType.X), r=[b_Pm[i]], w=[bs_])
                    k.op("dve", lambda e: e.tensor_tensor(out=s_[:, 4:5], in0=s_[:, 2:3], in1=s_[:, 3:4], op=ALU.add), r=[bs_], w=[bs_])
                    k.op("dve", lambda e: e.reciprocal(out=s_[:, 5:6], in_=s_[:, 4:5]), r=[bs_], w=[bs_])

                def stC(n):
                    si, hq, c, kvt, mt, g, i = geo(n)
                    pT = 4 + i
                    pvb = bank[pT].bitcast(BF16)
                    for kb in range(3):
                        k.op("pe", lambda e, kb=kb: e.transpose(out=pvb[:, kb * 128:(kb + 1) * 128], in_=Pm[i][:, kb * 128:(kb + 1) * 128], identity=identb), r=[b_Pm[i], b_c16], w=[bkb[pT]])
                    k.op("act", lambda e: e.activation(out=PT[i][:].rearrange("p a b -> p (a b)"), in_=pvb[:, 0:384], func=AF.Identity), r=[bkb[pT]], w=[b_PT[i]])

                def stD(n):
                    si, hq, c, kvt, mt, g, i = geo(n)
                    pO = 6 + i
                    for kb in range(3):
                        k.op("pe", lambda e, kb=kb: e.matmul(bank[pO][:, 0:128], lhsT=PT[i][:, kb, :], rhs=v_tok[:, kvt[kb], g * 128:(g + 1) * 128], start=(kb == 0), stop=(kb == 2)), r=[b_PT[i], b_v], w=[bkb[pO]])
                    k.op("dve", lambda e: e.tensor_scalar(out=ao[si][:, hq * 128:(hq + 1) * 128], in0=bank[pO][:, 0:128], scalar1=st[i][:, 5:6], scalar2=None, op0=ALU.mult), r=[bkb[pO], b_st[i]], w=[b_ao[si]])
                    if hq == 15:
                        k.op("dve", lambda e: e.tensor_tensor(out=aosq[:], in0=ao[si][:], in1=ao[si][:], op=ALU.mult), r=[b_ao[si]], w=[b_aosq])
                        k.op("dve", lambda e: e.reduce_sum(out=sq[:, 16:17], in_=aosq[:], axis=mybir.AxisListType.X), r=[b_aosq], w=[b_sq])
                        k.op("dve", lambda e: e.tensor_scalar(out=sq[:, 17:18], in0=sq[:, 16:17], scalar1=1.0 / 2048, scalar2=RMS_EPS, op0=ALU.mult, op1=ALU.add), r=[b_sq], w=[b_sq])
                        k.op("act", lambda e: e.activation(out=sq[:, 18:19], in_=sq[:, 17:18], func=AF.Ln), r=[b_sq], w=[b_sq])
                        k.op("act", lambda e: e.activation(out=sq[:, 18:19], in_=sq[:, 18:19], func=AF.Exp, scale=-0.5), r=[b_sq], w=[b_sq])
                        k.op("dve", lambda e: e.tensor_scalar(out=aon[:], in0=ao[si][:], scalar1=sq[:, 18:19], scalar2=None, op0=ALU.mult), r=[b_ao[si], b_sq], w=[b_aon])
                        k.dma("sp", mix_d[c * 128:(c + 1) * 128, 0:2048], aon[:], r=[b_aon])

                stA(0)
                for n in range(len(items)):
                    if n + 1 < len(items):
                        stA(n + 1)
                    stB(n)
                    if not nomod:
                        ada_unit(64 + pr * 32 + n)
                        flush_epi()
                    stC(n)
                    stD(n)
            k.barrier()

        ast_.close()
        k.barrier()
        k.barrier()
        with ExitStack() as ps_:
            Hb = sb("Hb", [128, 2048], stack=ps_); b_Hb = Buf("Hb")
            sbt = [sb("sbt%d" % i, [128, 2048], stack=ps_) for i in range(2)]; b_sbt = [Buf("sbt0"), Buf("sbt1")]
            ebt = [sb("ebt%d" % i, [128, 32], stack=ps_) for i in range(2)]; b_ebt = [Buf("ebt0"), Buf("ebt1")]
            hsv = [sb("hsv%d" % i, [128, 2048], BF16, stack=ps_) for i in range(2)]; b_hsv = [Buf("hsv0"), Buf("hsv1")]
            k.op("dve", lambda e: e.memset(Hb[:], 0.0), w=[b_Hb])
            for n_, s in enumerate(range(31, -1, -1)):
                i = n_ % 2
                k.dma("sp", sbt[i][:], sb_d[s], w=[b_sbt[i]])
                k.dma("sp", ebt[i][:], eb_d[s], w=[b_ebt[i]])
                if s <= 7:
                    k.op("act", lambda e, i=i: e.activation(out=hsv[i][:], in_=Hb[:], func=AF.Identity), r=[b_Hb], w=[b_hsv[i]])
                    k.dma("sp", hb_d[s], hsv[i][:], r=[b_hsv[i]])
                k.op("dve", lambda e, i=i, s=s: e.tensor_scalar(out=ebt[i][:], in0=ebt[i][:], scalar1=ab_t[:, s:s + 1], scalar2=None, op0=ALU.mult), r=[b_ab, b_ebt[i]], w=[b_ebt[i]])
                k.op("dve", lambda e, i=i: e.tensor_tensor(out=V(Hb[:], 64), in0=V(Hb[:], 64), in1=bc_hp(ebt[i][:]), op=ALU.mult), r=[b_ebt[i], b_Hb], w=[b_Hb])
                k.op("dve", lambda e, i=i, s=s: e.scalar_tensor_tensor(out=Hb[:], in0=sbt[i][:], scalar=ab_t[:, s:s + 1], in1=Hb[:], op0=ALU.mult, op1=ALU.add), r=[b_sbt[i], b_ab, b_Hb], w=[b_Hb])
            k.barrier()
        if "hb" in dbg:
            k.dma("sp", dbg_out["hb"], hb_d, is_out=True)

        gst2 = ExitStack()
        alloc_group(gst2, "b", False)
        with ExitStack() as ps_:
            BT = sb("BT", [128, 4, 3, 128], BF16, stack=ps_); b_BT = Buf("BT")
            CT = sb("CT", [128, 4, 3, 128], BF16, stack=ps_); b_CT = Buf("CT")
            cacc = sb("cacc", [128, 3, 128], stack=ps_); b_cacc = Buf("cacc")
            gz = sb("gz", [128, 3, 2048], BF16, stack=ps_); b_gz = Buf("gz")
            zT = sb("zT", [128, 3 * SW], BF16, stack=ps_); b_zT = Buf("zT")
            R = [sb("R%d" % i, [128, 16, 128], stack=ps_) for i in range(2)]; b_R = [Buf("R0"), Buf("R1")]
            dec = [sb("dec%d" % i, [128, 512], stack=ps_) for i in range(2)]; b_dec = [Buf("dec0"), Buf("dec1")]
            Mt = [sb("Mt%d" % i, [128, 4, 128], BF16, stack=ps_) for i in range(2)]; b_Mt = [Buf("Mt0"), Buf("Mt1")]
            cbm = [sb("cbm%d" % i, [128, 4, 128], stack=ps_) for i in range(2)]; b_cbm = [Buf("cbF"), Buf("cbB")]
            xdt = [sb("xdt%d" % i, [128, 2048], BF16, stack=ps_) for i in range(2)]; b_xdt = [Buf("xdtf"), Buf("xdtb")]
            hin = [sb("hin%d" % i, [128, 2048], BF16, stack=ps_) for i in range(2)]; b_hin = [Buf("hinf"), Buf("hinb")]
            yacc = sb("yacc", [128, 2048], stack=ps_); b_yacc = Buf("yacc")
            ytmp = sb("ytmp", [128, 512], stack=ps_); b_ytmp = Buf("ytmp")
            ysq = sb("ysq", [128, 8], stack=ps_); b_ysq = Buf("ysq")
            yn = sb("yn", [128, 2048], BF16, stack=ps_); b_yn = Buf("yn")
            xs_tok = xsL[0]; B_tok = BtL[0]; b_xs0 = b_xs[0]; b_Bt0 = b_Bt[0]; b_hTg0 = b_hTg[0]
            for G in OWNG:
                ns = len(G)
                for t_ in group_hT_tasks(G, 0):
                    t_()
                gemm(hT_fn(ns), ns * SW, KC, w_in, CX, 16, lambda blk, ps, pbuf, G=G: conv_epi(blk, ps, pbuf, G, xs_tok, b_xs0, blk * 128), [b_hTg0], two_phase=True)
                gemm(hT_fn(ns), ns * SW, KC, w_in, CB, 4, lambda blk, ps, pbuf, G=G: conv_epi(16 + blk, ps, pbuf, G, B_tok, b_Bt0, blk * 128, feat=(BT[:, blk, :, :], b_BT, cacc, b_cacc)), [b_hTg0], two_phase=True)
                gemm(hT_fn(ns), ns * SW, KC, w_in, CC, 4, lambda blk, ps, pbuf, G=G: conv_epi(20 + blk, ps, pbuf, G, None, None, 0, feat=(CT[:, blk, :, :], b_CT, cacc, b_cacc)), [b_hTg0], two_phase=True)

                def epi_z(blk, ps, pbuf, ns=ns):
                    k.op("act", lambda e: e.activation(out=zT[:, 0:ns * SW], in_=ps[:, 0:ns * SW], func=AF.Silu), r=[pbuf], w=[b_zT])
                    pvb = bank[3].bitcast(BF16)
                    for si in range(ns):
                        k.op("pe", lambda e, si=si: e.transpose(out=pvb[:, si * 128:(si + 1) * 128], in_=zT[:, si * SW + 2:si * SW + 130], identity=identb), r=[b_zT, b_c16], w=[bkb[3]])
                    k.op("dve", lambda e: e.tensor_copy(out=gz[:, 0:ns, blk * 128:(blk + 1) * 128], in_=V(pvb[:, 0:ns * 128], 128)), r=[bkb[3]], w=[b_gz])
                gemm(hT_fn(ns), ns * SW, KC, w_in, CZ, 16, epi_z, [b_hTg0])
                flush_epi()
                dt_matmuls(G, 0)
                for t_ in chain_dt_tasks(G, 0):
                    t_()
                for si, s in enumerate(G):
                    dts_ = dts[:, 0, si, :]; adt_ = adt[:, 0, si, :]; ex_ = ex[:, 0, si, :]
                    bdts_ = b_dts[0][si]; badt_ = b_adt[0][si]; bex_ = b_ex[0][si]
                    xs3 = V(xs_tok[:, si, :], 64)
                    if s == 0:
                        dump("dts", dts_, [bdts_]); dump("ex", ex_, [bex_]); dump("xs", xs_tok[:, 0, :], [b_xs0], BF16)
                        dump("Btok", B_tok[:, 0, :], [b_Bt0], BF16); dump("BT", BT[:, :, 0, :], [b_BT], BF16); dump("CT", CT[:, :, 0, :], [b_CT], BF16)
                        dump("gz", gz[:, 0, :], [b_gz], BF16); dump("hf", Hf[:], [b_Hf])
                    k.dma("sp", hin[1][:], hb_d[s], w=[b_hin[1]])
                    k.op("act", lambda e: e.activation(out=hin[0][:], in_=Hf[:], func=AF.Identity), r=[b_Hf], w=[b_hin[0]])
                    for d in range(2):
                        k.op("dve", lambda e, d=d: e.tensor_tensor(out=V(xdt[d][:], 64), in0=xs3, in1=bc_hp(dts_[:, d * 32:(d + 1) * 32]), op=ALU.mult), r=[b_xs0, bdts_], w=[b_xdt[d]])
                    for g in range(4):
                        k.op("pe", lambda e, g=g: e.matmul(bank[3][:, g * 128:(g + 1) * 128], lhsT=BT[:, g, si, :], rhs=CT[:, g, si, :], start=True, stop=True), r=[b_BT, b_CT], w=[bkb[3]])
                    for d, M in ((0, Uin), (1, Lin)):
                        k.op("dve", lambda e, d=d, M=M: e.tensor_tensor(out=cbm[d][:], in0=V(bank[3][:, :], 128), in1=M.unsqueeze(1).broadcast_to([128, 4, 128]), op=ALU.mult), r=[bkb[3], b_cm], w=[b_cbm[d]])
                    for hh in range(2):
                        for d, M in ((0, Uin), (1, Lin)):
                            k.op("dve", lambda e, d=d, M=M: e.tensor_tensor(out=R[d][:], in0=M.unsqueeze(1).broadcast_to([128, 16, 128]), in1=adt_[:, d * 32 + hh * 16:d * 32 + hh * 16 + 16].unsqueeze(2).broadcast_to([128, 16, 128]), op=ALU.mult), r=[b_cm, badt_], w=[b_R[d]])
                        for q4 in range(4):
                            h0 = hh * 16 + q4 * 4
                            g = h0 // 8
                            for d, M2 in ((0, Lst), (1, Ust)):
                                pb = 2 + d
                                k.op("pe", lambda e, d=d, M2=M2, pb=pb: e.matmul(bank[pb][:, :], lhsT=M2, rhs=R[d][:, q4 * 4:(q4 + 1) * 4, :].rearrange("p a b -> p (a b)"), start=True, stop=True), r=[b_cm, b_R[d]], w=[bkb[pb]])
                                k.op("act", lambda e, d=d, pb=pb: e.activation(out=dec[d][:], in_=bank[pb][:, :], func=AF.Exp), r=[bkb[pb]], w=[b_dec[d]])
                                k.op("dve", lambda e, d=d, g=g: e.tensor_tensor(out=Mt[d][:], in0=V(dec[d][:], 128), in1=cbm[d][:, g, :].unsqueeze(1).broadcast_to([128, 4, 128]), op=ALU.mult), r=[b_dec[d], b_cbm[d]], w=[b_Mt[d]])
                            for i4 in range(4):
                                h = h0 + i4
                                pb = 4 + ((h % 16) // 8)
                                o = bank[pb][:, (h % 8) * 64:(h % 8) * 64 + 64]
                                k.op("pe", lambda e, o=o, i4=i4, h=h: e.matmul(o, lhsT=Mt[0][:, i4, :], rhs=xdt[0][:, h * 64:(h + 1) * 64], start=True, stop=False), r=[b_Mt[0], b_xdt[0]], w=[bkb[pb]])
                                k.op("pe", lambda e, o=o, i4=i4, h=h: e.matmul(o, lhsT=Mt[1][:, i4, :], rhs=xdt[1][:, h * 64:(h + 1) * 64], start=False, stop=True), r=[b_Mt[1], b_xdt[1]], w=[bkb[pb]])
                        for gg in range(2):
                            g = hh * 2 + gg
                            ys = yacc[:, g * 512:(g + 1) * 512]
                            k.op("dve", lambda e, g=g, ys=ys: e.tensor_tensor(out=V(ys, 64), in0=V(xs_tok[:, si, g * 512:(g + 1) * 512], 64), in1=bc_hp(hp[:, 128 + g * 8:128 + g * 8 + 8], 8), op=ALU.mult), r=[b_xs0, b_hp], w=[b_yacc])
                            k.op("dve", lambda e, gg=gg, ys=ys: e.tensor_tensor(out=ys, in0=ys, in1=bank[4 + gg][:, :], op=ALU.add), r=[bkb[4 + gg]], w=[b_yacc])
                            if s == 0:
                                dump("yd%d" % g, ys, [b_yacc])
                            for d in range(2):
                                pb = 2 + d
                                k.op("pe", lambda e, d=d, g=g, pb=pb: e.matmul(bank[pb][:, :], lhsT=CT[:, g, si, :], rhs=hin[d][:, g * 512:(g + 1) * 512], start=True, stop=True), r=[b_CT, b_hin[d]], w=[bkb[pb]])
                                k.op("dve", lambda e, d=d, g=g, pb=pb: e.tensor_tensor(out=V(ytmp[:], 64), in0=V(bank[pb][:, :], 64), in1=bc_hp(ex_[:, d * 32 + g * 8:d * 32 + g * 8 + 8], 8), op=ALU.mult), r=[bkb[pb], bex_], w=[b_ytmp])
                                k.op("dve", lambda e, ys=ys: e.tensor_tensor(out=ys, in0=ys, in1=ytmp[:], op=ALU.add), r=[b_ytmp], w=[b_yacc])
                    if s == 0:
                        dump("ypre", yacc[:], [b_yacc]); dump("hinb", hin[1][:], [b_hin[1]], BF16)
                    k.op("dve", lambda e: e.tensor_tensor(out=yacc[:], in0=yacc[:], in1=gz[:, si, :], op=ALU.mult), r=[b_gz], w=[b_yacc])
                    rsc = R[0][:].rearrange("p a b -> p (a b)")
                    k.op("dve", lambda e: e.tensor_tensor(out=rsc, in0=yacc[:], in1=yacc[:], op=ALU.mult), r=[b_yacc], w=[b_R[0]])
                    k.op("dve", lambda e: e.reduce_sum(out=ysq[:, 0:4], in_=V(rsc, 512), axis=mybir.AxisListType.X), r=[b_R[0]], w=[b_ysq])
                    k.op("dve", lambda e: e.tensor_scalar(out=ysq[:, 4:8], in0=ysq[:, 0:4], scalar1=1.0 / 512, scalar2=RMS_EPS, op0=ALU.mult, op1=ALU.add), r=[b_ysq], w=[b_ysq])
                    k.op("act", lambda e: e.activation(out=ysq[:, 4:8], in_=ysq[:, 4:8], func=AF.Ln), r=[b_ysq], w=[b_ysq])
                    k.op("act", lambda e: e.activation(out=ysq[:, 4:8], in_=ysq[:, 4:8], func=AF.Exp, scale=-0.5), r=[b_ysq], w=[b_ysq])
                    k.op("dve", lambda e: e.tensor_tensor(out=V(yn[:], 512), in0=V(yacc[:], 512), in1=ysq[:, 4:8].unsqueeze(2).broadcast_to([128, 4, 512]), op=ALU.mult), r=[b_yacc, b_ysq], w=[b_yn])
                    k.dma("sp", mix_d[s * 128:(s + 1) * 128, 2048:4096], yn[:], r=[b_yn])
                    slot_states(si, s, do_f=True, do_b=False, act_f_col=None)
            k.barrier()
        gst2.close()
        k.barrier()
        with ExitStack() as ps_:
            stg_ = sb("mcB_s", [64, 128], stack=ps_); bs_ = Buf("mcB_s")
            k.dma("sp", stg_[:], mod_d[3 * D:5 * D].rearrange("(a b) -> a b", b=128), w=[bs_])
            k.op("pe", lambda e: e.transpose(out=bank[2][:, 0:64], in_=stg_[:], identity=ident[0:64, 0:64]), r=[bs_, b_cm], w=[bkb[2]])
            k.op("dve", lambda e: e.tensor_copy(out=mB[:, 0:64], in_=bank[2][:, 0:64]), r=[bkb[2]], w=[b_mcB])
            k.op("dve", lambda e: e.tensor_scalar(out=mB[:, 32:64], in0=mB[:, 32:64], scalar1=1.0, scalar2=None, op0=ALU.add), r=[b_mcB], w=[b_mcB])
            if "mod" in dbg:
                k.dma("sp", dbg_out["mod"], mod_d, is_out=True)
            k.barrier()
        if "mix" in dbg:
            k.dma("sp", dbg_out["mix"], mix_d, is_out=True)

        if full:
            def store_T(ps, pbuf, stg, b_stg, ost, b_ost, dst_d, half, blk):
                k.op("act", lambda e: e.activation(out=stg[:], in_=ps[:, 0:512], func=AF.Identity), r=[pbuf], w=[b_stg])
                for tt in range(4):
                    k.op("pe", lambda e, tt=tt: e.transpose(out=bank[3][:, tt * 128:(tt + 1) * 128], in_=stg[:, tt * 128:(tt + 1) * 128], identity=ident), r=[b_stg, b_cm], w=[bkb[3]])
                k.op("dve", lambda e: e.tensor_copy(out=ost[:].rearrange("p a b -> p (a b)"), in_=bank[3][:, :]), r=[bkb[3]], w=[b_ost])
                k.dma("sp", dst_d[half * 512:(half + 1) * 512, blk * 128:(blk + 1) * 128].rearrange("(t p) c -> p t c", p=128), ost[:], r=[b_ost])

            with ExitStack() as ps_:
                mixT = sb("mixT", [128, KC, 1024], BF16, stack=ps_); b_mixT = Buf("mixT")
                mt_ = sb("mixtile", [128, D], BF16, stack=ps_); b_mt = Buf("mixtile")
                stg = sb("ostg", [128, 512], stack=ps_); b_stg = Buf("ostg")
                ost = sb("oost", [128, 4, 128], stack=ps_); b_ost = Buf("oost")
                for t in range(8):
                    k.dma("sp", mt_[:], mix_d[t * 128:(t + 1) * 128, :], w=[b_mt])
                    for k0 in range(0, KC, 8):
                        pb = 4 + (k0 // 8) % 2
                        pvb = bank[pb].bitcast(BF16)
                        for j in range(8):
                            kc = k0 + j
                            k.op("pe", lambda e, kc=kc, j=j, pvb=pvb: e.transpose(out=pvb[:, j * 128:(j + 1) * 128], in_=mt_[:, kc * 128:(kc + 1) * 128], identity=identb), r=[b_mt, b_c16], w=[bkb[pb]])
                        for j in range(8):
                            kc = k0 + j
                            nwc = anw[:, 0, kc:kc + 1] if kc < 16 else snw[:, 0, kc - 16:kc - 15]
                            k.op("act", lambda e, kc=kc, j=j, pvb=pvb, nwc=nwc: e.activation(out=mixT[:, kc, t * 128:(t + 1) * 128], in_=pvb[:, j * 128:(j + 1) * 128], func=AF.Identity, scale=nwc), r=[bkb[pb], b_anw, b_snw], w=[b_mixT])
                for half in range(2):
                    gemm(lambda kc, half=half: mixT[:, kc, half * 512:(half + 1) * 512], 512, KC, w_out, 0, 32,
                         lambda blk, ps, pbuf, half=half: store_T(ps, pbuf, stg, b_stg, ost, b_ost, mixr_d, half, blk), [b_mixT])
                flush_epi()
                k.barrier()
            if "mixr" in dbg:
                k.dma("sp", dbg_out["mixr"], mixr_d, is_out=True)

            def ln_phase(stack, tiles, br_d, res_fn, rows3, emit):
                rows = sb("rows", [8, D], stack=stack); b_rows = Buf("rows")
                sel = sb("selr", [8, 8, 128], stack=stack); b_sel = Buf("sel")
                k.dma("sp", sel[:].rearrange("p a b -> p (a b)"), sel_d[:, :], w=[b_sel])
                k.op("dve", lambda e: e.memset(rows[:], 0.0), w=[b_rows])
                k.dma("sp", rows[0:1, :], mod_d[2 * D:3 * D].rearrange("(a b) -> a b", a=1), w=[b_rows])
                k.dma("sp", rows[1:2, :], mod_d[5 * D:6 * D].rearrange("(a b) -> a b", a=1), w=[b_rows])
                for r_, src in ((2, ln1_g), (3, ln1_b), (4, ln2_g), (5, ln2_b)):
                    k.dma("sp", rows[r_:r_ + 1, :], src.rearrange("(a b) -> a b", a=1), w=[b_rows])
                bc3 = sb("bc3", [128, 3, D], stack=stack); b_bc3 = Buf("bc3")
                rt_ = sb("lnr", [128, D], stack=stack); b_rt = Buf("lnr")
                bst = sb("bst", [128, 8, 6], stack=stack); b_bst = Buf("bst")
                mv = sb("mv", [128, 4], stack=stack); b_mv = Buf("mv")
                for j, r_ in enumerate(rows3):
                    for n in range(8):
                        pb = 2 + n % 2
                        k.op("pe", lambda e, r_=r_, n=n, pb=pb: e.matmul(bank[pb][:, :], lhsT=sel[:, r_, :], rhs=rows[:, n * 512:(n + 1) * 512], start=True, stop=True), r=[b_sel, b_rows], w=[bkb[pb]])
                        k.op("act", lambda e, j=j, n=n, pb=pb: e.activation(out=bc3[:, j, n * 512:(n + 1) * 512], in_=bank[pb][:, :], func=AF.Identity), r=[bkb[pb]], w=[b_bc3])
                for t in tiles:
                    k.dma("sp", rt_[:], br_d[t * 128:(t + 1) * 128, :], w=[b_rt])
                    res_fn(t)
                    k.op("dve", lambda e: e.tensor_tensor(out=rt_[:], in0=rt_[:], in1=bc3[:, 0, :], op=ALU.mult), r=[b_bc3], w=[b_rt])
                    k.op("dve", lambda e: e.scalar_tensor_tensor(out=rt_[:], in0=xt[:], scalar=ALPHA, in1=rt_[:], op0=ALU.mult, op1=ALU.add), r=[b_xt], w=[b_rt])
                    for n in range(8):
                        k.op("dve", lambda e, n=n: e.bn_stats(out=bst[:, n, :], in_=rt_[:, n * 512:(n + 1) * 512]), r=[b_rt], w=[b_bst])
                    k.op("dve", lambda e: e.bn_aggr(out=mv[:, 0:2], in_=bst[:].rearrange("p a b -> p (a b)")), r=[b_bst], w=[b_mv])
                    k.op("dve", lambda e: e.tensor_scalar(out=mv[:, 2:3], in0=mv[:, 1:2], scalar1=LN_EPS, scalar2=None, op0=ALU.add), r=[b_mv], w=[b_mv])
                    k.op("act", lambda e: e.activation(out=mv[:, 2:3], in_=mv[:, 2:3], func=AF.Ln), r=[b_mv], w=[b_mv])
                    k.op("act", lambda e: e.activation(out=mv[:, 2:3], in_=mv[:, 2:3], func=AF.Exp, scale=-0.5), r=[b_mv], w=[b_mv])
                    k.op("dve", lambda e: e.tensor_scalar(out=rt_[:], in0=rt_[:], scalar1=mv[:, 0:1], scalar2=mv[:, 2:3], op0=ALU.subtract, op1=ALU.mult), r=[b_mv], w=[b_rt])
                    k.op("dve", lambda e: e.tensor_tensor(out=rt_[:], in0=rt_[:], in1=bc3[:, 1, :], op=ALU.mult), r=[b_bc3], w=[b_rt])
                    k.op("dve", lambda e: e.tensor_tensor(out=rt_[:], in0=rt_[:], in1=bc3[:, 2, :], op=ALU.add), r=[b_bc3], w=[b_rt])
                    emit(t, rt_, b_rt)

            for half in range(2):
                h2s = ExitStack()
                h2T = sb("h2T", [128, KC, 512], BF16, stack=h2s); b_h2T = Buf("h2T")
                with ExitStack() as ps_:
                    def res1(t):
                        k.dma("sp", xt[:], xm[t * 128:(t + 1) * 128, :], w=[b_xt])

                    def emit1(t, rt_, b_rt, half=half):
                        k.dma("sp", x1_d[t * 128:(t + 1) * 128, :], rt_[:], r=[b_rt])
                        tl = t - half * 4
                        transp_mod(rt_, b_rt, 128, lambda kc, src, tl=tl: [(h2T[:, kc, tl * 128:(tl + 1) * 128], src)], b_h2T, mB, b_mcB)
                    ln_phase(ps_, range(half * 4, half * 4 + 4), mixr_d, res1, (0, 2, 3), emit1)
                    k.barrier()
                with ExitStack() as ps_:
                    actT = sb("actT", [128, KCF, 512], BF16, stack=ps_); b_actT = Buf("actT")
                    sg = [sb("sg%d" % i, [128, 512], stack=ps_) for i in range(2)]; b_sg = [Buf("sg0"), Buf("sg1")]
                    stg = sb("fstg", [128, 512], stack=ps_); b_stg = Buf("fstg")
                    ost = sb("fost", [128, 4, 128], stack=ps_); b_ost = Buf("fost")
                    for blk in range(KCF):
                        i = blk % 2
                        gemm(lambda kc: h2T[:, kc, :], 512, KC, w_gate, blk * 128, 1,
                             lambda b_, ps, pbuf, i=i: k.op("act", lambda e: e.activation(out=sg[i][:], in_=ps[:, 0:512], func=AF.Silu), r=[pbuf], w=[b_sg[i]]), [b_h2T])
                        gemm(lambda kc: h2T[:, kc, :], 512, KC, w_up, blk * 128, 1,
                             lambda b_, ps, pbuf, i=i, blk=blk: k.op("dve", lambda e: e.tensor_tensor(out=actT[:, blk, :], in0=sg[i][:], in1=ps[:, 0:512], op=ALU.mult), r=[pbuf, b_sg[i]], w=[b_actT]), [b_h2T])
                    gemm(lambda kc: actT[:, kc, :], 512, KCF, w_down, 0, 32,
                         lambda blk, ps, pbuf, half=half: store_T(ps, pbuf, stg, b_stg, ost, b_ost, ffn_d, half, blk), [b_actT], units=3)
                    flush_epi()
                    k.barrier()
                h2s.close()
                k.barrier()
            if "x1" in dbg:
                k.dma("sp", dbg_out["x1"], x1_d, is_out=True)
            if "ffn" in dbg:
                k.dma("sp", dbg_out["ffn"], ffn_d, is_out=True)

            with ExitStack() as ps_:
                def res2(t):
                    k.dma("sp", xt[:], x1_d[t * 128:(t + 1) * 128, :], w=[b_xt])

                def emit2(t, rt_, b_rt):
                    k.dma("sp", y_out[t * 128:(t + 1) * 128, :], rt_[:], r=[b_rt], is_out=True)
                ln_phase(ps_, range(8), ffn_d, res2, (1, 4, 5), emit2)
                k.barrier()
        k.finish()
        print("instructions emitted:", k.ninst, flush=True)
    nc._in_names = in_names
    return nc


def _consts():
    t = np.arange(128)
    ident = np.eye(128, dtype=np.float32)
    Uin = (t[:, None] <= t[None, :]).astype(np.float32)
    Lin = (t[:, None] >= t[None, :]).astype(np.float32)
    Ust = (t[:, None] < t[None, :]).astype(np.float32)
    Lst = (t[:, None] > t[None, :]).astype(np.float32)
    ones = np.ones((128, 128), np.float32)
    cmat = np.concatenate([ident, Uin, Lin, Ust, Lst, ones], axis=1)
    sel = np.zeros((8, 8, 128), np.float32)
    for r in range(8):
        sel[r, r, :] = 1.0
    invf = (500000.0 ** (-np.arange(0, 32, 2, dtype=np.float32) / 32)).astype(np.float32)
    ropec = np.zeros((32, 2), np.float32)
    ropec[:, 0] = np.concatenate([invf, invf])
    ropec[:16, 1] = -1.0
    ropec[16:, 1] = 1.0
    psw = np.zeros((32, 32), np.float32)
    for m in range(32):
        psw[(m + 16) % 32, m] = 1.0
    return cmat, sel.reshape(8, 1024), ropec, psw


def prep_inputs(inputs):
    x = np.asarray(inputs["x"], np.float32)
    c = np.asarray(inputs["c"], np.float32)
    pos = np.asarray(inputs["positions"]).astype(np.int32)
    cmat, sel, ropec, psw = _consts()
    shared = {"cmat": cmat, "sel": sel, "ropec": ropec, "pswap": psw}
    for nm in ("w_ada", "w_in", "conv_w", "w_out", "w_gate", "w_up", "w_down"):
        shared[nm] = np.ascontiguousarray(np.asarray(inputs[nm], np.float32)[0])
    for nm in ("b_ada", "conv_b", "attn_sink", "a_log_fwd", "a_log_bwd", "dt_bias_fwd", "dt_bias_bwd",
               "ssd_d", "ssd_norm_w", "attn_norm_w", "ln1_g", "ln1_b", "ln2_g", "ln2_b"):
        shared[nm] = np.ascontiguousarray(np.asarray(inputs[nm], np.float32)[0])
    NEG = -30000.0
    qi = np.arange(128)[:, None]
    kj = np.arange(128)[None, :]
    in_maps = []
    for core in range(NCORES):
        b, j = core // 4, core % 4
        m = dict(shared)
        m["xm"] = np.ascontiguousarray(np.roll(x[b], -1024 * j, axis=0))
        xh = np.zeros((NSLOT * 4, D), np.float32)
        hmk = np.zeros((NSLOT, 4), np.float32)
        for s in range(NSLOT):
            t0 = ((8 * j + s) % 32) * 128
            for ii, tt in enumerate((t0 - 2, t0 - 1, t0 + 128, t0 + 129)):
                if 0 <= tt < 4096:
                    xh[4 * s + ii] = x[b, tt]
                    hmk[s, ii] = 1.0
        m["xh"] = xh
        m["hmask"] = np.ascontiguousarray(np.broadcast_to(hmk.reshape(1, -1), (128, NSLOT * 4)))
        pr = np.zeros((10, 128), np.int32)
        for ti, s in ((0, 0), (1, 1), (2, 2), (3, 3), (4, 4), (5, 5), (6, 6), (7, 7), (8, 8), (9, 31)):
            ch = (8 * j + s) % 32
            pr[ti] = pos[b, ch * 128:(ch + 1) * 128]
        m["posr"] = pr.reshape(-1)
        am = np.zeros((3, 128, 384), np.float32)
        prev = np.where(kj >= qi, 0.0, NEG)
        nxt = np.where(kj <= qi, 0.0, NEG)
        for ty in range(3):
            am[ty, :, 0:128] = prev
            am[ty, :, 256:384] = nxt
        if j == 0:
            am[0, :, 0:128] = NEG
        if j == 3:
            am[2, :, 256:384] = NEG
        m["amask"] = am
        af = np.zeros(NSLOT, np.float32)
        ab = np.zeros(NSLOT, np.float32)
        for s in range(NSLOT):
            if s >= 8 and s >= 32 - 8 * j:
                af[s] = 1.0
            if s < 8 or (8 <= s <= 31 - 8 * j):
                ab[s] = 1.0
        m["actf"] = np.ascontiguousarray(np.broadcast_to(af[None, :], (128, NSLOT)))
        m["actb"] = np.ascontiguousarray(np.broadcast_to(ab[None, :], (128, NSLOT)))
        m["cvec"] = np.ascontiguousarray(c[b].reshape(32, 128))
        in_maps.append(m)
    return in_maps


def kernel(**inputs):
    in_maps = prep_inputs(inputs)
    nc = build_nc()
    res = run_bass_kernel_spmd(nc, in_maps, core_ids=list(range(NCORES)))
    out = np.zeros((2, 4096, 4096), np.float32)
    for core in range(NCORES):
        b, j = core // 4, core % 4
        out[b, 1024 * j:1024 * (j + 1)] = res.results[core]["y_out"]
    return out
```
